# Optimizing a Trainium2 kernel written in Bass

```python
import jax, jax.numpy as jnp
from jax import lax
import numpy as np

D_MODEL = 1024
BATCH = 4
SEQ = 4096
DEPTH = 4

CHUNK = 64
N_MEM = 256
N_BRANCH = 4
MIX_W = D_MODEL // 2
CONV_WIDTH = 31
GLA_H = 4
GLA_DK = MIX_W // (2 * GLA_H)
GLA_DV = MIX_W // GLA_H
GLA_LOWRANK = 16
GLA_TAU = 16.0
SB_DH = 64
SB_H = MIX_W // SB_DH
SB_BLOCK = 128
LRU_BLOCKS = 8
LRU_BW = MIX_W // LRU_BLOCKS
LRU_CONV = 4
LRU_C = 8.0
MEM_H = 4
MEM_DH = D_MODEL // MEM_H
N_EXPERTS = 32
TOP_K = 4
MOE_FF = D_MODEL
SWIGLU_LIMIT = 7.0
SWIGLU_ALPHA = 1.702
MOE_BLOCK = 128
LN_EPS = 1e-5
DEEPNORM_ALPHA = (2 * DEPTH) ** 0.25
DEEPNORM_BETA = (8 * DEPTH) ** -0.25
IN_SIZES = (MIX_W, MIX_W,
            GLA_H * GLA_DK, GLA_H * GLA_DK, GLA_H * GLA_DV, GLA_H * GLA_DV, GLA_LOWRANK,
            SB_H * SB_DH, SB_H * SB_DH, SB_H * SB_DH,
            MIX_W, MIX_W,
            N_BRANCH * D_MODEL)
IN_COLS = sum(IN_SIZES)
SPLIT_POINTS = tuple(int(v) for v in np.cumsum(IN_SIZES)[:-1])

kernel_name = "hybrid_streaming_encoder_deepnorm_moe"


def layer_norm(x, g, b):
    xf = x.astype(jnp.float32)
    mu = jnp.mean(xf, axis=-1, keepdims=True)
    var = jnp.mean(jnp.square(xf - mu), axis=-1, keepdims=True)
    return ((xf - mu) * lax.rsqrt(var + LN_EPS) * g + b).astype(x.dtype)


def causal_depthwise_conv(u, w, b):
    width = w.shape[0]
    y = lax.conv_general_dilated(u, w[:, None, :], window_strides=(1,), padding=[(width - 1, 0)],
                                 dimension_numbers=("NWC", "WIO", "NWC"),
                                 feature_group_count=u.shape[-1])
    return y + b


def gla_chunked(q, k, v, log_a):
    B, S, H, DK = q.shape
    DV = v.shape[-1]
    nc = S // CHUNK
    f32 = jnp.float32
    qf = q.astype(f32).reshape(B, nc, CHUNK, H, DK)
    kf = k.astype(f32).reshape(B, nc, CHUNK, H, DK)
    vf = v.astype(f32).reshape(B, nc, CHUNK, H, DV)
    b = jnp.cumsum(log_a.astype(f32).reshape(B, nc, CHUNK, H, DK), axis=2)
    b_last = b[:, :, -1:]
    q_dec = qf * jnp.exp(b)
    k_inv = kf * jnp.exp(-b)
    k_end = kf * jnp.exp(b_last - b)
    causal = jnp.tril(jnp.ones((CHUNK, CHUNK), bool))
    s_intra = jnp.where(causal, jnp.einsum("bcthk,bcshk->bchts", q_dec, k_inv), 0.0)
    o_intra = jnp.einsum("bchts,bcshv->bcthv", s_intra, vf)
    kv = jnp.einsum("bcshk,bcshv->bchkv", k_end, vf)
    decay = jnp.exp(b_last[:, :, 0])

    def step(state, inp):
        dec_c, kv_c = inp
        return dec_c[..., None] * state + kv_c, state

    init = jnp.zeros((B, H, DK, DV), f32)
    _, s_prev = lax.scan(step, init, (jnp.moveaxis(decay, 1, 0), jnp.moveaxis(kv, 1, 0)))
    s_prev = jnp.moveaxis(s_prev, 0, 1)
    o_inter = jnp.einsum("bcthk,bchkv->bcthv", q_dec, s_prev)
    return (o_intra + o_inter).reshape(B, S, H, DV)


def stick_breaking(q, k, v):
    B, S, H, DH = q.shape
    scale = DH ** -0.5
    outs = []
    for blk in range(S // SB_BLOCK):
        q0 = blk * SB_BLOCK
        end = q0 + SB_BLOCK
        z = jnp.einsum("bthd,bshd->bhts", q[:, q0:end], k[:, :end]).astype(jnp.float32) * scale
        t_pos = q0 + jnp.arange(SB_BLOCK)
        s_pos = jnp.arange(end)
        before = s_pos[None, :] < t_pos[:, None]
        log_keep = jnp.where(before, jax.nn.log_sigmoid(-z), 0.0)
        log_tail = lax.cumsum(log_keep, axis=3, reverse=True) - log_keep
        w = jnp.where(before, jnp.exp(jax.nn.log_sigmoid(z) + log_tail), 0.0)
        outs.append(jnp.einsum("bhts,bshd->bthd", w.astype(v.dtype), v[:, :end]))
    return jnp.concatenate(outs, axis=1)


def rg_lru(xc, wa, ba, wx, bx, lam):
    B, S, C = xc.shape
    xb = xc.reshape(B, S, LRU_BLOCKS, LRU_BW)
    r = jax.nn.sigmoid(jnp.einsum("bsnc,ncd->bsnd", xb, wa).reshape(B, S, C) + ba)
    i = jax.nn.sigmoid(jnp.einsum("bsnc,ncd->bsnd", xb, wx).reshape(B, S, C) + bx)
    log_a = LRU_C * r.astype(jnp.float32) * jax.nn.log_sigmoid(lam.astype(jnp.float32))
    a = jnp.exp(log_a)
    u = jnp.sqrt(-jnp.expm1(2.0 * log_a)) * (i * xc).astype(jnp.float32)

    def combine(c1, c2):
        a1, b1 = c1
        a2, b2 = c2
        return a1 * a2, a2 * b1 + b2

    _, h = lax.associative_scan(combine, (a, u), axis=1)
    return h.astype(xc.dtype)


def hybrid_mixer(h, w_in, b_in, conv_a_w, conv_a_b, ln_a_g, ln_a_b, gla_wa2, gla_ba, gla_norm_g,
                 conv_d_w, conv_d_b, lru_wa, lru_ba, lru_wx, lru_bx, lru_lambda, w_branch, w_out, b_out):
    B, S, D = h.shape
    z = h @ w_in + b_in
    (a_val, a_gate, b_q, b_k, b_v, b_r, b_lr, c_q, c_k, c_v, d_x, d_g, g_merge) = jnp.split(
        z, SPLIT_POINTS, axis=-1)
    u = a_val * jax.nn.sigmoid(a_gate)
    y_a = jax.nn.silu(layer_norm(causal_depthwise_conv(u, conv_a_w, conv_a_b), ln_a_g, ln_a_b))
    log_a = jax.nn.log_sigmoid((b_lr @ gla_wa2 + gla_ba).astype(jnp.float32)) / GLA_TAU
    o_b = gla_chunked(b_q.reshape(B, S, GLA_H, GLA_DK) * (GLA_DK ** -0.5),
                      b_k.reshape(B, S, GLA_H, GLA_DK),
                      b_v.reshape(B, S, GLA_H, GLA_DV),
                      log_a.reshape(B, S, GLA_H, GLA_DK))
    o_b = o_b * lax.rsqrt(jnp.mean(jnp.square(o_b), axis=-1, keepdims=True) + LN_EPS) * gla_norm_g
    y_b = o_b.reshape(B, S, MIX_W).astype(h.dtype) * jax.nn.silu(b_r)
    y_c = stick_breaking(c_q.reshape(B, S, SB_H, SB_DH), c_k.reshape(B, S, SB_H, SB_DH),
                         c_v.reshape(B, S, SB_H, SB_DH)).reshape(B, S, MIX_W)
    xc = causal_depthwise_conv(d_x, conv_d_w, conv_d_b)
    y_d = rg_lru(xc, lru_wa, lru_ba, lru_wx, lru_bx, lru_lambda) * jax.nn.gelu(d_g)
    ys = jnp.stack([y_a, y_b, y_c, y_d], axis=2)
    proj = jnp.einsum("bsnc,ncd->bsnd", ys, w_branch)
    gates = jax.nn.sigmoid(g_merge.reshape(B, S, N_BRANCH, D))
    merged = jnp.sum(gates * proj, axis=2)
    return merged @ w_out + b_out


def memory_cross_attention(h, mem, wq, wk, wv, wo):
    B, S, D = h.shape
    M = mem.shape[1]
    q = (h @ wq).reshape(B, S, MEM_H, MEM_DH)
    k = (mem @ wk).reshape(B, M, MEM_H, MEM_DH)
    v = (mem @ wv).reshape(B, M, MEM_H, MEM_DH)
    s = jnp.einsum("bshd,bmhd->bhsm", q, k).astype(jnp.float32) * (MEM_DH ** -0.5)
    p = jax.nn.softmax(s, axis=-1).astype(v.dtype)
    o = jnp.einsum("bhsm,bmhd->bshd", p, v).reshape(B, S, D)
    return o @ wo


def moe_ffn(x2, router_w, router_b, w1, b1, w2, b2):
    T, D = x2.shape
    logits = (x2 @ router_w).astype(jnp.float32) + router_b
    top_val, top_idx = lax.top_k(logits, TOP_K)
    gate = jax.nn.softmax(top_val, axis=-1)
    n_assign = T * TOP_K
    e_flat = top_idx.reshape(n_assign)
    tok_flat = jnp.arange(n_assign, dtype=jnp.int32) // TOP_K
    order = jnp.argsort(e_flat)
    e_sorted = e_flat[order]
    tok_sorted = tok_flat[order]
    g_sorted = gate.reshape(n_assign)[order]
    counts = jnp.bincount(e_flat, length=N_EXPERTS)
    start = jnp.cumsum(counts) - counts
    padded = ((counts + MOE_BLOCK - 1) // MOE_BLOCK) * MOE_BLOCK
    pend = jnp.cumsum(padded)
    pstart = pend - padded
    dest = pstart[e_sorted] + (jnp.arange(n_assign, dtype=jnp.int32) - start[e_sorted])
    n_blocks = (n_assign + MOE_BLOCK - 1) // MOE_BLOCK + N_EXPERTS
    n_slots = n_blocks * MOE_BLOCK
    slot_tok = jnp.full((n_slots,), T, jnp.int32).at[dest].set(tok_sorted)
    slot_gate = jnp.zeros((n_slots,), x2.dtype).at[dest].set(g_sorted.astype(x2.dtype))
    block_e = jnp.minimum(jnp.searchsorted(pend, jnp.arange(n_blocks, dtype=jnp.int32) * MOE_BLOCK,
                                           side="right"), N_EXPERTS - 1)
    x_pad = jnp.concatenate([x2, jnp.zeros((1, D), x2.dtype)], axis=0)
    xs = x_pad[slot_tok].reshape(n_blocks, MOE_BLOCK, D)

    def expert_block(args):
        xb, e = args
        hcat = xb @ w1[e] + b1[e]
        g = jnp.minimum(hcat[:, :MOE_FF], SWIGLU_LIMIT)
        lin = jnp.clip(hcat[:, MOE_FF:], -SWIGLU_LIMIT, SWIGLU_LIMIT)
        act = g * jax.nn.sigmoid(SWIGLU_ALPHA * g) * (lin + 1.0)
        return act @ w2[e] + b2[e]

    ys = lax.map(expert_block, (xs, block_e)).reshape(n_slots, D)
    out = jnp.zeros((T + 1, D), x2.dtype).at[slot_tok].add(ys * slot_gate[:, None])
    return out[:T]


def setup_inputs(seed: int = 0) -> dict:
    key = jax.random.key(seed)
    ks = iter(jax.random.split(key, 64))
    f32 = jnp.float32
    L, D = DEPTH, D_MODEL
    beta = DEEPNORM_BETA

    def nrm(shape, scale):
        return scale * jax.random.normal(next(ks), shape, f32)

    def gain(shape):
        return 1.0 + nrm(shape, 0.02)

    a0 = jax.random.uniform(next(ks), (L, MIX_W), f32, minval=0.9, maxval=0.999)
    p = a0 ** (1.0 / LRU_C)
    lru_lambda = jnp.log(p) - jnp.log1p(-p)
    return {
        "x": nrm((BATCH, SEQ, D), 1.0),
        "mem": nrm((BATCH, N_MEM, D), 1.0),
        "ln0_g": gain((D,)),
        "ln0_b": nrm((D,), 0.02),
        "w_in": nrm((L, D, IN_COLS), D ** -0.5),
        "b_in": nrm((L, IN_COLS), 0.02),
        "conv_a_w": nrm((L, CONV_WIDTH, MIX_W), CONV_WIDTH ** -0.5),
        "conv_a_b": nrm((L, MIX_W), 0.02),
        "ln_a_g": gain((L, MIX_W)),
        "ln_a_b": nrm((L, MIX_W), 0.02),
        "gla_wa2": nrm((L, GLA_LOWRANK, GLA_H * GLA_DK), GLA_LOWRANK ** -0.5),
        "gla_ba": nrm((L, GLA_H * GLA_DK), 0.02),
        "gla_norm_g": gain((L, GLA_DV)),
        "conv_d_w": nrm((L, LRU_CONV, MIX_W), LRU_CONV ** -0.5),
        "conv_d_b": nrm((L, MIX_W), 0.02),
        "lru_wa": nrm((L, LRU_BLOCKS, LRU_BW, LRU_BW), LRU_BW ** -0.5),
        "lru_ba": nrm((L, MIX_W), 0.02),
        "lru_wx": nrm((L, LRU_BLOCKS, LRU_BW, LRU_BW), LRU_BW ** -0.5),
        "lru_bx": nrm((L, MIX_W), 0.02),
        "lru_lambda": lru_lambda,
        "w_branch": nrm((L, N_BRANCH, MIX_W, D), beta * MIX_W ** -0.5),
        "w_out": nrm((L, D, D), beta * D ** -0.5),
        "b_out": nrm((L, D), 0.02),
        "ln1_g": gain((L, D)),
        "ln1_b": nrm((L, D), 0.02),
        "ca_wq": nrm((L, D, D), D ** -0.5),
        "ca_wk": nrm((L, D, D), D ** -0.5),
        "ca_wv": nrm((L, D, D), beta * D ** -0.5),
        "ca_wo": nrm((L, D, D), beta * D ** -0.5),
        "ln2_g": gain((L, D)),
        "ln2_b": nrm((L, D), 0.02),
        "router_w": nrm((L, D, N_EXPERTS), D ** -0.5),
        "router_b": nrm((L, N_EXPERTS), 0.01),
        "moe_w1": nrm((L, N_EXPERTS, D, 2 * MOE_FF), beta * D ** -0.5),
        "moe_b1": nrm((L, N_EXPERTS, 2 * MOE_FF), 0.02),
        "moe_w2": nrm((L, N_EXPERTS, MOE_FF, D), beta * MOE_FF ** -0.5),
        "moe_b2": nrm((L, N_EXPERTS, D), 0.02),
        "ln3_g": gain((L, D)),
        "ln3_b": nrm((L, D), 0.02),
    }


def reference(x, mem, ln0_g, ln0_b, w_in, b_in, conv_a_w, conv_a_b, ln_a_g, ln_a_b, gla_wa2, gla_ba,
              gla_norm_g, conv_d_w, conv_d_b, lru_wa, lru_ba, lru_wx, lru_bx, lru_lambda, w_branch,
              w_out, b_out, ln1_g, ln1_b, ca_wq, ca_wk, ca_wv, ca_wo, ln2_g, ln2_b, router_w, router_b,
              moe_w1, moe_b1, moe_w2, moe_b2, ln3_g, ln3_b):
    B, S, D = x.shape
    h = layer_norm(x, ln0_g, ln0_b)
    for l in range(DEPTH):
        mix = hybrid_mixer(h, w_in[l], b_in[l], conv_a_w[l], conv_a_b[l], ln_a_g[l], ln_a_b[l],
                           gla_wa2[l], gla_ba[l], gla_norm_g[l], conv_d_w[l], conv_d_b[l],
                           lru_wa[l], lru_ba[l], lru_wx[l], lru_bx[l], lru_lambda[l],
                           w_branch[l], w_out[l], b_out[l])
        h = layer_norm(DEEPNORM_ALPHA * h + mix, ln1_g[l], ln1_b[l])
        ca = memory_cross_attention(h, mem, ca_wq[l], ca_wk[l], ca_wv[l], ca_wo[l])
        h = layer_norm(DEEPNORM_ALPHA * h + ca, ln2_g[l], ln2_b[l])
        ff = moe_ffn(h.reshape(B * S, D), router_w[l], router_b[l], moe_w1[l], moe_b1[l],
                     moe_w2[l], moe_b2[l]).reshape(B, S, D)
        h = layer_norm(DEEPNORM_ALPHA * h + ff, ln3_g[l], ln3_b[l])
    return h
```

```python
from contextlib import ExitStack, contextmanager
import numpy as np
import concourse.bass as bass
import concourse.mybir as mybir
from concourse.bass_utils import run_bass_kernel_spmd

F32 = mybir.dt.float32
BF16 = mybir.dt.bfloat16
I32 = mybir.dt.int32
U32 = mybir.dt.uint32
AF = mybir.ActivationFunctionType
ALU = mybir.AluOpType
AX = mybir.AxisListType

ENGS = ("pe", "dve", "act", "pool", "sp")
NDMASEM = 24


class Buf:
    __slots__ = ("name", "w", "r")

    def __init__(self, name=""):
        self.name = name
        self.w = None
        self.r = []


class T:
    def __init__(self, t, name):
        self.t = t
        self.b = Buf(name)

    def __getitem__(self, idx):
        return self.t[idx]


def _bufs(xs):
    out = []
    for x in xs:
        if x is None:
            continue
        out.append(x.b if isinstance(x, T) else x)
    return out


class KB:
    def __init__(self, nc, stack):
        self.nc = nc
        self.stack = stack
        self.prog = {e: [] for e in ENGS}
        self.cnt = {e: 0 for e in ENGS}
        self.known = {e: {} for e in ENGS}
        self.csem = {e: stack.enter_context(nc.semaphore("c_" + e)) for e in ENGS}
        self.dsem = {e: [stack.enter_context(nc.semaphore("d_%s%d" % (e, i))) for i in range(NDMASEM)]
                     for e in ("sp", "act", "pool")}
        self.dcnt = {e: 0 for e in ("sp", "act", "pool")}
        self.dtarget = {}
        self.nins = 0
        self.stack0 = stack
        self.bsem = None
        self.bcnt = 0
        self.bgsems = []
        self.bgtgt = []

    def _nm(self, name):
        self.nalloc = getattr(self, "nalloc", 0) + 1
        return "%s_%d" % (name, self.nalloc)

    def sbuf(self, name, shape, dtype):
        return T(self.stack.enter_context(self.nc.sbuf_tensor(self._nm(name), list(shape), dtype)), name)

    def psum(self, name, shape, dtype=F32):
        return T(self.stack.enter_context(self.nc.psum_tensor(self._nm(name), list(shape), dtype)), name)

    def dram(self, name, shape, dtype, kind="Internal"):
        return T(self.nc.dram_tensor(name, list(shape), dtype, kind=kind), name)

    def _need(self, eng, dep):
        kind = dep[0]
        if kind == "b":
            _, idx, tgt = dep
            key = ("b", idx)
            if self.known[eng].get(key, 0) >= tgt:
                return
            self.known[eng][key] = tgt
            sem = self.bgsems[idx]
            self.prog[eng].append(lambda e, sem=sem, tgt=tgt: e.wait_ge(sem, tgt))
            return
        if kind == "c":
            _, e2, n = dep
            if e2 == eng:
                if eng == "pe":
                    return
                if n < self.cnt[eng] - 1:
                    return
            key = ("c", e2)
            if self.known[eng].get(key, 0) >= n:
                return
            self.known[eng][key] = n
            sem = self.csem[e2]
            self.prog[eng].append(lambda e, sem=sem, n=n: e.wait_ge(sem, n))
        else:
            _, q, i, tgt = dep
            key = ("d", q, i)
            if self.known[eng].get(key, 0) >= tgt:
                return
            self.known[eng][key] = tgt
            sem = self.dsem[q][i]
            self.prog[eng].append(lambda e, sem=sem, tgt=tgt: e.wait_ge(sem, tgt))

    def _deps(self, eng, r, w):
        rb, wb = _bufs(r), _bufs(w)
        for b in rb:
            if b.w is not None:
                self._need(eng, b.w)
        for b in wb:
            if b.w is not None:
                self._need(eng, b.w)
            for d in b.r:
                self._need(eng, d)
        return rb, wb

    def op(self, eng, fn, r=(), w=()):
        rb, wb = self._deps(eng, r, w)
        self.cnt[eng] += 1
        n = self.cnt[eng]
        sem = self.csem[eng]
        self.prog[eng].append(lambda e, fn=fn, sem=sem: fn(e).then_inc(sem, 1))
        dep = ("c", eng, n)
        for b in wb:
            b.w = dep
            b.r = []
        for b in rb:
            if b not in wb:
                b.r.append(dep)
        self.nins += 1

    def dma(self, q, out, in_, r=(), w=(), **kw):
        rb, wb = self._deps(q, r, w)
        i = self.dcnt[q] % NDMASEM
        self.dcnt[q] += 1
        key = (q, i)
        prev = self.dtarget.get(key, 0)
        if prev:
            self._need(q, ("d", q, i, prev))
        tgt = prev + 16
        self.dtarget[key] = tgt
        sem = self.dsem[q][i]
        self.prog[q].append(lambda e, out=out, in_=in_, sem=sem, kw=kw: e.dma_start(out=out, in_=in_, **kw).then_inc(sem, 16))
        dep = ("d", q, i, tgt)
        for b in wb:
            b.w = dep
            b.r = []
        for b in rb:
            if b not in wb:
                b.r.append(dep)
        self.nins += 1
        return dep

    def dma_custom(self, q, fn, r=(), w=()):
        rb, wb = self._deps(q, r, w)
        i = self.dcnt[q] % NDMASEM
        self.dcnt[q] += 1
        key = (q, i)
        prev = self.dtarget.get(key, 0)
        if prev:
            self._need(q, ("d", q, i, prev))
        tgt = prev + 16
        self.dtarget[key] = tgt
        sem = self.dsem[q][i]
        self.prog[q].append(lambda e, fn=fn, sem=sem: fn(e).then_inc(sem, 16))
        dep = ("d", q, i, tgt)
        for b in wb:
            b.w = dep
            b.r = []
        for b in rb:
            if b not in wb:
                b.r.append(dep)
        self.nins += 1
        return dep

    def dma_bg(self, q, out, in_, slot, w=()):
        wb = _bufs(w)
        while len(self.bgsems) <= slot:
            self.bgsems.append(self.stack0.enter_context(self.nc.semaphore("bg%d" % len(self.bgsems))))
            self.bgtgt.append(0)
        idx = slot
        sem = self.bgsems[idx]
        self.bgtgt[idx] += 16
        tgt = self.bgtgt[idx]
        self._deps(q, (), w)
        self.prog[q].append(lambda e, out=out, in_=in_, sem=sem: e.dma_start(out=out, in_=in_).then_inc(sem, 16))
        dep = ("b", idx, tgt)
        for b in wb:
            b.w = dep
            b.r = []
        self.nins += 1

    def raw(self, eng, fn):
        self.prog[eng].append(fn)

    def wait_all(self, eng, bufs):
        for b in _bufs(bufs):
            if b.w is not None:
                self._need(eng, b.w)

    @contextmanager
    def scope(self):
        old = self.stack
        with ExitStack() as st:
            self.stack = st
            yield
            self.barrier()
        self.stack = old

    def barrier(self):
        if self.bsem is None:
            self.bsem = self.stack0.enter_context(self.nc.semaphore("bar"))
            self.bscr = self.nc.dram_tensor("bar_scr", [2, 64], F32, kind="Internal")
        for e in ENGS:
            if e != "sp" and self.cnt[e]:
                self._need("sp", ("c", e, self.cnt[e]))
        for (q, i), tgt in self.dtarget.items():
            self._need("sp", ("d", q, i, tgt))
        self.bcnt += 1
        n = self.bcnt * 16
        bsem, bscr = self.bsem, self.bscr
        self.prog["sp"].append(lambda e: e.dma_start(out=bscr.ap()[1:2, :], in_=bscr.ap()[0:1, :]).then_inc(bsem, 16))
        for e in ENGS:
            self.prog[e].append(lambda e_, n=n: e_.wait_ge(bsem, n))
            for e2 in ENGS:
                self.known[e][("c", e2)] = self.cnt[e2]
            for (q, i), tgt in self.dtarget.items():
                self.known[e][("d", q, i)] = tgt

    def finish(self):
        nc = self.nc
        prog = self.prog
        emap = {"pe": "tensor", "dve": "vector", "act": "scalar", "pool": "gpsimd", "sp": "sync"}
        with nc.Block() as block:
            for en in ENGS:
                def body(e, en=en):
                    for f in prog[en]:
                        f(e)
                getattr(block, emap[en])(body)


S = 4096
D = 1024
NT = 32
NTG = 8
NL = 4
NE = 32
ALPHA = 8.0 ** 0.25
EPS = 1e-5
SEG = dict(a_val=(0, 512), a_gate=(512, 512), b_q=(1024, 256), b_k=(1280, 256), b_v=(1536, 512),
           b_r=(2048, 512), b_lr=(2560, 16), c_q=(2576, 512), c_k=(3088, 512), c_v=(3600, 512),
           d_x=(4112, 512), d_g=(4624, 512), g_m=(5136, 4096))
TOKMAJ = ("b_v", "b_r", "c_v")
INCOLS = 9232

WSPEC = [
    ("ln0_g", [1024]), ("ln0_b", [1024]), ("w_in", [4, 1024, 9232]), ("b_in", [4, 9232]),
    ("conv_a_w", [4, 31, 512]), ("conv_a_b", [4, 512]), ("ln_a_g", [4, 512]), ("ln_a_b", [4, 512]),
    ("gla_wa2", [4, 16, 256]), ("gla_ba", [4, 256]), ("gla_norm_g", [4, 128]),
    ("conv_d_w", [4, 4, 512]), ("conv_d_b", [4, 512]), ("lru_wa", [4, 8, 64, 64]), ("lru_ba", [4, 512]),
    ("lru_wx", [4, 8, 64, 64]), ("lru_bx", [4, 512]), ("lru_lambda", [4, 512]),
    ("w_branch", [4, 4, 512, 1024]), ("w_out", [4, 1024, 1024]), ("b_out", [4, 1024]),
    ("ln1_g", [4, 1024]), ("ln1_b", [4, 1024]), ("ca_wq", [4, 1024, 1024]), ("ca_wk", [4, 1024, 1024]),
    ("ca_wv", [4, 1024, 1024]), ("ca_wo", [4, 1024, 1024]), ("ln2_g", [4, 1024]), ("ln2_b", [4, 1024]),
    ("router_w", [4, 1024, 32]), ("router_b", [4, 32]), ("moe_w1", [4, 32, 1024, 2048]),
    ("moe_b1", [4, 32, 2048]), ("moe_w2", [4, 32, 1024, 1024]), ("moe_b2", [4, 32, 1024]),
    ("ln3_g", [4, 1024]), ("ln3_b", [4, 1024]),
]


def make_consts():
    c = {}
    c["ident"] = np.eye(128, dtype=np.float32)
    j = np.arange(128)
    c["U"] = (j[:, None] > j[None, :]).astype(np.float32)
    s64 = np.arange(64)
    c["tri64"] = (s64[:, None] <= s64[None, :]).astype(np.float32)
    sp = np.arange(128)[:, None]
    tp = np.arange(512)[None, :]
    c["sbmask"] = np.stack([((128 * jj + sp) < tp).astype(np.float32) for jj in range(4)], axis=1)
    t = np.arange(S)
    c["rmask"] = np.broadcast_to(((t % 64) != 0).astype(np.float32)[None, :], (128, S)).copy()
    return c


CONST_SHAPES = dict(ident=[128, 128], U=[128, 128], tri64=[64, 64], sbmask=[128, 4, 512], rmask=[128, S])


class Ctx:
    pass


LAYER_STAGES = []


def mm(k, P, lhsT, rhs, start, stop, r, w):
    k.op("pe", lambda e: e.matmul(P, lhsT=lhsT, rhs=rhs, start=start, stop=stop), r=r, w=w)


def act(k, out, in_, func, r, w, bias=None, scale=None, eng="act"):
    kw = {}
    if bias is not None:
        kw["bias"] = bias
    if scale is not None:
        kw["scale"] = scale
    k.op("act", lambda e: e.activation(out=out, in_=in_, func=func, **kw), r=r, w=w)


def tt(k, eng, out, in0, in1, op, r, w):
    k.op(eng, lambda e: e.tensor_tensor(out=out, in0=in0, in1=in1, op=op), r=r, w=w)


def ts(k, eng, out, in0, s1, s2, op0, op1, r, w):
    if op1 is None:
        k.op(eng, lambda e: e.tensor_scalar(out=out, in0=in0, scalar1=s1, scalar2=None, op0=op0), r=r, w=w)
    else:
        k.op(eng, lambda e: e.tensor_scalar(out=out, in0=in0, scalar1=s1, scalar2=s2, op0=op0, op1=op1), r=r, w=w)


def stt(k, out, in0, scalar, in1, op0, op1, r, w):
    k.op("dve", lambda e: e.scalar_tensor_tensor(out=out, in0=in0, scalar=scalar, in1=in1, op0=op0, op1=op1), r=r, w=w)


def cp(k, eng, out, in_, r, w):
    if eng == "act":
        k.op("act", lambda e: e.activation(out=out, in_=in_, func=AF.Copy), r=r, w=w)
    else:
        k.op(eng, lambda e: e.tensor_copy(out=out, in_=in_), r=r, w=w)


def bcast_load(k, q, tile, src_ap, n=128):
    k.dma(q, tile[:], src_ap.partition_broadcast(n), w=[tile])


def ln_alloc(k, c, tag, tp=True):
    c.ln_s6 = [k.sbuf("ln_s6%s%d" % (tag, i), [128, 2, 6], F32) for i in range(2)]
    c.ln_mv = [k.sbuf("ln_mv%s%d" % (tag, i), [128, 2], F32) for i in range(2)]
    c.ln_rs = [k.sbuf("ln_rs%s%d" % (tag, i), [128, 1], F32) for i in range(2)]
    c.ln_yb = [k.sbuf("ln_yb%s%d" % (tag, i), [128, 1024], BF16) for i in range(2)]
    c.ln_tp = [k.psum("ln_tp%s%d" % (tag, i), [128, 1024], BF16) for i in range(2)] if tp else [None, None]
    c.ln_g = k.sbuf("ln_g" + tag, [128, 1024], F32)
    c.ln_b = k.sbuf("ln_b" + tag, [128, 1024], F32)


def ln_tile(k, c, i, V, write_hT=True, hb=False, defer=False):
    S6, MV, RS, YB, TP = c.ln_s6[i % 2], c.ln_mv[i % 2], c.ln_rs[i % 2], c.ln_yb[i % 2], c.ln_tp[i % 2]
    for h in range(2):
        k.op("dve", lambda e, h=h: e.bn_stats(out=S6[:, h, :], in_=V[:, h * 512:(h + 1) * 512]), r=[V], w=[S6])
    k.op("dve", lambda e: e.bn_aggr(out=MV[:], in_=S6[:].rearrange("p a b -> p (a b)")), r=[S6], w=[MV])
    act(k, RS[:], MV[:, 1:2], AF.Ln, [MV], [RS], bias=EPS, scale=1.0)
    act(k, RS[:], RS[:], AF.Exp, [RS], [RS], scale=-0.5)
    ts(k, "dve", V[:], V[:], MV[:, 0:1], RS[:, 0:1], ALU.subtract, ALU.mult, [V, MV, RS], [V])
    tt(k, "pool", V[:], V[:], c.ln_g[:], ALU.mult, [V, c.ln_g], [V])
    tt(k, "pool", V[:], V[:], c.ln_b[:], ALU.add, [V, c.ln_b], [V])
    k.dma("sp", c.h_tok.t.ap()[i * 128:(i + 1) * 128, :], V[:], r=[V], w=[c.h_tok_b[i]])
    cp(k, "act", YB[:], V[:], [V], [YB])
    if hb:
        k.dma("sp", c.hb.t.ap()[i * 128:(i + 1) * 128, :], YB[:], r=[YB], w=[c.hb])
    if not write_hT:
        return None

    def back():
        for j in range(8):
            k.op("pe", lambda e, j=j: e.transpose(out=TP[:, j * 128:(j + 1) * 128], in_=YB[:, j * 128:(j + 1) * 128],
                                                  identity=c.ident_bf[:]), r=[YB, c.ident_bf], w=[TP])
        cp(k, "dve", c.hT[:, :, i * 128:(i + 1) * 128], TP[:].rearrange("p (a b) -> p a b", a=8), [TP], [c.hT_b[i]])
    if defer:
        return back
    back()
    return None


def hTr(c, tg):
    return c.hT_b[4 * tg:4 * tg + 4]


def stage_prologue(k, c):
    with k.scope():
        idf = k.sbuf("idf", [128, 128], F32)
        k.dma("sp", idf[:], c.consts["ident"].ap(), w=[idf])
        cp(k, "dve", c.ident_bf[:], idf[:], [idf], [c.ident_bf])
        cp(k, "pool", c.ident_f[:], idf[:], [idf], [c.ident_f])
        tp = k.psum("mtp", [128, 1024], BF16)
        for mt in range(2):
            mf = k.sbuf("memf%d" % mt, [128, 1024], F32)
            mb = k.sbuf("memb%d" % mt, [128, 1024], BF16)
            k.dma("sp", mf[:], c.mem.ap()[mt * 128:(mt + 1) * 128, :], w=[mf])
            cp(k, "act", mb[:], mf[:], [mf], [mb])
            for j in range(8):
                k.op("pe", lambda e, j=j, mb=mb: e.transpose(out=tp[:, j * 128:(j + 1) * 128], in_=mb[:, j * 128:(j + 1) * 128],
                                                             identity=c.ident_bf[:]), r=[mb, c.ident_bf], w=[tp])
            cp(k, "dve", c.memT[:, :, mt * 128:(mt + 1) * 128], tp[:].rearrange("p (a b) -> p a b", a=8), [tp], [c.memT])
        zr = k.sbuf("zrow", [1, 1024], BF16)
        k.op("pool", lambda e: e.memset(zr[:], 0.0), w=[zr])
        k.dma("sp", c.hb.t.ap()[S:S + 1, :], zr[:], r=[zr], w=[c.hb])
        ln_alloc(k, c, "0")
        bcast_load(k, "sp", c.ln_g, c.W["ln0_g"].ap())
        bcast_load(k, "sp", c.ln_b, c.W["ln0_b"].ap())
        V = [k.sbuf("ln0v%d" % i, [128, 1024], F32) for i in range(2)]
        pend = None
        for i in range(NT):
            k.dma("sp", V[i % 2][:], c.x.ap()[i * 128:(i + 1) * 128, :], w=[V[i % 2]])
            nb_ = ln_tile(k, c, i, V[i % 2], defer=True)
            if pend is not None:
                pend()
            pend = nb_
        pend()


def stage_inproj(k, c, l):
    with k.scope():
        w_in = c.W["w_in"].ap()[l].rearrange("(k p) n -> p k n", p=128)
        b_in = c.W["b_in"].ap()[l]
        biasA = k.sbuf("biasA", [128, 20], F32)
        biasL = k.sbuf("biasL", [16, 1], F32)
        biasB = k.sbuf("biasB", [128, 52], F32)
        k.dma("sp", biasA[:], b_in[0:2560].rearrange("(j p) -> p j", p=128), w=[biasA], allow_slow_non_contiguous=True)
        k.dma("sp", biasL[:], b_in[2560:2576].rearrange("(p o) -> p o", o=1), w=[biasL], allow_slow_non_contiguous=True)
        k.dma("sp", biasB[:], b_in[2576:9232].rearrange("(j p) -> p j", p=128), w=[biasB], allow_slow_non_contiguous=True)
        bbc = {}
        for name in TOKMAJ:
            bbc[name] = k.sbuf("bbc_" + name, [128, 512], F32)
            c0 = SEG[name][0]
            bcast_load(k, "sp", bbc[name], b_in[c0:c0 + 512])
        wblk = [k.sbuf("wblk%d" % i, [128, 8, 512], BF16) for i in range(2)]
        ps = [k.psum("ips%d" % i, [128, 512], F32) for i in range(4)]
        zrow = [k.sbuf("zrow%d" % i, [128, S], BF16) for i in range(2)]
        ztile = [k.sbuf("ztile%d" % i, [128, 512], BF16) for i in range(3)]
        blocks = []
        for name, (c0, w) in SEG.items():
            for o in range(0, w, 512):
                blocks.append((name, c0 + o, min(512, w - o)))
        pi = 0
        ci = 0
        zi = 0
        for bi, (name, c0, w) in enumerate(blocks):
            WB = wblk[bi % 2]
            k.dma("pool", WB[:, :, :w], w_in[:, :, c0:c0 + w], w=[WB])
            if name in TOKMAJ:
                for i in range(NT):
                    P = ps[pi % 4]
                    pi += 1
                    for kk in range(8):
                        mm(k, P[:, :w], c.hT[:, kk, i * 128:(i + 1) * 128], WB[:, kk, :w], kk == 0, kk == 7,
                           [c.hT_b[i], WB], [P])
                    Z = ztile[zi % 3]
                    zi += 1
                    tt(k, "dve", Z[:, :w], P[:, :w], bbc[name][:, :w], ALU.add, [P, bbc[name]], [Z])
                    k.dma("sp", c.ztok[name].t.ap()[i * 128:(i + 1) * 128, :], Z[:, :w], r=[Z], w=[c.ztok[name]])
            else:
                for cc in range(0, w, 128):
                    cw = min(128, w - cc)
                    a0 = c0 + cc
                    if a0 < 2560:
                        bias = biasA[:cw, a0 // 128:a0 // 128 + 1]
                        bt = biasA
                    elif a0 == 2560:
                        bias = biasL[:cw, 0:1]
                        bt = biasL
                    else:
                        j = (a0 - 2576) // 128
                        bias = biasB[:cw, j:j + 1]
                        bt = biasB
                    ZT = zrow[ci % 2]
                    ci += 1
                    for tg in range(NTG):
                        P = ps[pi % 4]
                        pi += 1
                        for kk in range(8):
                            mm(k, P[:cw, :], WB[:, kk, cc:cc + cw], c.hT[:, kk, tg * 512:(tg + 1) * 512], kk == 0, kk == 7,
                               hTr(c, tg) + [WB], [P])
                        if tg % 2 == 0:
                            act(k, ZT[:cw, tg * 512:(tg + 1) * 512], P[:cw, :], AF.Identity, [P, bt], [ZT], bias=bias, scale=1.0)
                        else:
                            ts(k, "dve", ZT[:cw, tg * 512:(tg + 1) * 512], P[:cw, :], bias, None, ALU.add, None, [P, bt], [ZT])
                    k.dma("sp", c.zT.t.ap()[a0:a0 + cw, :], ZT[:cw, :], r=[ZT], w=[c.zT_b])


def load_cols(k, c, name, rows, width=512):
    R = sum(r for _, r in rows)
    nch = width // 128
    stg = k.sbuf(name + "_stg", [R, width], F32)
    off = 0
    for ap, r in rows:
        k.dma("sp", stg[off:off + r, :], ap, w=[stg])
        off += r
    P = k.psum(name + "_ps", [128, nch, R], F32)
    out = k.sbuf(name, [128, nch, R], F32)
    for cc in range(nch):
        k.op("pe", lambda e, cc=cc: e.transpose(out=P[:, cc, :], in_=stg[:R, cc * 128:(cc + 1) * 128],
                                                identity=c.ident_f[:R, :R]), r=[stg, c.ident_f], w=[P])
    cp(k, "dve", out[:], P[:], [P], [out])
    return out


def row(ap):
    return ap.rearrange("(o n) -> o n", o=1)


def stage_A(k, c, l):
    W = c.W
    with k.scope():
        cols = load_cols(k, c, "acol", [(W["conv_a_w"].ap()[l], 31), (row(W["conv_a_b"].ap()[l]), 1),
                                        (row(W["ln_a_g"].ap()[l]), 1), (row(W["ln_a_b"].ap()[l]), 1)])
        diag = k.sbuf("diag", [128, 4, 31, 128], BF16)
        n = 0
        for cc in range(4):
            for j in range(31):
                ts(k, "dve", diag[:, cc, j, :], c.ident_f[:], cols[:, cc, j:j + 1], None, ALU.mult, None,
                   [c.ident_f, cols], [diag])
                n += 1
        upad = k.sbuf("upad", [128, 4, 30 + S], BF16)
        upb = [Buf("upad%d" % i) for i in range(4)]
        avt = [k.sbuf("avt%d" % i, [128, 2048], BF16) for i in range(2)]
        agt = [k.sbuf("agt%d" % i, [128, 2048], BF16) for i in range(2)]
        sg = [k.sbuf("asg%d" % i, [128, 2048], F32) for i in range(2)]
        n = 0
        for cc in range(4):
            k.op("pool", lambda e, cc=cc: e.memset(upad[:, cc, 0:30], 0.0), w=[upb[cc]])
            for hf in range(2):
                AV, AG, SG = avt[n % 2], agt[n % 2], sg[n % 2]
                n += 1
                k.dma("sp", AV[:], c.zT.t.ap()[cc * 128:(cc + 1) * 128, hf * 2048:(hf + 1) * 2048], r=[c.zT_b], w=[AV])
                k.dma("sp", AG[:], c.zT.t.ap()[512 + cc * 128:512 + (cc + 1) * 128, hf * 2048:(hf + 1) * 2048], r=[c.zT_b], w=[AG])
                act(k, SG[:], AG[:], AF.Sigmoid, [AG], [SG])
                tt(k, "dve" if hf == 0 else "pool", upad[:, cc, 30 + hf * 2048:30 + (hf + 1) * 2048], AV[:], SG[:], ALU.mult, [AV, SG], [upb[cc]])
        ones512 = k.sbuf("ones512", [128, 128], F32)
        k.op("pool", lambda e: e.memset(ones512[:], 1.0 / 512.0), w=[ones512])
        co = [k.sbuf("co%d" % i, [128, 512], F32) for i in range(4)]
        sq = [k.sbuf("sq%d" % i, [128, 512], F32) for i in range(4)]
        psc = [k.psum("psc%d" % i, [128, 512], F32) for i in range(2)]
        pmean = k.psum("pmean", [128, 512], F32)
        pex2 = k.psum("pex2", [128, 512], F32)
        mean = k.sbuf("amean", [128, 512], F32)
        msq = k.sbuf("amsq", [128, 512], F32)
        rstd = k.sbuf("arstd", [128, 512], F32)
        xn = [k.sbuf("axn%d" % i, [128, 512], F32) for i in range(2)]
        yt = [k.sbuf("ayt%d" % i, [128, 512], BF16) for i in range(2)]
        n = 0
        for tg in range(NTG):
            for cc in range(4):
                P = psc[n % 2]
                n += 1
                for j in range(31):
                    mm(k, P[:], diag[:, cc, j, :], upad[:, cc, tg * 512 + j:tg * 512 + j + 512], j == 0, j == 30, [diag, upb[cc]], [P])
                act(k, co[cc][:], P[:], AF.Identity, [P, cols], [co[cc]], bias=cols[:, cc, 31:32], scale=1.0)
                tt(k, "pool", sq[cc][:], co[cc][:], co[cc][:], ALU.mult, [co[cc]], [sq[cc]])
            for cc in range(4):
                mm(k, pmean[:], ones512[:], co[cc][:], cc == 0, cc == 3, [ones512, co[cc]], [pmean])
            for cc in range(4):
                mm(k, pex2[:], ones512[:], sq[cc][:], cc == 0, cc == 3, [ones512, sq[cc]], [pex2])
            cp(k, "act", mean[:], pmean[:], [pmean], [mean])
            act(k, msq[:], pmean[:], AF.Square, [pmean], [msq])
            tt(k, "dve", rstd[:], pex2[:], msq[:], ALU.subtract, [pex2, msq], [rstd])
            act(k, rstd[:], rstd[:], AF.Ln, [rstd], [rstd], bias=EPS, scale=1.0)
            act(k, rstd[:], rstd[:], AF.Exp, [rstd], [rstd], scale=-0.5)
            for cc in range(4):
                X = xn[cc % 2]
                Y = yt[cc % 2]
                tt(k, "dve", X[:], co[cc][:], mean[:], ALU.subtract, [co[cc], mean], [X])
                tt(k, "pool", X[:], X[:], rstd[:], ALU.mult, [X, rstd], [X])
                act(k, Y[:], X[:], AF.Silu, [X, cols], [Y], bias=cols[:, cc, 33:34], scale=cols[:, cc, 32:33])
                k.dma("sp", c.yT.t.ap()[cc * 128:(cc + 1) * 128, tg * 512:(tg + 1) * 512], Y[:], r=[Y], w=[c.yT.b])


LAYER_STAGES.append(("A", stage_A))


def stage_D(k, c, l):
    W = c.W
    with k.scope():
        cols = load_cols(k, c, "dcol", [(W["conv_d_w"].ap()[l], 4), (row(W["conv_d_b"].ap()[l]), 1), (row(W["lru_ba"].ap()[l]), 1),
                                        (row(W["lru_bx"].ap()[l]), 1), (row(W["lru_lambda"].ap()[l]), 1)])
        xe = k.sbuf("d_xe", [128, 4], F32)
        ln1 = k.sbuf("d_ln1", [128, 4], F32)
        tq = k.sbuf("d_tq", [128, 4], F32)
        msk = k.sbuf("d_msk", [128, 4], F32)
        cl = k.sbuf("d_cl", [128, 4], F32)
        cl2 = k.sbuf("d_cl2", [128, 4], F32)
        act(k, xe[:], cols[:, :, 7], AF.Exp, [cols], [xe], scale=-1.0)
        act(k, ln1[:], xe[:], AF.Ln, [xe], [ln1], bias=1.0, scale=1.0)
        ts(k, "dve", tq[:], xe[:], -0.25, 1.0 / 3.0, ALU.mult, ALU.add, [xe], [tq])
        tt(k, "dve", tq[:], tq[:], xe[:], ALU.mult, [tq, xe], [tq])
        ts(k, "dve", tq[:], tq[:], -1.0, 0.5, ALU.mult, ALU.add, [tq], [tq])
        tt(k, "dve", tq[:], tq[:], xe[:], ALU.mult, [tq, xe], [tq])
        ts(k, "dve", tq[:], tq[:], -1.0, 1.0, ALU.mult, ALU.add, [tq], [tq])
        tt(k, "dve", tq[:], tq[:], xe[:], ALU.mult, [tq, xe], [tq])
        ts(k, "dve", msk[:], xe[:], 0.05, None, ALU.is_lt, None, [xe], [msk])
        tt(k, "dve", tq[:], tq[:], ln1[:], ALU.subtract, [tq, ln1], [tq])
        tt(k, "dve", tq[:], tq[:], msk[:], ALU.mult, [tq, msk], [tq])
        tt(k, "dve", tq[:], tq[:], ln1[:], ALU.add, [tq, ln1], [tq])
        ts(k, "dve", cl[:], tq[:], -8.0, None, ALU.mult, None, [tq], [cl])
        ts(k, "dve", cl2[:], tq[:], -16.0, None, ALU.mult, None, [tq], [cl2])
        bd = {}
        for nm in ("lru_wa", "lru_wx"):
            stg = k.sbuf(nm + "_stg", [128, 4, 128], F32)
            k.op("pool", lambda e, stg=stg: e.memset(stg[:], 0.0), w=[stg])
            for cc in range(4):
                k.dma("sp", stg[0:64, cc, 0:64], W[nm].ap()[l, 2 * cc], w=[stg])
                k.dma("sp", stg[64:128, cc, 64:128], W[nm].ap()[l, 2 * cc + 1], w=[stg])
            bd[nm] = k.sbuf(nm + "_bd", [128, 4, 128], BF16)
            cp(k, "dve", bd[nm][:], stg[:], [stg], [bd[nm]])
        HW_ = 2048
        xin = k.sbuf("d_xin", [128, 3 + HW_], BF16)
        dg = k.sbuf("d_dg", [128, HW_], BF16)
        xc = k.sbuf("d_xc", [128, HW_], F32)
        xcb = k.sbuf("d_xcb", [128, HW_], BF16)
        rr = k.sbuf("d_r", [128, HW_], F32)
        ii = k.sbuf("d_i", [128, HW_], F32)
        aa = k.sbuf("d_a", [128, HW_], F32)
        s1 = k.sbuf("d_s1", [128, HW_], F32)
        hh = [k.sbuf("d_h%d" % i, [128, HW_], F32) for i in range(2)]
        g1 = k.sbuf("d_g1", [128, HW_], F32)
        g2 = k.sbuf("d_g2", [128, HW_], F32)
        yo = k.sbuf("d_yo", [128, HW_], BF16)
        pg = [k.psum("d_pg%d" % i, [128, 512], F32) for i in range(4)]
        n = 0
        for cc in range(4):
            r0 = SEG["d_x"][0] + cc * 128
            g0 = SEG["d_g"][0] + cc * 128
            for hf in range(2):
                if hf == 0:
                    k.op("pool", lambda e: e.memset(xin[:, 0:3], 0.0), w=[xin])
                    k.dma("sp", xin[:, 3:3 + HW_], c.zT.t.ap()[r0:r0 + 128, 0:HW_], r=[c.zT_b], w=[xin])
                else:
                    k.dma("sp", xin[:], c.zT.t.ap()[r0:r0 + 128, HW_ - 3:2 * HW_], r=[c.zT_b], w=[xin])
                k.dma("sp", dg[:], c.zT.t.ap()[g0:g0 + 128, hf * HW_:(hf + 1) * HW_], r=[c.zT_b], w=[dg])
                ts(k, "dve", xc[:], xin[:, 0:HW_], cols[:, cc, 0:1], cols[:, cc, 4:5], ALU.mult, ALU.add, [xin, cols], [xc])
                for j in range(1, 4):
                    stt(k, xc[:], xin[:, j:j + HW_], cols[:, cc, j:j + 1], xc[:], ALU.mult, ALU.add, [xin, cols, xc], [xc])
                cp(k, "pool", xcb[:], xc[:], [xc], [xcb])
                for q in range(4):
                    sl = slice(q * 512, (q + 1) * 512)
                    P = pg[n % 4]
                    n += 1
                    mm(k, P[:], bd["lru_wa"][:, cc, :], xcb[:, sl], True, True, [bd["lru_wa"], xcb], [P])
                    act(k, rr[:, sl], P[:], AF.Sigmoid, [P, cols], [rr], bias=cols[:, cc, 5:6], scale=1.0)
                    P = pg[n % 4]
                    n += 1
                    mm(k, P[:], bd["lru_wx"][:, cc, :], xcb[:, sl], True, True, [bd["lru_wx"], xcb], [P])
                    act(k, ii[:, sl], P[:], AF.Sigmoid, [P, cols], [ii], bias=cols[:, cc, 6:7], scale=1.0)
                act(k, aa[:], rr[:], AF.Exp, [rr, cl], [aa], scale=cl[:, cc:cc + 1])
                act(k, s1[:], rr[:], AF.Exp, [rr, cl2], [s1], scale=cl2[:, cc:cc + 1])
                act(k, s1[:], s1[:], AF.Sqrt, [s1], [s1], bias=1.0, scale=-1.0)
                tt(k, "pool", ii[:], ii[:], xc[:], ALU.mult, [ii, xc], [ii])
                tt(k, "pool", ii[:], ii[:], s1[:], ALU.mult, [ii, s1], [ii])
                H = hh[hf]
                init = 0.0 if hf == 0 else hh[0][:, HW_ - 1:HW_]
                k.op("dve", lambda e, H=H, init=init: e.tensor_tensor_scan(out=H[:], data0=aa[:], data1=ii[:], initial=init,
                                                                           op0=ALU.mult, op1=ALU.add), r=[aa, ii, hh[0]], w=[H])
                tt(k, "pool", g1[:], dg[:], dg[:], ALU.mult, [dg], [g1])
                ts(k, "dve", g1[:], g1[:], 0.044715, 1.0, ALU.mult, ALU.add, [g1], [g1])
                tt(k, "pool", g1[:], g1[:], dg[:], ALU.mult, [g1, dg], [g1])
                act(k, g2[:], g1[:], AF.Sigmoid, [g1], [g2], scale=1.5957691216057308)
                tt(k, "pool", g2[:], g2[:], dg[:], ALU.mult, [g2, dg], [g2])
                tt(k, "dve", yo[:], H[:], g2[:], ALU.mult, [H, g2], [yo])
                k.dma("sp", c.yT.t.ap()[1536 + cc * 128:1536 + (cc + 1) * 128, hf * HW_:(hf + 1) * HW_], yo[:], r=[yo], w=[c.yT.b])


LAYER_STAGES.append(("D", stage_D))


def stage_B(k, c, l):
    W = c.W
    zT = c.zT.t.ap()
    with k.scope():
        qd = [k.sbuf("b_qd%d" % i, [128, S], BF16) for i in range(2)]
        ki = [k.sbuf("b_ki%d" % i, [128, S], BF16) for i in range(2)]
        ke = [k.sbuf("b_ke%d" % i, [128, S], BF16) for i in range(2)]
        dec = [k.sbuf("b_dec%d" % i, [128, 64], F32) for i in range(2)]
        with k.scope():
            cols = load_cols(k, c, "bcol", [(row(W["gla_ba"].ap()[l]), 1)], width=256)
            nba = k.sbuf("b_nba", [128, 2], F32)
            ts(k, "dve", nba[:], cols[:, :, 0], -1.0, None, ALU.mult, None, [cols], [nba])
            wa2f = k.sbuf("b_wa2f", [16, 256], F32)
            wa2b = k.sbuf("b_wa2b", [16, 256], BF16)
            k.dma("sp", wa2f[:], W["gla_wa2"].ap()[l], w=[wa2f])
            cp(k, "dve", wa2b[:], wa2f[:], [wa2f], [wa2b])
            lrT = k.sbuf("b_lrT", [16, S], BF16)
            k.dma("sp", lrT[:], zT[2560:2576, :], r=[c.zT_b], w=[lrT])
            rmask = k.sbuf("b_rmask", [128, S], F32)
            k.dma("sp", rmask[:], c.consts["rmask"].ap(), w=[rmask])
            la = k.sbuf("b_la", [128, S], F32)
            bb = k.sbuf("b_bb", [128, S], F32)
            tf = la
            qT = k.sbuf("b_qT", [128, S], BF16)
            kT = k.sbuf("b_kT", [128, S], BF16)
            pl = [k.psum("b_pl%d" % i, [128, 512], F32) for i in range(2)]
            for ch in range(2):
                k.dma("sp", qT[:], zT[1024 + ch * 128:1024 + (ch + 1) * 128, :], r=[c.zT_b], w=[qT])
                k.dma("sp", kT[:], zT[1280 + ch * 128:1280 + (ch + 1) * 128, :], r=[c.zT_b], w=[kT])
                for tg in range(NTG):
                    P = pl[tg % 2]
                    mm(k, P[:], wa2b[:, ch * 128:(ch + 1) * 128], lrT[:, tg * 512:(tg + 1) * 512], True, True, [wa2b, lrT], [P])
                    act(k, la[:, tg * 512:(tg + 1) * 512], P[:], AF.Exp, [P, nba], [la], bias=nba[:, ch:ch + 1], scale=-1.0)
                act(k, la[:], la[:], AF.Ln, [la], [la], bias=1.0, scale=1.0)
                ts(k, "dve", la[:], la[:], -1.0 / 16.0, None, ALU.mult, None, [la], [la])
                k.op("dve", lambda e: e.tensor_tensor_scan(out=bb[:], data0=rmask[:], data1=la[:], initial=0.0,
                                                           op0=ALU.mult, op1=ALU.add), r=[rmask, la], w=[bb])
                bbv = bb[:].rearrange("p (c t) -> p c t", t=64)
                act(k, dec[ch][:], bbv[:, :, 63], AF.Exp, [bb], [dec[ch]])
                act(k, tf[:], bb[:], AF.Exp, [bb], [tf])
                stt(k, qd[ch][:], qT[:], 0.125, tf[:], ALU.mult, ALU.mult, [qT, tf], [qd[ch]])
                act(k, tf[:], bb[:], AF.Exp, [bb], [tf], scale=-1.0)
                tt(k, "pool", ki[ch][:], kT[:], tf[:], ALU.mult, [kT, tf], [ki[ch]])
                tfv = tf[:].rearrange("p (c t) -> p c t", t=64)
                tt(k, "dve", tfv, bbv[:, :, 63:64].to_broadcast([128, 64, 64]), bbv, ALU.subtract, [bb], [tf])
                act(k, tf[:], tf[:], AF.Exp, [tf], [tf])
                tt(k, "pool", ke[ch][:], kT[:], tf[:], ALU.mult, [kT, tf], [ke[ch]])
        tri = k.sbuf("b_tri", [64, 64], F32)
        k.dma("sp", tri[:], c.consts["tri64"].ap(), w=[tri])
        gn = k.sbuf("b_gn", [64, 512], F32)
        for h in range(4):
            k.dma("sp", gn[:, h * 128:(h + 1) * 128], W["gla_norm_g"].ap()[l].partition_broadcast(64), w=[gn])
        ybT = k.sbuf("b_ybT", [128, 4, S], BF16)
        vt = [k.sbuf("b_vt%d" % i, [64, 512], BF16) for i in range(3)]
        rt = [k.sbuf("b_rt%d" % i, [64, 512], BF16) for i in range(3)]
        ket = [k.sbuf("b_ket%d" % i, [64, 256], BF16) for i in range(2)]
        Sst = [k.sbuf("b_S%d" % i, [128, 128], F32) for i in range(2)]
        Sbf = [k.sbuf("b_Sbf%d" % i, [128, 128], BF16) for i in range(2)]
        stm = [k.sbuf("b_stm%d" % i, [64, 64], BF16) for i in range(2)]
        osb = k.sbuf("b_osb", [64, 512], F32)
        sqv = k.sbuf("b_sqv", [64, 512], F32)
        ssum = k.sbuf("b_ssum", [64, 4], F32)
        sig = k.sbuf("b_sig", [64, 512], F32)
        yb = k.sbuf("b_yb", [64, 512], BF16)
        for ch in range(2):
            k.op("pool", lambda e, ch=ch: e.memset(Sst[ch][:], 0.0), w=[Sst[ch]])
            k.op("pool", lambda e, ch=ch: e.memset(Sbf[ch][:], 0.0), w=[Sbf[ch]])
        pT = k.psum("b_pT", [64, 256], BF16)
        pS = [k.psum("b_pS%d" % i, [64, 64], F32) for i in range(2)]
        pO = [k.psum("b_pO%d" % i, [64, 512], F32) for i in range(2)]
        pKV = [k.psum("b_pKV%d" % i, [128, 128], F32) for i in range(2)]
        pY = k.psum("b_pY", [128, 4, 64], BF16)
        n = 0
        posts = []

        def post(PO, RT, sl):
            _gla_post(k, c, PO, RT, sl, osb, sqv, ssum, sig, yb, gn, pY, ybT)
        for ci in range(64):
            t0 = ci * 64
            sl = slice(t0, t0 + 64)
            VT, RT, KET, PO = vt[ci % 3], rt[ci % 3], ket[ci % 2], pO[ci % 2]
            k.dma("sp", VT[:], c.ztok["b_v"].t.ap()[t0:t0 + 64, :], r=[c.ztok["b_v"]], w=[VT])
            k.dma("sp", RT[:], c.ztok["b_r"].t.ap()[t0:t0 + 64, :], r=[c.ztok["b_r"]], w=[RT])
            for ch in range(2):
                k.op("pe", lambda e, ch=ch, sl=sl: e.transpose(out=pT[:, ch * 128:(ch + 1) * 128], in_=ke[ch][:, sl], identity=c.ident_bf[:]),
                     r=[ke[ch], c.ident_bf], w=[pT])
            cp(k, "act", KET[:], pT[:], [pT], [KET])
            for h in range(4):
                ch, pb = h // 2, (h % 2) * 64
                PS, STM, PK = pS[n % 2], stm[n % 2], pKV[n % 2]
                n += 1
                hs = slice(h * 128, (h + 1) * 128)
                mm(k, PS[:], ki[ch][pb:pb + 64, sl], qd[ch][pb:pb + 64, sl], True, True, [ki[ch], qd[ch]], [PS])
                tt(k, "dve", STM[:], PS[:], tri[:], ALU.mult, [PS, tri], [STM])
                mm(k, PO[:, hs], STM[:], VT[:, hs], True, False, [STM, VT], [PO])
                mm(k, PO[:, hs], qd[ch][pb:pb + 64, sl], Sbf[ch][pb:pb + 64, :], False, True, [qd[ch], Sbf[ch]], [PO])
                mm(k, PK[:], KET[:, ch * 128:(ch + 1) * 128], VT[:, hs], True, True, [KET, VT], [PK])
                stt(k, Sst[ch][pb:pb + 64, :], Sst[ch][pb:pb + 64, :], dec[ch][pb:pb + 64, ci:ci + 1], PK[pb:pb + 64, :], ALU.mult, ALU.add,
                    [Sst[ch], dec[ch], PK], [Sst[ch]])
                cp(k, "pool", Sbf[ch][pb:pb + 64, :], Sst[ch][pb:pb + 64, :], [Sst[ch]], [Sbf[ch]])
            posts.append((PO, RT, sl))
            if len(posts) > 1:
                post(*posts.pop(0))
        post(*posts.pop(0))
        for j in range(4):
            k.dma("sp", c.yT.t.ap()[512 + j * 128:512 + (j + 1) * 128, :], ybT[:, j, :], r=[ybT], w=[c.yT.b])


def _gla_post(k, c, PO, RT, sl, osb, sqv, ssum, sig, yb, gn, pY, ybT):
    if True:
        if True:
            cp(k, "act", osb[:], PO[:], [PO], [osb])
            tt(k, "pool", sqv[:], osb[:], osb[:], ALU.mult, [osb], [sqv])
            k.op("dve", lambda e: e.tensor_reduce(out=ssum[:], in_=sqv[:].rearrange("p (h d) -> p h d", h=4), axis=AX.X, op=ALU.add),
                 r=[sqv], w=[ssum])
            act(k, ssum[:], ssum[:], AF.Ln, [ssum], [ssum], bias=EPS, scale=1.0 / 128.0)
            act(k, ssum[:], ssum[:], AF.Exp, [ssum], [ssum], scale=-0.5)
            osv = osb[:].rearrange("p (h d) -> p h d", h=4)
            tt(k, "dve", osv, osv, ssum[:].rearrange("p (h o) -> p h o", o=1).to_broadcast([64, 4, 128]), ALU.mult, [osb, ssum], [osb])
            tt(k, "pool", osb[:], osb[:], gn[:], ALU.mult, [osb, gn], [osb])
            act(k, sig[:], RT[:], AF.Sigmoid, [RT], [sig])
            tt(k, "pool", sig[:], sig[:], RT[:], ALU.mult, [sig, RT], [sig])
            tt(k, "dve", yb[:], osb[:], sig[:], ALU.mult, [osb, sig], [yb])
            for j in range(4):
                k.op("pe", lambda e, j=j: e.transpose(out=pY[:, j, :], in_=yb[:, j * 128:(j + 1) * 128], identity=c.ident_bf[:64, :64]),
                     r=[yb, c.ident_bf], w=[pY])
            cp(k, "dve", ybT[:, :, sl], pY[:], [pY], [ybT])


LAYER_STAGES.append(("B", stage_B))


def stage_C(k, c, l):
    zT = c.zT.t.ap()
    with k.scope():
        uf = k.sbuf("c_uf", [128, 128], F32)
        Ub = k.sbuf("c_Ub", [128, 128], BF16)
        onesb = k.sbuf("c_ones", [128, 128], BF16)
        k.dma("sp", uf[:], c.consts["U"].ap(), w=[uf])
        ts(k, "dve", Ub[:], uf[:], -1.0, None, ALU.mult, None, [uf], [Ub])
        k.op("pool", lambda e: e.memset(onesb[:], -1.0), w=[onesb])
        mk = k.sbuf("c_mk", [128, 4, 512], F32)
        k.dma("sp", mk[:], c.consts["sbmask"].ap(), w=[mk])
        cv = k.sbuf("c_cv", [128, 32, 512], BF16)
        k.dma("sp", cv[:], c.ztok["c_v"].t.ap().rearrange("(j p) n -> p j n", p=128), r=[c.ztok["c_v"]], w=[cv])
        yct = k.sbuf("c_yct", [128, 4, S], BF16)
        qT = [k.sbuf("c_qT%d" % i, [128, S], BF16) for i in range(1)]
        kT = [k.sbuf("c_kT%d" % i, [128, S], BF16) for i in range(1)]
        E = [k.sbuf("c_E%d" % i, [128, 512], F32) for i in range(2)]
        SP = [k.sbuf("c_SP%d" % i, [128, 512], F32) for i in range(3)]
        LK = [k.sbuf("c_LK%d" % i, [128, 512], BF16) for i in range(4)]
        T1 = [k.sbuf("c_T1%d" % i, [128, 512], F32) for i in range(6)]
        WT = [k.sbuf("c_WT%d" % i, [128, 512], BF16) for i in range(4)]
        pz = [k.psum("c_pz%d" % i, [128, 512], F32) for i in range(2)]
        pt = [k.psum("c_pt%d" % i, [128, 512], F32) for i in range(2)]
        pc = [k.psum("c_pc%d" % i, [128, 512], F32) for i in range(2)]
        po = [k.psum("c_po%d" % i, [128, 512], F32) for i in range(2)]
        Rb = [k.sbuf("c_R%d" % i, [128, 512], F32) for i in range(4)]
        for ch in range(4):
            Q, Kt = qT[0], kT[0]
            k.dma("sp", Q[:], zT[2576 + ch * 128:2576 + (ch + 1) * 128, :], r=[c.zT_b], w=[Q])
            k.dma("sp", Kt[:], zT[3088 + ch * 128:3088 + (ch + 1) * 128, :], r=[c.zT_b], w=[Kt])
            tiles = []
            for hh in range(2):
                for tg in range(NTG):
                    Js = list(range(4 * tg + 3, -1, -1))
                    for idx, J in enumerate(Js):
                        tiles.append(dict(hh=hh, tg=tg, J=J, first=idx == 0, last=idx == len(Js) - 1, grp=hh * NTG + tg))

            def info(n, t):
                pb = t["hh"] * 64
                J, tg = t["J"], t["tg"]
                return pb, J, tg, slice(tg * 512, (tg + 1) * 512), J >= 4 * tg, J - 4 * tg

            def stA(n, t):
                pb, J, tg, ts_, dg, jj = info(n, t)
                PZ, e_, sp_, lk_, t1_ = pz[n % 2], E[n % 2], SP[n % 3], LK[n % 4], T1[n % 6]
                mm(k, PZ[:], Kt[pb:pb + 64, J * 128:(J + 1) * 128], Q[pb:pb + 64, ts_], True, True, [Kt, Q], [PZ])
                act(k, e_[:], PZ[:], AF.Exp, [PZ], [e_], scale=0.125)
                act(k, sp_[:], e_[:], AF.Ln, [e_], [sp_], bias=1.0, scale=1.0)
                if dg:
                    tt(k, "pool", lk_[:], sp_[:], mk[:, jj, :], ALU.mult, [sp_, mk], [lk_])
                else:
                    cp(k, "act", lk_[:], sp_[:], [sp_], [lk_])
                stt(k, t1_[:], PZ[:], 0.125, sp_[:], ALU.mult, ALU.subtract, [PZ, sp_], [t1_])

            def stB(n, t):
                pb, J, tg, ts_, dg, jj = info(n, t)
                PT, PC, lk_, t1_ = pt[n % 2], pc[n % 2], LK[n % 4], T1[n % 6]
                mm(k, PT[:], Ub[:], lk_[:], True, True, [Ub, lk_], [PT])
                if not t["last"]:
                    mm(k, PC[:], onesb[:], lk_[:], True, True, [onesb, lk_], [PC])
                tt(k, "dve", t1_[:], PT[:], t1_[:], ALU.add, [PT, t1_], [t1_])
                if not t["first"]:
                    tt(k, "pool", t1_[:], t1_[:], Rb[n % 4][:], ALU.add, [t1_, Rb[n % 4]], [t1_])
                if not t["last"]:
                    if t["first"]:
                        cp(k, "dve", Rb[(n + 1) % 4][:], PC[:], [PC], [Rb[(n + 1) % 4]])
                    else:
                        tt(k, "dve", Rb[(n + 1) % 4][:], PC[:], Rb[n % 4][:], ALU.add, [PC, Rb[n % 4]], [Rb[(n + 1) % 4]])

            def stC(n, t):
                pb, J, tg, ts_, dg, jj = info(n, t)
                t1_, w_ = T1[n % 6], WT[n % 4]
                act(k, w_[:], t1_[:], AF.Exp, [t1_], [w_])
                if dg:
                    tt(k, "pool", w_[:], w_[:], mk[:, jj, :], ALU.mult, [w_, mk], [w_])

            def stD(n, t):
                pb, J, tg, ts_, dg, jj = info(n, t)
                w_ = WT[n % 4]
                PO = po[t["grp"] % 2]
                mm(k, PO[:], cv[:, J, ch * 128:(ch + 1) * 128], w_[:], t["first"], t["last"], [cv, w_], [PO])
                if t["last"]:
                    cp(k, "act", yct[pb:pb + 64, ch, ts_], PO[pb:pb + 64, :], [PO], [yct])

            NTL = len(tiles)
            for step in range(NTL + 3):
                for off, fn in ((0, stA), (1, stB), (2, stC), (3, stD)):
                    n = step - off
                    if 0 <= n < NTL:
                        fn(n, tiles[n])
        for ch in range(4):
            k.dma("sp", c.yT.t.ap()[1024 + ch * 128:1024 + (ch + 1) * 128, :], yct[:, ch, :], r=[yct], w=[c.yT.b])


LAYER_STAGES.append(("C", stage_C))


def stage_M(k, c, l):
    W = c.W
    zT = c.zT.t.ap()
    with k.scope():
        wbs = [k.sbuf("m_wb%d" % i, [128, 4, 1024], BF16) for i in range(2)]
        wo = k.sbuf("m_wo", [128, 8, 1024], BF16)
        k.dma("pool", wo[:], W["w_out"].ap()[l].rearrange("(kk p) d -> p kk d", p=128), w=[wo])
        ln_alloc(k, c, "1")
        bcast_load(k, "sp", c.ln_g, W["ln1_g"].ap()[l])
        bcast_load(k, "sp", c.ln_b, W["ln1_b"].ap()[l])
        bo = k.sbuf("m_bo", [128, 1024], F32)
        bcast_load(k, "sp", bo, W["b_out"].ap()[l])
        yt = [k.sbuf("m_yt%d" % i, [128, 16, 512], BF16) for i in range(1)]
        gt = [k.sbuf("m_gt%d" % i, [128, 8, 512], BF16) for i in range(2)]
        sg = [k.sbuf("m_sg%d" % i, [128, 512], F32) for i in range(2)]
        tmp = [k.sbuf("m_tmp%d" % i, [128, 512], F32) for i in range(2)]
        acc = k.sbuf("m_acc", [128, 8, 512], F32)
        mT = [k.sbuf("m_mT%d" % i, [128, 8, 512], BF16) for i in range(1)]
        V = [k.sbuf("m_V%d" % i, [128, 1024], F32) for i in range(2)]
        hold = [k.sbuf("m_hold%d" % i, [128, 1024], F32) for i in range(2)]
        pp = [k.psum("m_pp%d" % i, [128, 512], F32) for i in range(2)]
        pm = [k.psum("m_pm%d" % i, [128, 512], F32) for i in range(4)]
        n = 0
        ng = 0
        pend = [None]

        def mload(g):
            tg_, nb_ = divmod(g, 4)
            tsl_ = slice(tg_ * 512, (tg_ + 1) * 512)
            k.dma("pool", wbs[g % 2][:], W["w_branch"].ap()[l, nb_].rearrange("(cc p) d -> p cc d", p=128), w=[wbs[g % 2]])
            g0 = SEG["g_m"][0] + nb_ * 1024
            k.dma("sp", gt[g % 2][:], zT[g0:g0 + 1024, :].rearrange("(j p) t -> p j t", p=128)[:, :, tsl_], r=[c.zT_b], w=[gt[g % 2]])

        for tg in range(NTG):
            ts_ = slice(tg * 512, (tg + 1) * 512)
            YT, MT = yt[0], mT[0]
            k.dma("sp", YT[:], c.yT.t.ap().rearrange("(j p) t -> p j t", p=128)[:, :, ts_], r=[c.yT.b], w=[YT])
            for nb in range(4):
                GT = gt[ng % 2]
                wb = wbs[ng % 2]
                if ng == 0:
                    mload(0)
                if ng + 1 < 4 * NTG:
                    mload(ng + 1)
                ng += 1
                for dc in range(8):
                    P, SG, TM = pp[n % 2], sg[n % 2], tmp[n % 2]
                    n += 1
                    for cc in range(4):
                        mm(k, P[:], wb[:, cc, dc * 128:(dc + 1) * 128], YT[:, nb * 4 + cc, :], cc == 0, cc == 3, [wb, YT], [P])
                    act(k, SG[:], GT[:, dc, :], AF.Sigmoid, [GT], [SG])
                    if nb == 0:
                        tt(k, "dve", acc[:, dc, :], P[:], SG[:], ALU.mult, [P, SG], [acc])
                    else:
                        tt(k, "dve", TM[:], P[:], SG[:], ALU.mult, [P, SG], [TM])
                        if nb < 3:
                            tt(k, "pool", acc[:, dc, :], acc[:, dc, :], TM[:], ALU.add, [acc, TM], [acc])
                        else:
                            tt(k, "pool", MT[:, dc, :], acc[:, dc, :], TM[:], ALU.add, [acc, TM], [MT])
            for t4 in range(4):
                i = tg * 4 + t4
                H, VV = hold[i % 2], V[i % 2]
                k.dma("sp", H[:], c.h_tok.t.ap()[i * 128:(i + 1) * 128, :], r=[c.h_tok_b[i]], w=[H])
                for hf in range(2):
                    P = pm[(i % 2) * 2 + hf]
                    hs = slice(hf * 512, (hf + 1) * 512)
                    for dc in range(8):
                        mm(k, P[:], MT[:, dc, t4 * 128:(t4 + 1) * 128], wo[:, dc, hs], dc == 0, dc == 7, [MT, wo], [P])
                    stt(k, VV[:, hs], H[:, hs], ALPHA, P[:], ALU.mult, ALU.add, [H, P], [VV])
                tt(k, "pool", VV[:], VV[:], bo[:], ALU.add, [VV, bo], [VV])
                nb_ = ln_tile(k, c, i, VV, defer=True)
                if pend[0] is not None:
                    pend[0]()
                pend[0] = nb_
        pend[0]()


LAYER_STAGES.append(("M", stage_M))


SPARSE_LN2 = True


def stage_X(k, c, l):
    W = c.W

    def wload(nm, tag):
        t = k.sbuf(tag, [128, 8, 1024], BF16)
        k.dma("pool", t[:], W[nm].ap()[l].rearrange("(kk p) d -> p kk d", p=128), w=[t])
        return t

    with k.scope():
        kT = k.sbuf("x_kT", [128, 8, 256], BF16)
        vtok = k.sbuf("x_v", [128, 2, 1024], BF16)
        with k.scope():
            wk = wload("ca_wk", "x_wk")
            wv = wload("ca_wv", "x_wv")
            ps = [k.psum("x_ps%d" % i, [128, 512], F32) for i in range(2)]
            n = 0
            for dcc in range(8):
                P = ps[n % 2]
                n += 1
                for kk in range(8):
                    mm(k, P[:, :256], wk[:, kk, dcc * 128:(dcc + 1) * 128], c.memT[:, kk, :], kk == 0, kk == 7, [wk, c.memT], [P])
                cp(k, "act" if dcc % 2 else "dve", kT[:, dcc, :], P[:, :256], [P], [kT])
            for mt in range(2):
                for hf in range(2):
                    P = ps[n % 2]
                    n += 1
                    for kk in range(8):
                        mm(k, P[:], c.memT[:, kk, mt * 128:(mt + 1) * 128], wv[:, kk, hf * 512:(hf + 1) * 512], kk == 0, kk == 7, [wv, c.memT], [P])
                    cp(k, "act" if hf else "dve", vtok[:, mt, hf * 512:(hf + 1) * 512], P[:], [P], [vtok])
        wq = wload("ca_wq", "x_wq")
        wo = wload("ca_wo", "x_wo")
        ln_alloc(k, c, "2", tp=not SPARSE_LN2)
        bcast_load(k, "sp", c.ln_g, W["ln2_g"].ap()[l])
        bcast_load(k, "sp", c.ln_b, W["ln2_b"].ap()[l])
        qT = [k.sbuf("x_qT%d" % i, [128, 8, 512], BF16) for i in range(2)]
        pT = [k.sbuf("x_pT%d" % i, [128, 2, 512], BF16) for i in range(2)]
        oT = [k.sbuf("x_oT%d" % i, [128, 8, 512], BF16) for i in range(2)]
        pf4 = [k.sbuf("x_pf%d" % i, [128, 256], F32) for i in range(4)]
        mx4 = [k.sbuf("x_mx4%d" % i, [128, 1], F32) for i in range(4)]
        pb = [k.sbuf("x_pb%d" % i, [128, 256], BF16) for i in range(2)]
        mx = [k.sbuf("x_mx%d" % i, [128, 1], F32) for i in range(2)]
        rs = [k.sbuf("x_rs%d" % i, [128, 1], F32) for i in range(2)]
        V = [k.sbuf("x_V%d" % i, [128, 1024], F32) for i in range(2)]
        hold = [k.sbuf("x_hold%d" % i, [128, 1024], F32) for i in range(2)]
        pq = [k.psum("x_pq%d" % i, [128, 512], F32) for i in range(2)]
        pscs = [k.psum("x_psc%d" % i, [128, 256], F32) for i in range(2)]
        ptrs = [k.psum("x_ptr%d" % i, [128, 2, 128], BF16) for i in range(2 if SPARSE_LN2 else 1)]
        pca = [k.psum("x_pca%d" % i, [128, 512], F32) for i in range(2)]
        nq = 0
        ns = 0
        for tg in range(NTG):
            ts_ = slice(tg * 512, (tg + 1) * 512)
            QT, OT = qT[tg % 2], oT[tg % 2]
            for dcc in range(8):
                P = pq[nq % 2]
                nq += 1
                for kk in range(8):
                    mm(k, P[:], wq[:, kk, dcc * 128:(dcc + 1) * 128], c.hT[:, kk, ts_], kk == 0, kk == 7, [wq] + hTr(c, tg), [P])
                cp(k, "act" if dcc % 2 else "dve", QT[:, dcc, :], P[:], [P], [QT])
            units = [(hd, t4) for hd in range(4) for t4 in range(4)]

            def st1(ui, QT=QT):
                hd, t4 = units[ui]
                g = ns0 + ui
                psc, MX, PF = pscs[g % 2], mx4[g % 4], pf4[g % 4]
                tsl = slice(t4 * 128, (t4 + 1) * 128)
                for j in range(2):
                    mm(k, psc[:], QT[:, 2 * hd + j, tsl], kT[:, 2 * hd + j, :], j == 0, j == 1, [QT, kT], [psc])
                k.op("dve", lambda e, psc=psc, MX=MX: e.tensor_reduce(out=MX[:], in_=psc[:], axis=AX.X, op=ALU.max), r=[psc], w=[MX])
                ts(k, "dve", MX[:], MX[:], -1.0 / 16.0, None, ALU.mult, None, [MX], [MX])
                act(k, PF[:], psc[:], AF.Exp, [psc, MX], [PF], bias=MX[:, 0:1], scale=1.0 / 16.0)

            def st2(ui, OT=OT):
                hd, t4 = units[ui]
                g = ns0 + ui
                PF, RS, PB, ptr = pf4[g % 4], rs[g % 2], pb[g % 2], ptrs[g % len(ptrs)]
                PT_ = pT[hd % 2]
                tsl = slice(t4 * 128, (t4 + 1) * 128)
                k.op("dve", lambda e, RS=RS, PF=PF: e.tensor_reduce(out=RS[:], in_=PF[:], axis=AX.X, op=ALU.add), r=[PF], w=[RS])
                k.op("dve", lambda e, RS=RS: e.reciprocal(out=RS[:], in_=RS[:]), r=[RS], w=[RS])
                ts(k, "dve", PB[:], PF[:], RS[:, 0:1], None, ALU.mult, None, [PF, RS], [PB])
                for mt in range(2):
                    k.op("pe", lambda e, PB=PB, mt=mt, ptr=ptr: e.transpose(out=ptr[:, mt, :], in_=PB[:, mt * 128:(mt + 1) * 128], identity=c.ident_bf[:]),
                         r=[PB, c.ident_bf], w=[ptr])
                cp(k, "act", PT_[:, :, tsl], ptr[:], [ptr], [PT_])
                if t4 == 3:
                    for j in range(2):
                        P = pq[nqc[0] % 2]
                        nqc[0] += 1
                        for mt in range(2):
                            mm(k, P[:], vtok[:, mt, (2 * hd + j) * 128:(2 * hd + j + 1) * 128], PT_[:, mt, :], mt == 0, mt == 1, [vtok, PT_], [P])
                        cp(k, "act" if j else "dve", OT[:, 2 * hd + j, :], P[:], [P], [OT])

            ns0 = ns
            nqc = [nq]
            for step in range(len(units) + 2):
                if step < len(units):
                    st1(step)
                if step >= 2:
                    st2(step - 2)
            ns += len(units)
            nq = nqc[0]
            for t4 in range(4):
                i = tg * 4 + t4
                H, VV = hold[i % 2], V[i % 2]
                k.dma("sp", H[:], c.h_tok.t.ap()[i * 128:(i + 1) * 128, :], r=[c.h_tok_b[i]], w=[H])
                for hf in range(2):
                    P = pca[hf]
                    hs = slice(hf * 512, (hf + 1) * 512)
                    for kk in range(8):
                        mm(k, P[:], OT[:, kk, t4 * 128:(t4 + 1) * 128], wo[:, kk, hs], kk == 0, kk == 7, [OT, wo], [P])
                    stt(k, VV[:, hs], H[:, hs], ALPHA, P[:], ALU.mult, ALU.add, [H, P], [VV])
                ln_tile(k, c, i, VV, write_hT=not SPARSE_LN2, hb=SPARSE_LN2)


LAYER_STAGES.append(("X", stage_X))


def stage_R(k, c, l):
    W = c.W
    with k.scope():
        rw = k.sbuf("r_rw", [128, 8, 32], F32)
        k.dma("sp", rw[:], W["router_w"].ap()[l].rearrange("(kk p) e -> p kk e", p=128), w=[rw])
        rb = k.sbuf("r_rb", [128, 32], F32)
        bcast_load(k, "sp", rb, W["router_b"].ap()[l])
        ht = [k.sbuf("r_ht%d" % i, [128, 1024], F32) for i in range(2)]
        h32 = [k.sbuf("r_h32%d" % i, [128, 8, 128], F32) for i in range(2)]
        lg = [k.sbuf("r_lg%d" % i, [128, 32], F32) for i in range(2)]
        m8 = [k.sbuf("r_m8%d" % i, [128, 8], F32) for i in range(2)]
        msk = [k.sbuf("r_msk%d" % i, [128, 32], F32) for i in range(2)]
        nm = [k.sbuf("r_nm%d" % i, [128, 1], F32) for i in range(2)]
        den = [k.sbuf("r_den%d" % i, [128, 1], F32) for i in range(2)]
        p32 = [k.psum("r_p32%d" % i, [128, 4, 128], F32) for i in range(4)]
        plog = [k.psum("r_plog%d" % i, [128, 32], F32) for i in range(2)]
        for i in range(NT):
            b = i % 2
            HT, H32, LG, M8, MSK, NM, DEN = ht[b], h32[b], lg[b], m8[b], msk[b], nm[b], den[b]
            k.dma("sp", HT[:], c.h_tok.t.ap()[i * 128:(i + 1) * 128, :], r=[c.h_tok_b[i]], w=[HT])
            for j in range(8):
                PP = p32[b * 2 + j // 4]
                k.op("pe", lambda e, j=j, PP=PP, HT=HT: e.transpose(out=PP[:, j % 4, :], in_=HT[:, j * 128:(j + 1) * 128], identity=c.ident_f[:]),
                     r=[HT, c.ident_f], w=[PP])
            cp(k, "dve", H32[:, 0:4, :], p32[b * 2][:], [p32[b * 2]], [H32])
            cp(k, "act", H32[:, 4:8, :], p32[b * 2 + 1][:], [p32[b * 2 + 1]], [H32])
            PL = plog[b]
            for kk in range(8):
                mm(k, PL[:], H32[:, kk, :], rw[:, kk, :], kk == 0, kk == 7, [H32, rw], [PL])
            tt(k, "dve", LG[:], PL[:], rb[:], ALU.add, [PL, rb], [LG])
            k.op("dve", lambda e, M8=M8, LG=LG: e.max(out=M8[:], in_=LG[:]), r=[LG], w=[M8])
            ts(k, "dve", MSK[:], LG[:], M8[:, 3:4], None, ALU.is_ge, None, [LG, M8], [MSK])
            ts(k, "dve", NM[:], M8[:, 0:1], -1.0, None, ALU.mult, None, [M8], [NM])
            act(k, LG[:], LG[:], AF.Exp, [LG, NM], [LG], bias=NM[:, 0:1], scale=1.0)
            tt(k, "dve", LG[:], LG[:], MSK[:], ALU.mult, [LG, MSK], [LG])
            k.op("dve", lambda e, DEN=DEN, LG=LG: e.tensor_reduce(out=DEN[:], in_=LG[:], axis=AX.X, op=ALU.add), r=[LG], w=[DEN])
            k.op("dve", lambda e, DEN=DEN: e.reciprocal(out=DEN[:], in_=DEN[:]), r=[DEN], w=[DEN])
            ts(k, "dve", c.G[:, i, :], LG[:], DEN[:, 0:1], None, ALU.mult, None, [LG, DEN], [c.G])


LAYER_STAGES.append(("R", stage_R))


def stage_E(k, c, l):
    W = c.W
    NW = 5
    with k.scope():
        b1T = k.sbuf("e_b1T", [128, 16, 32], F32)
        with k.scope():
            t_ = load_cols(k, c, "e_b1", [(W["moe_b1"].ap()[l], 32)], width=2048)
            cp(k, "dve", b1T[:], t_[:], [t_], [b1T])
        b2 = k.sbuf("e_b2", [32, 1024], F32)
        k.dma("sp", b2[:], W["moe_b2"].ap()[l], w=[b2])
        ln_alloc(k, c, "3")
        bcast_load(k, "sp", c.ln_g, W["ln3_g"].ap()[l])
        bcast_load(k, "sp", c.ln_b, W["ln3_b"].ap()[l])
        ring = [k.sbuf("e_w%d" % i, [128, 8, 512], BF16) for i in range(NW)]
        AT = [k.sbuf("e_at%d" % i, [128, 8, 512], BF16) for i in range(2)]
        ffacc = k.sbuf("e_ff", [128, 4, 1024], F32)
        gc = [k.sbuf("e_gc%d" % i, [128, 512], F32) for i in range(2)]
        sgm = [k.sbuf("e_sg%d" % i, [128, 512], F32) for i in range(2)]
        lc = [k.sbuf("e_lc%d" % i, [128, 512], F32) for i in range(2)]
        GT = [k.sbuf("e_gt%d" % i, [32, 128], F32) for i in range(2)]
        V = [k.sbuf("e_V%d" % i, [128, 1024], F32) for i in range(2)]
        hold = [k.sbuf("e_hold%d" % i, [128, 1024], F32) for i in range(2)]
        pg = [k.psum("e_pg%d" % i, [128, 512], F32) for i in range(2)]
        pl = [k.psum("e_pl%d" % i, [128, 512], F32) for i in range(2)]
        py = [k.psum("e_py%d" % i, [128, 512], F32) for i in range(2)]
        w1 = W["moe_w1"].ap()[l]
        w2 = W["moe_w2"].ap()[l]
        nw = 0
        n = 0
        ny = 0
        na = 0
        for tg in range(NTG):
            ts_ = slice(tg * 512, (tg + 1) * 512)
            for e in range(NE):
                A = AT[na % 2]
                na += 1
                w1e = w1[e].rearrange("(kk p) f -> p kk f", p=128)
                w2e = w2[e].rearrange("(kk p) d -> p kk d", p=128)
                for q in range(4):
                    WQ = ring[nw % NW]
                    nw += 1
                    k.dma("pool", WQ[:, :, 0:256], w1e[:, :, q * 256:(q + 1) * 256], w=[WQ])
                    k.dma("pool", WQ[:, :, 256:512], w1e[:, :, 1024 + q * 256:1024 + (q + 1) * 256], w=[WQ])
                    for ci in range(2):
                        fc = 2 * q + ci
                        b = n % 2
                        n += 1
                        PG, PL, GC, SG, LC = pg[b], pl[b], gc[b], sgm[b], lc[b]
                        for kk in range(8):
                            mm(k, PG[:], WQ[:, kk, ci * 128:(ci + 1) * 128], c.hT[:, kk, ts_], kk == 0, kk == 7, [WQ] + hTr(c, tg), [PG])
                        for kk in range(8):
                            mm(k, PL[:], WQ[:, kk, 256 + ci * 128:256 + (ci + 1) * 128], c.hT[:, kk, ts_], kk == 0, kk == 7, [WQ] + hTr(c, tg), [PL])
                        ts(k, "dve", GC[:], PG[:], b1T[:, fc, e:e + 1], 7.0, ALU.add, ALU.min, [PG, b1T], [GC])
                        act(k, SG[:], GC[:], AF.Sigmoid, [GC], [SG], scale=1.702)
                        ts(k, "dve", LC[:], PL[:], b1T[:, 8 + fc, e:e + 1], 7.0, ALU.add, ALU.min, [PL, b1T], [LC])
                        ts(k, "dve", LC[:], LC[:], -7.0, 1.0, ALU.max, ALU.add, [LC], [LC])
                        tt(k, "dve", GC[:], GC[:], SG[:], ALU.mult, [GC, SG], [GC])
                        tt(k, "dve", A[:, fc, :], GC[:], LC[:], ALU.mult, [GC, LC], [A])
                for hf in range(2):
                    W2 = ring[nw % NW]
                    nw += 1
                    hs = slice(hf * 512, (hf + 1) * 512)
                    k.dma("pool", W2[:], w2e[:, :, hs], w=[W2])
                    for t4 in range(4):
                        PY = py[ny % 2]
                        ny += 1
                        for kk in range(8):
                            mm(k, PY[:], A[:, kk, t4 * 128:(t4 + 1) * 128], W2[:, kk, :], kk == 0, kk == 7, [A, W2], [PY])
                        gcol = c.G[:, tg * 4 + t4, e:e + 1]
                        if e == 0:
                            ts(k, "dve", ffacc[:, t4, hs], PY[:], gcol, None, ALU.mult, None, [PY, c.G], [ffacc])
                        else:
                            stt(k, ffacc[:, t4, hs], PY[:], gcol, ffacc[:, t4, hs], ALU.mult, ALU.add, [PY, c.G, ffacc], [ffacc])
            for t4 in range(4):
                i = tg * 4 + t4
                H, VV, GTt = hold[i % 2], V[i % 2], GT[i % 2]
                k.dma("sp", H[:], c.h_tok.t.ap()[i * 128:(i + 1) * 128, :], r=[c.h_tok_b[i]], w=[H])
                PY = py[ny % 2]
                ny += 1
                k.op("pe", lambda e_, i=i, PY=PY: e_.transpose(out=PY[:32, :128], in_=c.G[:, i, :], identity=c.ident_f[:]), r=[c.G, c.ident_f], w=[PY])
                cp(k, "act", GTt[:], PY[:32, :128], [PY], [GTt])
                for hf in range(2):
                    hs = slice(hf * 512, (hf + 1) * 512)
                    PY = py[ny % 2]
                    ny += 1
                    mm(k, PY[:], GTt[:], b2[:, hs], True, True, [GTt, b2], [PY])
                    stt(k, VV[:, hs], H[:, hs], ALPHA, ffacc[:, t4, hs], ALU.mult, ALU.add, [H, ffacc], [VV])
                    tt(k, "dve", VV[:, hs], VV[:, hs], PY[:], ALU.add, [VV, PY], [VV])
                ln_tile(k, c, i, VV)


LAYER_STAGES.append(("E", stage_E))


BS = 256
NBLK = 96
NSLOT = NBLK * BS


def stage_R2(k, c, l):
    with k.scope():
        def cload(name, shape):
            t = k.sbuf("r2_" + name, shape, F32)
            k.dma("sp", t[:], c.consts[name].ap(), w=[t])
            return t
        Lt = cload("Lt", [128, 128])
        thr = cload("thr", [128, 16])
        jrow = cload("jrow", [128, NBLK])
        kp = cload("kp", [128, 8])
        tokf = cload("tokid", [128, 32])
        toki = k.sbuf("r2_toki", [128, 32], I32)
        cp(k, "dve", toki[:], tokf[:], [tokf], [toki])
        onesf = k.sbuf("r2_ones", [128, 128], F32)
        k.op("pool", lambda e: e.memset(onesf[:], 1.0), w=[onesf])
        initf = k.sbuf("r2_initf", [128, NSLOT // 128], F32)
        initi = k.sbuf("r2_initi", [128, NSLOT // 128], I32)
        k.op("pool", lambda e: e.memset(initf[:], float(S)), w=[initf])
        cp(k, "dve", initi[:], initf[:], [initf], [initi])
        k.dma("sp", c.slot_tok.t.ap().rearrange("(p n) o -> p (n o)", p=128), initi[:], r=[initi], w=[c.slot_tok])
        mask = k.sbuf("r2_mask", [128, 32, 32], F32)
        ts(k, "dve", mask[:], c.G[:], 0.0, None, ALU.is_gt, None, [c.G], [mask])
        pos = k.sbuf("r2_pos", [128, 32, 32], F32)
        cm = [k.sbuf("r2_cm%d" % i, [128, 32], F32) for i in range(2)]
        k.op("pool", lambda e: e.memset(cm[0][:], 0.0), w=[cm[0]])
        pp = [k.psum("r2_pp%d" % i, [128, 32], F32) for i in range(2)]
        for i in range(NT):
            P = pp[i % 2]
            mm(k, P[:], Lt[:], mask[:, i, :], True, False, [Lt, mask], [P])
            mm(k, P[:], onesf[:], cm[i % 2][:], False, True, [onesf, cm[i % 2]], [P])
            cp(k, "act", pos[:, i, :], P[:], [P], [pos])
            tt(k, "dve", cm[(i + 1) % 2][:], cm[i % 2][:], mask[:, i, :], ALU.add, [cm[i % 2], mask], [cm[(i + 1) % 2]])
        P = pp[0]
        mm(k, P[:], onesf[:], cm[NT % 2][:], True, True, [onesf, cm[NT % 2]], [P])
        ncnt = k.sbuf("r2_n", [128, 32], F32)
        cp(k, "dve", ncnt[:], P[:], [P], [ncnt])
        cmp1 = k.sbuf("r2_cmp1", [128, 32, 16], F32)
        tt(k, "dve", cmp1[:], ncnt[:].rearrange("p (e o) -> p e o", o=1).to_broadcast([128, 32, 16]),
           thr[:].rearrange("p (o j) -> p o j", o=1).to_broadcast([128, 32, 16]), ALU.is_gt, [ncnt, thr], [cmp1])
        nblk = k.sbuf("r2_nblk", [128, 32], F32)
        k.op("dve", lambda e: e.tensor_reduce(out=nblk[:], in_=cmp1[:], axis=AX.X, op=ALU.add), r=[cmp1], w=[nblk])
        pend = k.sbuf("r2_pend", [128, 32], F32)
        k.op("dve", lambda e: e.tensor_tensor_scan(out=pend[:], data0=onesf[:, 0:32], data1=nblk[:], initial=0.0, op0=ALU.mult, op1=ALU.add),
             r=[onesf, nblk], w=[pend])
        pst = k.sbuf("r2_pst", [128, 32], F32)
        tt(k, "dve", pst[:], pend[:], nblk[:], ALU.subtract, [pend, nblk], [pst])
        ts(k, "dve", pst[:], pst[:], float(BS), 1.0, ALU.mult, ALU.add, [pst], [pst])
        cmp2 = k.sbuf("r2_cmp2", [128, NBLK, 32], F32)
        tt(k, "dve", cmp2[:], pend[:].rearrange("p (o e) -> p o e", o=1).to_broadcast([128, NBLK, 32]),
           jrow[:].rearrange("p (j o) -> p j o", o=1).to_broadcast([128, NBLK, 32]), ALU.is_le, [pend, jrow], [cmp2])
        blke = k.sbuf("r2_blke", [128, NBLK], F32)
        k.op("dve", lambda e: e.tensor_reduce(out=blke[:], in_=cmp2[:], axis=AX.X, op=ALU.add), r=[cmp2], w=[blke])
        ts(k, "dve", blke[:], blke[:], 31.0, None, ALU.min, None, [blke], [blke])
        blkg = k.sbuf("r2_blkg", [128, NBLK], F32)
        ts(k, "dve", blkg[:], blke[:], float(l * 32), None, ALU.add, None, [blke], [blkg])
        cp(k, "dve", c.eidx[:], blkg[:], [blkg], [c.eidx])
        widf = k.sbuf("r2_widf", [128, NBLK, 8], F32)
        stt(k, widf[:], blke[:].rearrange("p (j o) -> p j o", o=1).to_broadcast([128, NBLK, 8]), 1024.0,
            kp[:].rearrange("p (o q) -> p o q", o=1).to_broadcast([128, NBLK, 8]), ALU.mult, ALU.add, [blke, kp], [widf])
        cp(k, "dve", c.widx[:], widf[:], [widf], [c.widx])
        ss = [k.sbuf("r2_ss%d" % i, [128, 32], F32) for i in range(2)]
        m8 = [k.sbuf("r2_m8%d" % i, [128, 8], F32) for i in range(2)]
        oh = [k.sbuf("r2_oh%d" % i, [128, 32], F32) for i in range(2)]
        for i in range(NT):
            SS, M8 = ss[i % 2], m8[i % 2]
            tt(k, "dve", SS[:], pos[:, i, :], pst[:], ALU.add, [pos, pst], [SS])
            tt(k, "dve", SS[:], SS[:], mask[:, i, :], ALU.mult, [SS, mask], [SS])
            ts(k, "dve", SS[:], SS[:], -1.0, None, ALU.add, None, [SS], [SS])
            k.op("dve", lambda e, M8=M8, SS=SS: e.max(out=M8[:], in_=SS[:]), r=[SS], w=[M8])
            cp(k, "dve", c.dest[:, i, :], M8[:, 0:4], [M8], [c.dest])
            for q in range(4):
                OH = oh[q % 2]
                ts(k, "dve", OH[:], SS[:], M8[:, q:q + 1], None, ALU.is_equal, None, [SS, M8], [OH])
                tt(k, "dve", OH[:], OH[:], c.G[:, i, :], ALU.mult, [OH, c.G], [OH])
                k.op("dve", lambda e, OH=OH, i=i, q=q: e.tensor_reduce(out=c.gk[:, i, q:q + 1], in_=OH[:], axis=AX.X, op=ALU.add), r=[OH], w=[c.gk])
            for q in range(4):
                def scat(e, i=i, q=q):
                    return e.indirect_dma_start(out=c.slot_tok.t.ap(), out_offset=bass.IndirectOffsetOnAxis(ap=c.dest[:, i, q:q + 1], axis=0),
                                                in_=toki[:, i:i + 1], in_offset=None)
                k.dma_custom("pool", scat, r=[c.dest, toki, c.slot_tok], w=[])


def stage_CV(k, c, l):
    W = c.W
    for g in range(8):
        k.dma_bg("pool", c.w1b.t.ap()[g * 4096:(g + 1) * 4096, :], W["moe_w1"].ap()[l, 4 * g:4 * g + 4].rearrange("e r f -> (e r) f"),
                 slot=g, w=[c.cvb[g]])
        k.dma_bg("pool", c.w2b.t.ap()[g * 4096:(g + 1) * 4096, :], W["moe_w2"].ap()[l, 4 * g:4 * g + 4].rearrange("e r f -> (e r) f"),
                 slot=8 + g, w=[c.cvb[8 + g]])


def stage_E2(k, c, l):
    W = c.W
    w1rows = c.w1b.t.ap()
    w2rows = c.w2b.t.ap()
    b1rows = W["moe_b1"].ap().rearrange("l e f -> (l e) f")
    b2rows = W["moe_b2"].ap().rearrange("l e f -> (l e) f")
    with k.scope():
        w1b = [[Buf("w1t%d_%d" % (b_, q_)) for q_ in range(8)] for b_ in range(2)]
        w2b = [[Buf("w2t%d_%d" % (b_, q_)) for q_ in range(8)] for b_ in range(2)]
        w2t = [k.sbuf("e2_w2t%d" % i, [128, 8, 1024], BF16) for i in range(2)]
        b1t = [k.sbuf("e2_b1t%d" % i, [128, 2048], BF16) for i in range(2)]
        b2t = [k.sbuf("e2_b2t%d" % i, [128, 1024], F32) for i in range(2)]
        idxt = [k.sbuf("e2_idx%d" % i, [128, 2], I32) for i in range(3)]
        xg = [k.sbuf("e2_xg%d" % i, [128, 2, 1024], BF16) for i in range(2)]
        xgT = [k.sbuf("e2_xgT%d" % i, [128, 8, BS], BF16) for i in range(2)]
        AT = [k.sbuf("e2_at%d" % i, [128, 8, BS], BF16) for i in range(2)]
        gc = [k.sbuf("e2_gc%d" % i, [128, 512], F32) for i in range(2)]
        sgm = [k.sbuf("e2_sg%d" % i, [128, 512], F32) for i in range(2)]
        lc = [k.sbuf("e2_lc%d" % i, [128, 512], F32) for i in range(2)]
        atok = [k.sbuf("e2_atok%d" % i, [128, 1024], BF16) for i in range(2)]
        ysb = [k.sbuf("e2_ys%d" % i, [128, 1024], F32) for i in range(2)]
        onesr = k.sbuf("e2_onesr", [1, 128], BF16)
        k.op("pool", lambda e: e.memset(onesr[:], 1.0), w=[onesr])
        ptr = [k.psum("e2_ptr%d" % i, [128, 1024], BF16) for i in range(2)]
        pta = ptr
        pg = [k.psum("e2_pg%d" % i, [128, 512], F32) for i in range(2)]
        pl = [k.psum("e2_pl%d" % i, [128, 512], F32) for i in range(2)]
        py = [k.psum("e2_py%d" % i, [128, 512], F32) for i in range(2)]
        st2 = c.slot_tok.t.ap()

        def loads(j):
            b = j % 2
            IDX = idxt[j % 3]
            k.dma("sp", IDX[:], st2[j * BS:(j + 1) * BS, :].rearrange("(s p) o -> p (s o)", p=128), r=[c.slot_tok], w=[IDX],
                  allow_slow_non_contiguous=True)
            for s_ in range(2):
                def g(e, s_=s_, b=b, IDX=IDX):
                    return e.indirect_dma_start(out=xg[b][:, s_, :], out_offset=None, in_=c.hb.t.ap(),
                                                in_offset=bass.IndirectOffsetOnAxis(ap=IDX[:, s_:s_ + 1], axis=0))
                k.dma_custom("pool", g, r=[IDX, c.hb], w=[xg[b]])
            def gb1(e, b=b, j=j):
                return e.indirect_dma_start(out=b1t[b][:], out_offset=None, in_=b1rows,
                                            in_offset=bass.IndirectOffsetOnAxis(ap=c.eidx[:, j:j + 1], axis=0))
            k.dma_custom("pool", gb1, r=[c.eidx], w=[b1t[b]])
            for kk in range(8):
                def g1(e, kk=kk, b=b, j=j):
                    return e.indirect_dma_start(out=c.hT[:, kk, b * 2048:(b + 1) * 2048], out_offset=None, in_=w1rows,
                                                in_offset=bass.IndirectOffsetOnAxis(ap=c.widx[:, j, kk:kk + 1], axis=0))
                k.dma_custom("pool", g1, r=[c.widx] + c.cvb, w=[w1b[b][kk]])

        def loadsB(j):
            b = j % 2
            for kk in range(8):
                def g2(e, kk=kk, b=b, j=j):
                    return e.indirect_dma_start(out=w2t[b][:, kk, :], out_offset=None, in_=w2rows,
                                                in_offset=bass.IndirectOffsetOnAxis(ap=c.widx[:, j, kk:kk + 1], axis=0))
                k.dma_custom("pool", g2, r=[c.widx] + c.cvb, w=[w2b[b][kk]])

            def gb2(e, b=b, j=j):
                return e.indirect_dma_start(out=b2t[b][:], out_offset=None, in_=b2rows,
                                            in_offset=bass.IndirectOffsetOnAxis(ap=c.eidx[:, j:j + 1], axis=0))
            k.dma_custom("pool", gb2, r=[c.eidx], w=[b2t[b]])

        cnt = {"n": 0, "ny": 0}

        def xtrans(j):
            b = j % 2
            XT = xgT[b]
            for s_ in range(2):
                PT_ = ptr[s_]
                for kk in range(8):
                    k.op("pe", lambda e, kk=kk, s_=s_, PT_=PT_, b=b: e.transpose(out=PT_[:, kk * 128:(kk + 1) * 128], in_=xg[b][:, s_, kk * 128:(kk + 1) * 128],
                                                                                 identity=c.ident_bf[:]), r=[xg[b], c.ident_bf], w=[PT_])
                cp(k, "act" if s_ else "dve", XT[:, :, s_ * 128:(s_ + 1) * 128], PT_[:].rearrange("p (a q) -> p a q", a=8), [PT_], [XT])

        def phase1(j, s_):
            b = j % 2
            XT = xgT[b]
            ssl = slice(s_ * 128, (s_ + 1) * 128)
            for pr in range(2):
                bb = cnt["n"] % 2
                cnt["n"] += 1
                PG, PL, GC, SG, LC = pg[bb], pl[bb], gc[bb], sgm[bb], lc[bb]
                for (P, c0) in ((PG, pr * 512), (PL, 1024 + pr * 512)):
                    for kk in range(8):
                        mm(k, P[:], XT[:, kk, ssl], c.hT[:, kk, b * 2048 + c0:b * 2048 + c0 + 512], kk == 0, False, [XT, w1b[b][kk]], [P])
                    mm(k, P[:], onesr[0:1, :], b1t[b][0:1, c0:c0 + 512], False, True, [onesr, b1t[b]], [P])
                ts(k, "dve", GC[:], PG[:], 7.0, None, ALU.min, None, [PG], [GC])
                act(k, SG[:], GC[:], AF.Sigmoid, [GC], [SG], scale=1.702)
                ts(k, "dve", LC[:], PL[:], 7.0, -7.0, ALU.min, ALU.max, [PL], [LC])
                tt(k, "dve", GC[:], GC[:], SG[:], ALU.mult, [GC, SG], [GC])
                stt(k, atok[s_][:, pr * 512:(pr + 1) * 512], LC[:], 1.0, GC[:], ALU.add, ALU.mult, [LC, GC], [atok[s_]])

        def phase2a(j, s_):
            b = j % 2
            A = AT[b]
            ssl = slice(s_ * 128, (s_ + 1) * 128)
            PT_ = pta[s_]
            for kk in range(8):
                k.op("pe", lambda e, kk=kk, s_=s_, PT_=PT_: e.transpose(out=PT_[:, kk * 128:(kk + 1) * 128], in_=atok[s_][:, kk * 128:(kk + 1) * 128],
                                                                   identity=c.ident_bf[:]), r=[atok[s_], c.ident_bf], w=[PT_])
            cp(k, "act" if s_ else "dve", A[:, :, ssl], PT_[:].rearrange("p (a q) -> p a q", a=8), [PT_], [A])

        def phase2b(j, s_):
            b = j % 2
            A = AT[b]
            ssl = slice(s_ * 128, (s_ + 1) * 128)
            Y = ysb[s_]
            for hf in range(2):
                PY = py[cnt["ny"] % 2]
                cnt["ny"] += 1
                hs = slice(hf * 512, (hf + 1) * 512)
                for kk in range(8):
                    mm(k, PY[:], A[:, kk, ssl], w2t[b][:, kk, hs], kk == 0, kk == 7, [A, w2b[b][kk]], [PY])
                tt(k, "dve", Y[:, hs], PY[:], b2t[b][:, hs], ALU.add, [PY, b2t[b]], [Y])
            r0 = j * BS + s_ * 128
            k.dma("sp", c.ys.t.ap()[r0:r0 + 128, :], Y[:], r=[Y], w=[c.ys])

        loads(0)
        loadsB(0)
        loads(1)
        loadsB(1)
        NU = 2 * NBLK
        xtrans(0)
        for u in range(NU + 1):
            if u >= 1:
                phase2a(*divmod(u - 1, 2))
            if u < NU:
                j, s_ = divmod(u, 2)
                if s_ == 1 and j + 1 < NBLK:
                    xtrans(j + 1)
                phase1(j, s_)
                if s_ == 1 and j + 2 < NBLK:
                    loads(j + 2)
            if u >= 1:
                j2, s2 = divmod(u - 1, 2)
                phase2b(j2, s2)
                if s2 == 1 and j2 + 2 < NBLK:
                    loadsB(j2 + 2)


def stage_F(k, c, l):
    W = c.W
    with k.scope():
        ln_alloc(k, c, "3")
        bcast_load(k, "sp", c.ln_g, W["ln3_g"].ap()[l])
        bcast_load(k, "sp", c.ln_b, W["ln3_b"].ap()[l])
        rows = [k.sbuf("f_rows%d" % i, [128, 1024], F32) for i in range(8)]
        acc = [k.sbuf("f_acc%d" % i, [128, 1024], F32) for i in range(2)]
        hold = [k.sbuf("f_hold%d" % i, [128, 1024], F32) for i in range(2)]
        def fetch(i):
            H = hold[i % 2]
            k.dma("sp", H[:], c.h_tok.t.ap()[i * 128:(i + 1) * 128, :], r=[c.h_tok_b[i]], w=[H])
            for q in range(4):
                Rw = rows[(i % 2) * 4 + q]

                def g(e, Rw=Rw, i=i, q=q):
                    return e.indirect_dma_start(out=Rw[:], out_offset=None, in_=c.ys.t.ap(),
                                                in_offset=bass.IndirectOffsetOnAxis(ap=c.dest[:, i, q:q + 1], axis=0))
                k.dma_custom("pool", g, r=[c.dest, c.ys], w=[Rw])

        pend = None
        fetch(0)
        for i in range(NT):
            H, A = hold[i % 2], acc[i % 2]
            if i + 1 < NT:
                fetch(i + 1)
            for q in range(4):
                Rw = rows[(i % 2) * 4 + q]
                if q == 0:
                    ts(k, "dve", A[:], Rw[:], c.gk[:, i, 0:1], None, ALU.mult, None, [Rw, c.gk], [A])
                else:
                    stt(k, A[:], Rw[:], c.gk[:, i, q:q + 1], A[:], ALU.mult, ALU.add, [Rw, c.gk, A], [A])
            stt(k, A[:], H[:], ALPHA, A[:], ALU.mult, ALU.add, [H, A], [A])
            nb_ = ln_tile(k, c, i, A, defer=True)
            if pend is not None:
                pend()
            pend = nb_
        pend()


def make_consts2():
    c = {}
    j = np.arange(128)
    c["Lt"] = (j[:, None] < j[None, :]).astype(np.float32)
    c["thr"] = np.broadcast_to((np.arange(16) * BS).astype(np.float32)[None, :], (128, 16)).copy()
    c["jrow"] = np.broadcast_to(np.arange(NBLK).astype(np.float32)[None, :], (128, NBLK)).copy()
    c["kp"] = (np.arange(8)[None, :] * 128 + j[:, None]).astype(np.float32)
    c["tokid"] = (np.arange(32)[None, :] * 128 + j[:, None]).astype(np.float32)
    return c


CONST_SHAPES.update(dict(Lt=[128, 128], thr=[128, 16], jrow=[128, NBLK], kp=[128, 8], tokid=[128, 32]))
_mc1 = make_consts


def make_consts():
    c = _mc1()
    c.update(make_consts2())
    return c


SPARSE = True
if SPARSE:
    LAYER_STAGES[:] = [s for s in LAYER_STAGES if s[0] != "E"]
    LAYER_STAGES.insert(0, ("CV", stage_CV))
    LAYER_STAGES.extend([("R2", stage_R2), ("E2", stage_E2), ("F", stage_F)])


def build(nlayers=NL, upto=None, dbg=()):
    nc = bass.Bass("TRN2", target_bir_lowering=False)
    c = Ctx()
    c.nc = nc
    c.x = nc.dram_tensor("x", [S, D], F32, kind="ExternalInput")
    c.mem = nc.dram_tensor("mem", [256, D], F32, kind="ExternalInput")
    c.W = {n: nc.dram_tensor(n, (shp if n.startswith("ln0") else [nlayers] + shp[1:]), F32, kind="ExternalInput") for n, shp in WSPEC}
    c.consts = {n: nc.dram_tensor("c_" + n, shp, F32, kind="ExternalInput") for n, shp in CONST_SHAPES.items()}

    def scratch(name, shape, dt):
        kind = "ExternalOutput" if name in dbg else "Internal"
        t = T(nc.dram_tensor(name, list(shape), dt, kind=kind), name)
        return t

    c.h_tok = T(nc.dram_tensor("out", [S, D], F32, kind="ExternalOutput"), "out")
    c.h_tok_b = [Buf("htok%d" % i) for i in range(NT)]
    c.zT = scratch("zT", [INCOLS, S], BF16)
    c.zT_b = c.zT.b
    c.ztok = {n: scratch("ztok_" + n, [S, 512], BF16) for n in TOKMAJ}
    c.yT = scratch("yT", [2048, S], BF16)
    c.hb = scratch("hb", [S + 1, D], BF16)
    c.slot_tok = scratch("slot_tok", [NSLOT, 1], I32)
    c.ys = scratch("ys", [NSLOT, D], F32)
    c.w1b = scratch("w1b", [NE * 1024, 2048], BF16)
    c.w2b = scratch("w2b", [NE * 1024, 1024], BF16)
    c.cvb = [Buf("cv%d" % i) for i in range(16)]

    with ExitStack() as st:
        k = KB(nc, st)
        c.k = k
        c.hT = k.sbuf("hT", [128, 8, S], BF16)
        c.hT_b = [Buf("hT%d" % i) for i in range(NT)]
        c.ident_bf = k.sbuf("ident_bf", [128, 128], BF16)
        c.ident_f = k.sbuf("ident_f", [128, 128], F32)
        c.memT = k.sbuf("memT", [128, 8, 256], BF16)
        c.G = k.sbuf("G", [128, 32, 32], F32)
        c.gk = k.sbuf("gk", [128, 32, 4], F32)
        c.dest = k.sbuf("dest", [128, 32, 4], I32)
        c.widx = k.sbuf("widx", [128, NBLK, 8], I32)
        c.eidx = k.sbuf("eidx", [128, NBLK], I32)
        stages = []
        stages.append(("prologue", lambda: stage_prologue(k, c)))
        for l in range(nlayers):
            stages.append(("inproj%d" % l, lambda l=l: stage_inproj(k, c, l)))
            for nm, fn in LAYER_STAGES:
                stages.append((nm + str(l), lambda l=l, fn=fn: fn(k, c, l)))
        for nm, fn in stages:
            fn()
            if upto is not None and nm == upto:
                break
        k.barrier()
        k.finish()
    c.ninstr = k.nins
    return nc, c


_CACHE = {}


def _in_maps(inputs, ncores=4):
    consts = make_consts()
    maps = []
    for b in range(ncores):
        m = {"x": np.ascontiguousarray(inputs["x"][b]), "mem": np.ascontiguousarray(inputs["mem"][b])}
        for n, _ in WSPEC:
            m[n] = np.ascontiguousarray(inputs[n], dtype=np.float32)
        for n, v in consts.items():
            m["c_" + n] = v
        maps.append(m)
    return maps


def kernel(**inputs):
    if "nc" not in _CACHE:
        _CACHE["nc"] = build()[0]
    nc = _CACHE["nc"]
    maps = _in_maps(inputs, 4)
    res = run_bass_kernel_spmd(nc, maps, core_ids=[0, 1, 2, 3])
    return np.stack([np.asarray(r["out"], dtype=np.float32) for r in res.results], axis=0)
```

```python
from contextlib import ExitStack, contextmanager
import numpy as np
import concourse.bass as bass
import concourse.mybir as mybir
from concourse.bass_utils import run_bass_kernel_spmd

F32 = mybir.dt.float32
BF16 = mybir.dt.bfloat16
I32 = mybir.dt.int32
U32 = mybir.dt.uint32
AF = mybir.ActivationFunctionType
ALU = mybir.AluOpType
AX = mybir.AxisListType

ENGS = ("pe", "dve", "act", "pool", "sp")
NDMASEM = 24


class Buf:
    __slots__ = ("name", "w", "r")

    def __init__(self, name=""):
        self.name = name
        self.w = None
        self.r = []


class T:
    def __init__(self, t, name):
        self.t = t
        self.b = Buf(name)

    def __getitem__(self, idx):
        return self.t[idx]


def _bufs(xs):
    out = []
    for x in xs:
        if x is None:
            continue
        out.append(x.b if isinstance(x, T) else x)
    return out


class KB:
    def __init__(self, nc, stack):
        self.nc = nc
        self.stack = stack
        self.prog = {e: [] for e in ENGS}
        self.cnt = {e: 0 for e in ENGS}
        self.known = {e: {} for e in ENGS}
        self.csem = {e: stack.enter_context(nc.semaphore("c_" + e)) for e in ENGS}
        self.ndma = {"sp": NDMASEM, "act": 2, "pool": 2 * NDMASEM - 2}
        self.dsem = {e: [stack.enter_context(nc.semaphore("d_%s%d" % (e, i))) for i in range(self.ndma[e])]
                     for e in ("sp", "act", "pool")}
        self.dcnt = {e: 0 for e in ("sp", "act", "pool")}
        self.dtarget = {}
        self.nins = 0
        self.stack0 = stack
        self.bsem = None
        self.bcnt = 0
        self.bgsems = []
        self.bgtgt = []

    def _nm(self, name):
        self.nalloc = getattr(self, "nalloc", 0) + 1
        return "%s_%d" % (name, self.nalloc)

    def sbuf(self, name, shape, dtype):
        return T(self.stack.enter_context(self.nc.sbuf_tensor(self._nm(name), list(shape), dtype)), name)

    def psum(self, name, shape, dtype=F32):
        return T(self.stack.enter_context(self.nc.psum_tensor(self._nm(name), list(shape), dtype)), name)

    def dram(self, name, shape, dtype, kind="Internal"):
        return T(self.nc.dram_tensor(name, list(shape), dtype, kind=kind), name)

    def _need(self, eng, dep):
        kind = dep[0]
        if kind == "b":
            _, idx, tgt = dep
            key = ("b", idx)
            if self.known[eng].get(key, 0) >= tgt:
                return
            self.known[eng][key] = tgt
            sem = self.bgsems[idx]
            self.prog[eng].append(lambda e, sem=sem, tgt=tgt: e.wait_ge(sem, tgt))
            return
        if kind == "c":
            _, e2, n = dep
            if e2 == eng:
                if eng == "pe":
                    return
                if n < self.cnt[eng] - 1:
                    return
            key = ("c", e2)
            if self.known[eng].get(key, 0) >= n:
                return
            self.known[eng][key] = n
            sem = self.csem[e2]
            self.prog[eng].append(lambda e, sem=sem, n=n: e.wait_ge(sem, n))
        else:
            _, q, i, tgt = dep
            key = ("d", q, i)
            if self.known[eng].get(key, 0) >= tgt:
                return
            self.known[eng][key] = tgt
            sem = self.dsem[q][i]
            self.prog[eng].append(lambda e, sem=sem, tgt=tgt: e.wait_ge(sem, tgt))

    def _deps(self, eng, r, w):
        rb, wb = _bufs(r), _bufs(w)
        for b in rb:
            if b.w is not None:
                self._need(eng, b.w)
        for b in wb:
            if b.w is not None:
                self._need(eng, b.w)
            for d in b.r:
                self._need(eng, d)
        return rb, wb

    def op(self, eng, fn, r=(), w=()):
        rb, wb = self._deps(eng, r, w)
        self.cnt[eng] += 1
        n = self.cnt[eng]
        sem = self.csem[eng]
        self.prog[eng].append(lambda e, fn=fn, sem=sem: fn(e).then_inc(sem, 1))
        dep = ("c", eng, n)
        for b in wb:
            b.w = dep
            b.r = []
        for b in rb:
            if b not in wb:
                b.r.append(dep)
        self.nins += 1

    def dma(self, q, out, in_, r=(), w=(), **kw):
        rb, wb = self._deps(q, r, w)
        i = self.dcnt[q] % self.ndma[q]
        self.dcnt[q] += 1
        key = (q, i)
        prev = self.dtarget.get(key, 0)
        if prev:
            self._need(q, ("d", q, i, prev))
        tgt = prev + 16
        self.dtarget[key] = tgt
        sem = self.dsem[q][i]
        self.prog[q].append(lambda e, out=out, in_=in_, sem=sem, kw=kw: e.dma_start(out=out, in_=in_, **kw).then_inc(sem, 16))
        dep = ("d", q, i, tgt)
        for b in wb:
            b.w = dep
            b.r = []
        for b in rb:
            if b not in wb:
                b.r.append(dep)
        self.nins += 1
        return dep

    def dma_custom(self, q, fn, r=(), w=()):
        rb, wb = self._deps(q, r, w)
        i = self.dcnt[q] % self.ndma[q]
        self.dcnt[q] += 1
        key = (q, i)
        prev = self.dtarget.get(key, 0)
        if prev:
            self._need(q, ("d", q, i, prev))
        tgt = prev + 16
        self.dtarget[key] = tgt
        sem = self.dsem[q][i]
        self.prog[q].append(lambda e, fn=fn, sem=sem: fn(e).then_inc(sem, 16))
        dep = ("d", q, i, tgt)
        for b in wb:
            b.w = dep
            b.r = []
        for b in rb:
            if b not in wb:
                b.r.append(dep)
        self.nins += 1
        return dep

    def dma_bg(self, q, out, in_, slot, w=()):
        wb = _bufs(w)
        while len(self.bgsems) <= slot:
            self.bgsems.append(self.stack0.enter_context(self.nc.semaphore("bg%d" % len(self.bgsems))))
            self.bgtgt.append(0)
        idx = slot
        sem = self.bgsems[idx]
        self.bgtgt[idx] += 16
        tgt = self.bgtgt[idx]
        self._deps(q, (), w)
        self.prog[q].append(lambda e, out=out, in_=in_, sem=sem: e.dma_start(out=out, in_=in_).then_inc(sem, 16))
        dep = ("b", idx, tgt)
        for b in wb:
            b.w = dep
            b.r = []
        self.nins += 1

    def raw(self, eng, fn):
        self.prog[eng].append(fn)

    def wait_all(self, eng, bufs):
        for b in _bufs(bufs):
            if b.w is not None:
                self._need(eng, b.w)

    @contextmanager
    def scope(self):
        old = self.stack
        with ExitStack() as st:
            self.stack = st
            yield
            self.barrier()
        self.stack = old

    def barrier(self):
        if self.bsem is None:
            self.bsem = self.stack0.enter_context(self.nc.semaphore("bar"))
            self.bscr = self.nc.dram_tensor("bar_scr", [2, 64], F32, kind="Internal")
        for e in ENGS:
            if e != "sp" and self.cnt[e]:
                self._need("sp", ("c", e, self.cnt[e]))
        for (q, i), tgt in self.dtarget.items():
            self._need("sp", ("d", q, i, tgt))
        self.bcnt += 1
        n = self.bcnt * 16
        bsem, bscr = self.bsem, self.bscr
        self.prog["sp"].append(lambda e: e.dma_start(out=bscr.ap()[1:2, :], in_=bscr.ap()[0:1, :]).then_inc(bsem, 16))
        for e in ENGS:
            self.prog[e].append(lambda e_, n=n: e_.wait_ge(bsem, n))
            for e2 in ENGS:
                self.known[e][("c", e2)] = self.cnt[e2]
            for (q, i), tgt in self.dtarget.items():
                self.known[e][("d", q, i)] = tgt

    def finish(self):
        nc = self.nc
        prog = self.prog
        emap = {"pe": "tensor", "dve": "vector", "act": "scalar", "pool": "gpsimd", "sp": "sync"}
        with nc.Block() as block:
            for en in ENGS:
                def body(e, en=en):
                    for f in prog[en]:
                        f(e)
                getattr(block, emap[en])(body)


S = 4096
D = 1024
NT = 32
NTG = 8
NL = 4
NE = 32
ALPHA = 8.0 ** 0.25
EPS = 1e-5
SEG = dict(a_val=(0, 512), a_gate=(512, 512), b_q=(1024, 256), b_k=(1280, 256), b_v=(1536, 512),
           b_r=(2048, 512), b_lr=(2560, 16), c_q=(2576, 512), c_k=(3088, 512), c_v=(3600, 512),
           d_x=(4112, 512), d_g=(4624, 512), g_m=(5136, 4096))
TOKMAJ = ("b_v", "b_r", "c_v")
INCOLS = 9232

WSPEC = [
    ("ln0_g", [1024]), ("ln0_b", [1024]), ("w_in", [4, 1024, 9232]), ("b_in", [4, 9232]),
    ("conv_a_w", [4, 31, 512]), ("conv_a_b", [4, 512]), ("ln_a_g", [4, 512]), ("ln_a_b", [4, 512]),
    ("gla_wa2", [4, 16, 256]), ("gla_ba", [4, 256]), ("gla_norm_g", [4, 128]),
    ("conv_d_w", [4, 4, 512]), ("conv_d_b", [4, 512]), ("lru_wa", [4, 8, 64, 64]), ("lru_ba", [4, 512]),
    ("lru_wx", [4, 8, 64, 64]), ("lru_bx", [4, 512]), ("lru_lambda", [4, 512]),
    ("w_branch", [4, 4, 512, 1024]), ("w_out", [4, 1024, 1024]), ("b_out", [4, 1024]),
    ("ln1_g", [4, 1024]), ("ln1_b", [4, 1024]), ("ca_wq", [4, 1024, 1024]), ("ca_wk", [4, 1024, 1024]),
    ("ca_wv", [4, 1024, 1024]), ("ca_wo", [4, 1024, 1024]), ("ln2_g", [4, 1024]), ("ln2_b", [4, 1024]),
    ("router_w", [4, 1024, 32]), ("router_b", [4, 32]), ("moe_w1", [4, 32, 1024, 2048]),
    ("moe_b1", [4, 32, 2048]), ("moe_w2", [4, 32, 1024, 1024]), ("moe_b2", [4, 32, 1024]),
    ("ln3_g", [4, 1024]), ("ln3_b", [4, 1024]),
]


def make_consts():
    c = {}
    c["ident"] = np.eye(128, dtype=np.float32)
    j = np.arange(128)
    c["U"] = (j[:, None] > j[None, :]).astype(np.float32)
    s64 = np.arange(64)
    c["tri64"] = (s64[:, None] <= s64[None, :]).astype(np.float32)
    sp = np.arange(128)[:, None]
    tp = np.arange(512)[None, :]
    c["sbmask"] = np.stack([((128 * jj + sp) < tp).astype(np.float32) for jj in range(4)], axis=1)
    t = np.arange(S)
    c["rmask"] = np.broadcast_to(((t % 64) != 0).astype(np.float32)[None, :], (128, S)).copy()
    return c


CONST_SHAPES = dict(ident=[128, 128], U=[128, 128], tri64=[64, 64], sbmask=[128, 4, 512], rmask=[128, S])


class Ctx:
    pass


LAYER_STAGES = []


def mm(k, P, lhsT, rhs, start, stop, r, w):
    k.op("pe", lambda e: e.matmul(P, lhsT=lhsT, rhs=rhs, start=start, stop=stop), r=r, w=w)


def act(k, out, in_, func, r, w, bias=None, scale=None, eng="act"):
    kw = {}
    if bias is not None:
        kw["bias"] = bias
    if scale is not None:
        kw["scale"] = scale
    k.op("act", lambda e: e.activation(out=out, in_=in_, func=func, **kw), r=r, w=w)


def tt(k, eng, out, in0, in1, op, r, w):
    k.op(eng, lambda e: e.tensor_tensor(out=out, in0=in0, in1=in1, op=op), r=r, w=w)


def ts(k, eng, out, in0, s1, s2, op0, op1, r, w):
    if op1 is None:
        k.op(eng, lambda e: e.tensor_scalar(out=out, in0=in0, scalar1=s1, scalar2=None, op0=op0), r=r, w=w)
    else:
        k.op(eng, lambda e: e.tensor_scalar(out=out, in0=in0, scalar1=s1, scalar2=s2, op0=op0, op1=op1), r=r, w=w)


def stt(k, out, in0, scalar, in1, op0, op1, r, w):
    k.op("dve", lambda e: e.scalar_tensor_tensor(out=out, in0=in0, scalar=scalar, in1=in1, op0=op0, op1=op1), r=r, w=w)


def cp(k, eng, out, in_, r, w):
    if eng == "act":
        k.op("act", lambda e: e.activation(out=out, in_=in_, func=AF.Copy), r=r, w=w)
    else:
        k.op(eng, lambda e: e.tensor_copy(out=out, in_=in_), r=r, w=w)


def bcast_load(k, q, tile, src_ap, n=128):
    k.dma(q, tile[:], src_ap.partition_broadcast(n), w=[tile])


def ln_alloc(k, c, tag, tp=True):
    c.ln_s6 = [k.sbuf("ln_s6%s%d" % (tag, i), [128, 2, 6], F32) for i in range(2)]
    c.ln_mv = [k.sbuf("ln_mv%s%d" % (tag, i), [128, 2], F32) for i in range(2)]
    c.ln_rs = [k.sbuf("ln_rs%s%d" % (tag, i), [128, 1], F32) for i in range(2)]
    c.ln_yb = [k.sbuf("ln_yb%s%d" % (tag, i), [128, 1024], BF16) for i in range(2)]
    c.ln_tp = [k.psum("ln_tp%s%d" % (tag, i), [128, 1024], BF16) for i in range(2)] if tp else [None, None]
    c.ln_g = k.sbuf("ln_g" + tag, [128, 1024], F32)
    c.ln_b = k.sbuf("ln_b" + tag, [128, 1024], F32)


def ln_tile(k, c, i, V, write_hT=True, hb=False, defer=False):
    S6, MV, RS, YB, TP = c.ln_s6[i % 2], c.ln_mv[i % 2], c.ln_rs[i % 2], c.ln_yb[i % 2], c.ln_tp[i % 2]
    for h in range(2):
        k.op("dve", lambda e, h=h: e.bn_stats(out=S6[:, h, :], in_=V[:, h * 512:(h + 1) * 512]), r=[V], w=[S6])
    k.op("dve", lambda e: e.bn_aggr(out=MV[:], in_=S6[:].rearrange("p a b -> p (a b)")), r=[S6], w=[MV])
    act(k, RS[:], MV[:, 1:2], AF.Ln, [MV], [RS], bias=EPS, scale=1.0)
    act(k, RS[:], RS[:], AF.Exp, [RS], [RS], scale=-0.5)
    ts(k, "dve", V[:], V[:], MV[:, 0:1], RS[:, 0:1], ALU.subtract, ALU.mult, [V, MV, RS], [V])
    tt(k, "pool", V[:], V[:], c.ln_g[:], ALU.mult, [V, c.ln_g], [V])
    tt(k, "pool", V[:], V[:], c.ln_b[:], ALU.add, [V, c.ln_b], [V])
    k.dma("sp", c.h_tok.t.ap()[i * 128:(i + 1) * 128, :], V[:], r=[V], w=[c.h_tok_b[i]])
    cp(k, "act", YB[:], V[:], [V], [YB])
    if hb:
        k.dma("sp", c.hb.t.ap()[i * 128:(i + 1) * 128, :], YB[:], r=[YB], w=[c.hb])
    if not write_hT:
        return None

    def back():
        for j in range(8):
            k.op("pe", lambda e, j=j: e.transpose(out=TP[:, j * 128:(j + 1) * 128], in_=YB[:, j * 128:(j + 1) * 128],
                                                  identity=c.ident_bf[:]), r=[YB, c.ident_bf], w=[TP])
        cp(k, "dve", c.hT[:, :, i * 128:(i + 1) * 128], TP[:].rearrange("p (a b) -> p a b", a=8), [TP], [c.hT_b[i]])
    if defer:
        return back
    back()
    return None


def hTr(c, tg):
    return c.hT_b[4 * tg:4 * tg + 4]


def stage_prologue(k, c):
    with k.scope():
        idf = k.sbuf("idf", [128, 128], F32)
        k.dma("sp", idf[:], c.consts["ident"].ap(), w=[idf])
        cp(k, "dve", c.ident_bf[:], idf[:], [idf], [c.ident_bf])
        cp(k, "pool", c.ident_f[:], idf[:], [idf], [c.ident_f])
        tp = k.psum("mtp", [128, 1024], BF16)
        for mt in range(2):
            mf = k.sbuf("memf%d" % mt, [128, 1024], F32)
            mb = k.sbuf("memb%d" % mt, [128, 1024], BF16)
            k.dma("sp", mf[:], c.mem.ap()[mt * 128:(mt + 1) * 128, :], w=[mf])
            cp(k, "act", mb[:], mf[:], [mf], [mb])
            for j in range(8):
                k.op("pe", lambda e, j=j, mb=mb: e.transpose(out=tp[:, j * 128:(j + 1) * 128], in_=mb[:, j * 128:(j + 1) * 128],
                                                             identity=c.ident_bf[:]), r=[mb, c.ident_bf], w=[tp])
            cp(k, "dve", c.memT[:, :, mt * 128:(mt + 1) * 128], tp[:].rearrange("p (a b) -> p a b", a=8), [tp], [c.memT])
        zr = k.sbuf("zrow", [1, 1024], BF16)
        k.op("pool", lambda e: e.memset(zr[:], 0.0), w=[zr])
        k.dma("sp", c.hb.t.ap()[S:S + 1, :], zr[:], r=[zr], w=[c.hb])
        ln_alloc(k, c, "0")
        bcast_load(k, "sp", c.ln_g, c.W["ln0_g"].ap())
        bcast_load(k, "sp", c.ln_b, c.W["ln0_b"].ap())
        V = [k.sbuf("ln0v%d" % i, [128, 1024], F32) for i in range(2)]
        pend = None
        for i in range(NT):
            k.dma("sp", V[i % 2][:], c.x.ap()[i * 128:(i + 1) * 128, :], w=[V[i % 2]])
            nb_ = ln_tile(k, c, i, V[i % 2], defer=True)
            if pend is not None:
                pend()
            pend = nb_
        pend()


def stage_inproj(k, c, l):
    with k.scope():
        w_in = c.W["w_in"].ap()[l].rearrange("(k p) n -> p k n", p=128)
        b_in = c.W["b_in"].ap()[l]
        biasA = k.sbuf("biasA", [128, 20], F32)
        biasL = k.sbuf("biasL", [16, 1], F32)
        biasB = k.sbuf("biasB", [128, 52], F32)
        k.dma("sp", biasA[:], b_in[0:2560].rearrange("(j p) -> p j", p=128), w=[biasA], allow_slow_non_contiguous=True)
        k.dma("sp", biasL[:], b_in[2560:2576].rearrange("(p o) -> p o", o=1), w=[biasL], allow_slow_non_contiguous=True)
        k.dma("sp", biasB[:], b_in[2576:9232].rearrange("(j p) -> p j", p=128), w=[biasB], allow_slow_non_contiguous=True)
        bbc = {}
        for name in TOKMAJ:
            bbc[name] = k.sbuf("bbc_" + name, [128, 512], F32)
            c0 = SEG[name][0]
            bcast_load(k, "sp", bbc[name], b_in[c0:c0 + 512])
        wblk = [k.sbuf("wblk%d" % i, [128, 8, 512], BF16) for i in range(2)]
        ps = [k.psum("ips%d" % i, [128, 512], F32) for i in range(4)]
        zrow = [k.sbuf("zrow%d" % i, [128, S], BF16) for i in range(2)]
        ztile = [k.sbuf("ztile%d" % i, [128, 512], BF16) for i in range(3)]
        blocks = []
        for name, (c0, w) in SEG.items():
            for o in range(0, w, 512):
                blocks.append((name, c0 + o, min(512, w - o)))
        pi = 0
        ci = 0
        zi = 0
        for bi, (name, c0, w) in enumerate(blocks):
            WB = wblk[bi % 2]
            k.dma("pool", WB[:, :, :w], w_in[:, :, c0:c0 + w], w=[WB])
            if name in TOKMAJ:
                for i in range(NT):
                    P = ps[pi % 4]
                    pi += 1
                    for kk in range(8):
                        mm(k, P[:, :w], c.hT[:, kk, i * 128:(i + 1) * 128], WB[:, kk, :w], kk == 0, kk == 7,
                           [c.hT_b[i], WB], [P])
                    Z = ztile[zi % 3]
                    zi += 1
                    tt(k, "dve", Z[:, :w], P[:, :w], bbc[name][:, :w], ALU.add, [P, bbc[name]], [Z])
                    k.dma("sp", c.ztok[name].t.ap()[i * 128:(i + 1) * 128, :], Z[:, :w], r=[Z], w=[c.ztok[name]])
            else:
                for cc in range(0, w, 128):
                    cw = min(128, w - cc)
                    a0 = c0 + cc
                    if a0 < 2560:
                        bias = biasA[:cw, a0 // 128:a0 // 128 + 1]
                        bt = biasA
                    elif a0 == 2560:
                        bias = biasL[:cw, 0:1]
                        bt = biasL
                    else:
                        j = (a0 - 2576) // 128
                        bias = biasB[:cw, j:j + 1]
                        bt = biasB
                    ZT = zrow[ci % 2]
                    ci += 1
                    for tg in range(NTG):
                        P = ps[pi % 4]
                        pi += 1
                        for kk in range(8):
                            mm(k, P[:cw, :], WB[:, kk, cc:cc + cw], c.hT[:, kk, tg * 512:(tg + 1) * 512], kk == 0, kk == 7,
                               hTr(c, tg) + [WB], [P])
                        if tg % 2 == 0:
                            act(k, ZT[:cw, tg * 512:(tg + 1) * 512], P[:cw, :], AF.Identity, [P, bt], [ZT], bias=bias, scale=1.0)
                        else:
                            ts(k, "dve", ZT[:cw, tg * 512:(tg + 1) * 512], P[:cw, :], bias, None, ALU.add, None, [P, bt], [ZT])
                    k.dma("sp", c.zT.t.ap()[a0:a0 + cw, :], ZT[:cw, :], r=[ZT], w=[c.zT_b])


def load_cols(k, c, name, rows, width=512):
    R = sum(r for _, r in rows)
    nch = width // 128
    stg = k.sbuf(name + "_stg", [R, width], F32)
    off = 0
    for ap, r in rows:
        k.dma("sp", stg[off:off + r, :], ap, w=[stg])
        off += r
    P = k.psum(name + "_ps", [128, nch, R], F32)
    out = k.sbuf(name, [128, nch, R], F32)
    for cc in range(nch):
        k.op("pe", lambda e, cc=cc: e.transpose(out=P[:, cc, :], in_=stg[:R, cc * 128:(cc + 1) * 128],
                                                identity=c.ident_f[:R, :R]), r=[stg, c.ident_f], w=[P])
    cp(k, "dve", out[:], P[:], [P], [out])
    return out


def row(ap):
    return ap.rearrange("(o n) -> o n", o=1)


def stage_A(k, c, l):
    W = c.W
    with k.scope():
        cols = load_cols(k, c, "acol", [(W["conv_a_w"].ap()[l], 31), (row(W["conv_a_b"].ap()[l]), 1),
                                        (row(W["ln_a_g"].ap()[l]), 1), (row(W["ln_a_b"].ap()[l]), 1)])
        diag = k.sbuf("diag", [128, 4, 31, 128], BF16)
        n = 0
        for cc in range(4):
            for j in range(31):
                ts(k, "dve", diag[:, cc, j, :], c.ident_f[:], cols[:, cc, j:j + 1], None, ALU.mult, None,
                   [c.ident_f, cols], [diag])
                n += 1
        upad = k.sbuf("upad", [128, 4, 30 + S], BF16)
        upb = [Buf("upad%d" % i) for i in range(4)]
        avt = [k.sbuf("avt%d" % i, [128, 2048], BF16) for i in range(2)]
        agt = [k.sbuf("agt%d" % i, [128, 2048], BF16) for i in range(2)]
        sg = [k.sbuf("asg%d" % i, [128, 2048], F32) for i in range(2)]
        n = 0
        for cc in range(4):
            k.op("pool", lambda e, cc=cc: e.memset(upad[:, cc, 0:30], 0.0), w=[upb[cc]])
            for hf in range(2):
                AV, AG, SG = avt[n % 2], agt[n % 2], sg[n % 2]
                n += 1
                k.dma("sp", AV[:], c.zT.t.ap()[cc * 128:(cc + 1) * 128, hf * 2048:(hf + 1) * 2048], r=[c.zT_b], w=[AV])
                k.dma("sp", AG[:], c.zT.t.ap()[512 + cc * 128:512 + (cc + 1) * 128, hf * 2048:(hf + 1) * 2048], r=[c.zT_b], w=[AG])
                act(k, SG[:], AG[:], AF.Sigmoid, [AG], [SG])
                tt(k, "dve" if hf == 0 else "pool", upad[:, cc, 30 + hf * 2048:30 + (hf + 1) * 2048], AV[:], SG[:], ALU.mult, [AV, SG], [upb[cc]])
        ones512 = k.sbuf("ones512", [128, 128], F32)
        k.op("pool", lambda e: e.memset(ones512[:], 1.0 / 512.0), w=[ones512])
        co = [k.sbuf("co%d" % i, [128, 512], F32) for i in range(4)]
        sq = [k.sbuf("sq%d" % i, [128, 512], F32) for i in range(4)]
        psc = [k.psum("psc%d" % i, [128, 512], F32) for i in range(2)]
        pmean = k.psum("pmean", [128, 512], F32)
        pex2 = k.psum("pex2", [128, 512], F32)
        mean = k.sbuf("amean", [128, 512], F32)
        msq = k.sbuf("amsq", [128, 512], F32)
        rstd = k.sbuf("arstd", [128, 512], F32)
        xn = [k.sbuf("axn%d" % i, [128, 512], F32) for i in range(2)]
        yt = [k.sbuf("ayt%d" % i, [128, 512], BF16) for i in range(2)]
        n = 0
        for tg in range(NTG):
            for cc in range(4):
                P = psc[n % 2]
                n += 1
                for j in range(31):
                    mm(k, P[:], diag[:, cc, j, :], upad[:, cc, tg * 512 + j:tg * 512 + j + 512], j == 0, j == 30, [diag, upb[cc]], [P])
                act(k, co[cc][:], P[:], AF.Identity, [P, cols], [co[cc]], bias=cols[:, cc, 31:32], scale=1.0)
                tt(k, "pool", sq[cc][:], co[cc][:], co[cc][:], ALU.mult, [co[cc]], [sq[cc]])
            for cc in range(4):
                mm(k, pmean[:], ones512[:], co[cc][:], cc == 0, cc == 3, [ones512, co[cc]], [pmean])
            for cc in range(4):
                mm(k, pex2[:], ones512[:], sq[cc][:], cc == 0, cc == 3, [ones512, sq[cc]], [pex2])
            cp(k, "act", mean[:], pmean[:], [pmean], [mean])
            act(k, msq[:], pmean[:], AF.Square, [pmean], [msq])
            tt(k, "dve", rstd[:], pex2[:], msq[:], ALU.subtract, [pex2, msq], [rstd])
            act(k, rstd[:], rstd[:], AF.Ln, [rstd], [rstd], bias=EPS, scale=1.0)
            act(k, rstd[:], rstd[:], AF.Exp, [rstd], [rstd], scale=-0.5)
            for cc in range(4):
                X = xn[cc % 2]
                Y = yt[cc % 2]
                tt(k, "dve", X[:], co[cc][:], mean[:], ALU.subtract, [co[cc], mean], [X])
                tt(k, "pool", X[:], X[:], rstd[:], ALU.mult, [X, rstd], [X])
                act(k, Y[:], X[:], AF.Silu, [X, cols], [Y], bias=cols[:, cc, 33:34], scale=cols[:, cc, 32:33])
                k.dma("sp", c.yT.t.ap()[cc * 128:(cc + 1) * 128, tg * 512:(tg + 1) * 512], Y[:], r=[Y], w=[c.yT.b])


LAYER_STAGES.append(("A", stage_A))


def stage_D(k, c, l):
    W = c.W
    with k.scope():
        cols = load_cols(k, c, "dcol", [(W["conv_d_w"].ap()[l], 4), (row(W["conv_d_b"].ap()[l]), 1), (row(W["lru_ba"].ap()[l]), 1),
                                        (row(W["lru_bx"].ap()[l]), 1), (row(W["lru_lambda"].ap()[l]), 1)])
        xe = k.sbuf("d_xe", [128, 4], F32)
        ln1 = k.sbuf("d_ln1", [128, 4], F32)
        tq = k.sbuf("d_tq", [128, 4], F32)
        msk = k.sbuf("d_msk", [128, 4], F32)
        cl = k.sbuf("d_cl", [128, 4], F32)
        cl2 = k.sbuf("d_cl2", [128, 4], F32)
        act(k, xe[:], cols[:, :, 7], AF.Exp, [cols], [xe], scale=-1.0)
        act(k, ln1[:], xe[:], AF.Ln, [xe], [ln1], bias=1.0, scale=1.0)
        ts(k, "dve", tq[:], xe[:], -0.25, 1.0 / 3.0, ALU.mult, ALU.add, [xe], [tq])
        tt(k, "dve", tq[:], tq[:], xe[:], ALU.mult, [tq, xe], [tq])
        ts(k, "dve", tq[:], tq[:], -1.0, 0.5, ALU.mult, ALU.add, [tq], [tq])
        tt(k, "dve", tq[:], tq[:], xe[:], ALU.mult, [tq, xe], [tq])
        ts(k, "dve", tq[:], tq[:], -1.0, 1.0, ALU.mult, ALU.add, [tq], [tq])
        tt(k, "dve", tq[:], tq[:], xe[:], ALU.mult, [tq, xe], [tq])
        ts(k, "dve", msk[:], xe[:], 0.05, None, ALU.is_lt, None, [xe], [msk])
        tt(k, "dve", tq[:], tq[:], ln1[:], ALU.subtract, [tq, ln1], [tq])
        tt(k, "dve", tq[:], tq[:], msk[:], ALU.mult, [tq, msk], [tq])
        tt(k, "dve", tq[:], tq[:], ln1[:], ALU.add, [tq, ln1], [tq])
        ts(k, "dve", cl[:], tq[:], -8.0, None, ALU.mult, None, [tq], [cl])
        ts(k, "dve", cl2[:], tq[:], -16.0, None, ALU.mult, None, [tq], [cl2])
        bd = {}
        for nm in ("lru_wa", "lru_wx"):
            stg = k.sbuf(nm + "_stg", [128, 4, 128], F32)
            k.op("pool", lambda e, stg=stg: e.memset(stg[:], 0.0), w=[stg])
            for cc in range(4):
                k.dma("sp", stg[0:64, cc, 0:64], W[nm].ap()[l, 2 * cc], w=[stg])
                k.dma("sp", stg[64:128, cc, 64:128], W[nm].ap()[l, 2 * cc + 1], w=[stg])
            bd[nm] = k.sbuf(nm + "_bd", [128, 4, 128], BF16)
            cp(k, "dve", bd[nm][:], stg[:], [stg], [bd[nm]])
        HW_ = 2048
        xin = k.sbuf("d_xin", [128, 3 + HW_], BF16)
        dg = k.sbuf("d_dg", [128, HW_], BF16)
        xc = k.sbuf("d_xc", [128, HW_], F32)
        xcb = k.sbuf("d_xcb", [128, HW_], BF16)
        rr = k.sbuf("d_r", [128, HW_], F32)
        ii = k.sbuf("d_i", [128, HW_], F32)
        aa = k.sbuf("d_a", [128, HW_], F32)
        s1 = k.sbuf("d_s1", [128, HW_], F32)
        hh = [k.sbuf("d_h%d" % i, [128, HW_], F32) for i in range(2)]
        g1 = k.sbuf("d_g1", [128, HW_], F32)
        g2 = k.sbuf("d_g2", [128, HW_], F32)
        yo = k.sbuf("d_yo", [128, HW_], BF16)
        pg = [k.psum("d_pg%d" % i, [128, 512], F32) for i in range(4)]
        n = 0
        for cc in range(4):
            r0 = SEG["d_x"][0] + cc * 128
            g0 = SEG["d_g"][0] + cc * 128
            for hf in range(2):
                if hf == 0:
                    k.op("pool", lambda e: e.memset(xin[:, 0:3], 0.0), w=[xin])
                    k.dma("sp", xin[:, 3:3 + HW_], c.zT.t.ap()[r0:r0 + 128, 0:HW_], r=[c.zT_b], w=[xin])
                else:
                    k.dma("sp", xin[:], c.zT.t.ap()[r0:r0 + 128, HW_ - 3:2 * HW_], r=[c.zT_b], w=[xin])
                k.dma("sp", dg[:], c.zT.t.ap()[g0:g0 + 128, hf * HW_:(hf + 1) * HW_], r=[c.zT_b], w=[dg])
                ts(k, "dve", xc[:], xin[:, 0:HW_], cols[:, cc, 0:1], cols[:, cc, 4:5], ALU.mult, ALU.add, [xin, cols], [xc])
                for j in range(1, 4):
                    stt(k, xc[:], xin[:, j:j + HW_], cols[:, cc, j:j + 1], xc[:], ALU.mult, ALU.add, [xin, cols, xc], [xc])
                cp(k, "pool", xcb[:], xc[:], [xc], [xcb])
                for q in range(4):
                    sl = slice(q * 512, (q + 1) * 512)
                    P = pg[n % 4]
                    n += 1
                    mm(k, P[:], bd["lru_wa"][:, cc, :], xcb[:, sl], True, True, [bd["lru_wa"], xcb], [P])
                    act(k, rr[:, sl], P[:], AF.Sigmoid, [P, cols], [rr], bias=cols[:, cc, 5:6], scale=1.0)
                    P = pg[n % 4]
                    n += 1
                    mm(k, P[:], bd["lru_wx"][:, cc, :], xcb[:, sl], True, True, [bd["lru_wx"], xcb], [P])
                    act(k, ii[:, sl], P[:], AF.Sigmoid, [P, cols], [ii], bias=cols[:, cc, 6:7], scale=1.0)
                act(k, aa[:], rr[:], AF.Exp, [rr, cl], [aa], scale=cl[:, cc:cc + 1])
                act(k, s1[:], rr[:], AF.Exp, [rr, cl2], [s1], scale=cl2[:, cc:cc + 1])
                act(k, s1[:], s1[:], AF.Sqrt, [s1], [s1], bias=1.0, scale=-1.0)
                tt(k, "pool", ii[:], ii[:], xc[:], ALU.mult, [ii, xc], [ii])
                tt(k, "pool", ii[:], ii[:], s1[:], ALU.mult, [ii, s1], [ii])
                H = hh[hf]
                init = 0.0 if hf == 0 else hh[0][:, HW_ - 1:HW_]
                k.op("dve", lambda e, H=H, init=init: e.tensor_tensor_scan(out=H[:], data0=aa[:], data1=ii[:], initial=init,
                                                                           op0=ALU.mult, op1=ALU.add), r=[aa, ii, hh[0]], w=[H])
                tt(k, "pool", g1[:], dg[:], dg[:], ALU.mult, [dg], [g1])
                ts(k, "dve", g1[:], g1[:], 0.044715, 1.0, ALU.mult, ALU.add, [g1], [g1])
                tt(k, "pool", g1[:], g1[:], dg[:], ALU.mult, [g1, dg], [g1])
                act(k, g2[:], g1[:], AF.Sigmoid, [g1], [g2], scale=1.5957691216057308)
                tt(k, "pool", g2[:], g2[:], dg[:], ALU.mult, [g2, dg], [g2])
                tt(k, "dve", yo[:], H[:], g2[:], ALU.mult, [H, g2], [yo])
                k.dma("sp", c.yT.t.ap()[1536 + cc * 128:1536 + (cc + 1) * 128, hf * HW_:(hf + 1) * HW_], yo[:], r=[yo], w=[c.yT.b])


LAYER_STAGES.append(("D", stage_D))


def stage_B(k, c, l):
    W = c.W
    zT = c.zT.t.ap()
    with k.scope():
        qd = [k.sbuf("b_qd%d" % i, [128, S], BF16) for i in range(2)]
        ki = [k.sbuf("b_ki%d" % i, [128, S], BF16) for i in range(2)]
        ke = [k.sbuf("b_ke%d" % i, [128, S], BF16) for i in range(2)]
        dec = [k.sbuf("b_dec%d" % i, [128, 64], F32) for i in range(2)]
        with k.scope():
            cols = load_cols(k, c, "bcol", [(row(W["gla_ba"].ap()[l]), 1)], width=256)
            nba = k.sbuf("b_nba", [128, 2], F32)
            ts(k, "dve", nba[:], cols[:, :, 0], -1.0, None, ALU.mult, None, [cols], [nba])
            wa2f = k.sbuf("b_wa2f", [16, 256], F32)
            wa2b = k.sbuf("b_wa2b", [16, 256], BF16)
            k.dma("sp", wa2f[:], W["gla_wa2"].ap()[l], w=[wa2f])
            cp(k, "dve", wa2b[:], wa2f[:], [wa2f], [wa2b])
            lrT = k.sbuf("b_lrT", [16, S], BF16)
            k.dma("sp", lrT[:], zT[2560:2576, :], r=[c.zT_b], w=[lrT])
            rmask = k.sbuf("b_rmask", [128, S], F32)
            k.dma("sp", rmask[:], c.consts["rmask"].ap(), w=[rmask])
            la = k.sbuf("b_la", [128, S], F32)
            bb = k.sbuf("b_bb", [128, S], F32)
            tf = la
            qT = k.sbuf("b_qT", [128, S], BF16)
            kT = k.sbuf("b_kT", [128, S], BF16)
            pl = [k.psum("b_pl%d" % i, [128, 512], F32) for i in range(2)]
            for ch in range(2):
                k.dma("sp", qT[:], zT[1024 + ch * 128:1024 + (ch + 1) * 128, :], r=[c.zT_b], w=[qT])
                k.dma("sp", kT[:], zT[1280 + ch * 128:1280 + (ch + 1) * 128, :], r=[c.zT_b], w=[kT])
                for tg in range(NTG):
                    P = pl[tg % 2]
                    mm(k, P[:], wa2b[:, ch * 128:(ch + 1) * 128], lrT[:, tg * 512:(tg + 1) * 512], True, True, [wa2b, lrT], [P])
                    act(k, la[:, tg * 512:(tg + 1) * 512], P[:], AF.Exp, [P, nba], [la], bias=nba[:, ch:ch + 1], scale=-1.0)
                act(k, la[:], la[:], AF.Ln, [la], [la], bias=1.0, scale=1.0)
                ts(k, "dve", la[:], la[:], -1.0 / 16.0, None, ALU.mult, None, [la], [la])
                k.op("dve", lambda e: e.tensor_tensor_scan(out=bb[:], data0=rmask[:], data1=la[:], initial=0.0,
                                                           op0=ALU.mult, op1=ALU.add), r=[rmask, la], w=[bb])
                bbv = bb[:].rearrange("p (c t) -> p c t", t=64)
                act(k, dec[ch][:], bbv[:, :, 63], AF.Exp, [bb], [dec[ch]])
                act(k, tf[:], bb[:], AF.Exp, [bb], [tf])
                stt(k, qd[ch][:], qT[:], 0.125, tf[:], ALU.mult, ALU.mult, [qT, tf], [qd[ch]])
                act(k, tf[:], bb[:], AF.Exp, [bb], [tf], scale=-1.0)
                tt(k, "pool", ki[ch][:], kT[:], tf[:], ALU.mult, [kT, tf], [ki[ch]])
                tfv = tf[:].rearrange("p (c t) -> p c t", t=64)
                tt(k, "dve", tfv, bbv[:, :, 63:64].to_broadcast([128, 64, 64]), bbv, ALU.subtract, [bb], [tf])
                act(k, tf[:], tf[:], AF.Exp, [tf], [tf])
                tt(k, "pool", ke[ch][:], kT[:], tf[:], ALU.mult, [kT, tf], [ke[ch]])
        tri = k.sbuf("b_tri", [64, 64], F32)
        k.dma("sp", tri[:], c.consts["tri64"].ap(), w=[tri])
        gn = k.sbuf("b_gn", [64, 512], F32)
        for h in range(4):
            k.dma("sp", gn[:, h * 128:(h + 1) * 128], W["gla_norm_g"].ap()[l].partition_broadcast(64), w=[gn])
        ybT = k.sbuf("b_ybT", [128, 4, S], BF16)
        vt = [k.sbuf("b_vt%d" % i, [64, 512], BF16) for i in range(3)]
        rt = [k.sbuf("b_rt%d" % i, [64, 512], BF16) for i in range(3)]
        ket = [k.sbuf("b_ket%d" % i, [64, 256], BF16) for i in range(2)]
        Sst = [k.sbuf("b_S%d" % i, [128, 128], F32) for i in range(2)]
        Sbf = [k.sbuf("b_Sbf%d" % i, [128, 128], BF16) for i in range(2)]
        stm = [k.sbuf("b_stm%d" % i, [64, 64], BF16) for i in range(2)]
        osb = k.sbuf("b_osb", [64, 512], F32)
        sqv = k.sbuf("b_sqv", [64, 512], F32)
        ssum = k.sbuf("b_ssum", [64, 4], F32)
        sig = k.sbuf("b_sig", [64, 512], F32)
        yb = k.sbuf("b_yb", [64, 512], BF16)
        for ch in range(2):
            k.op("pool", lambda e, ch=ch: e.memset(Sst[ch][:], 0.0), w=[Sst[ch]])
            k.op("pool", lambda e, ch=ch: e.memset(Sbf[ch][:], 0.0), w=[Sbf[ch]])
        pT = k.psum("b_pT", [64, 256], BF16)
        pS = [k.psum("b_pS%d" % i, [64, 64], F32) for i in range(2)]
        pO = [k.psum("b_pO%d" % i, [64, 512], F32) for i in range(2)]
        pKV = [k.psum("b_pKV%d" % i, [128, 128], F32) for i in range(2)]
        pY = k.psum("b_pY", [128, 4, 64], BF16)
        n = 0
        posts = []

        def post(PO, RT, sl):
            _gla_post(k, c, PO, RT, sl, osb, sqv, ssum, sig, yb, gn, pY, ybT)
        for ci in range(64):
            t0 = ci * 64
            sl = slice(t0, t0 + 64)
            VT, RT, KET, PO = vt[ci % 3], rt[ci % 3], ket[ci % 2], pO[ci % 2]
            k.dma("sp", VT[:], c.ztok["b_v"].t.ap()[t0:t0 + 64, :], r=[c.ztok["b_v"]], w=[VT])
            k.dma("sp", RT[:], c.ztok["b_r"].t.ap()[t0:t0 + 64, :], r=[c.ztok["b_r"]], w=[RT])
            for ch in range(2):
                k.op("pe", lambda e, ch=ch, sl=sl: e.transpose(out=pT[:, ch * 128:(ch + 1) * 128], in_=ke[ch][:, sl], identity=c.ident_bf[:]),
                     r=[ke[ch], c.ident_bf], w=[pT])
            cp(k, "act", KET[:], pT[:], [pT], [KET])
            for h in range(4):
                ch, pb = h // 2, (h % 2) * 64
                PS, STM, PK = pS[n % 2], stm[n % 2], pKV[n % 2]
                n += 1
                hs = slice(h * 128, (h + 1) * 128)
                mm(k, PS[:], ki[ch][pb:pb + 64, sl], qd[ch][pb:pb + 64, sl], True, True, [ki[ch], qd[ch]], [PS])
                tt(k, "dve", STM[:], PS[:], tri[:], ALU.mult, [PS, tri], [STM])
                mm(k, PO[:, hs], STM[:], VT[:, hs], True, False, [STM, VT], [PO])
                mm(k, PO[:, hs], qd[ch][pb:pb + 64, sl], Sbf[ch][pb:pb + 64, :], False, True, [qd[ch], Sbf[ch]], [PO])
                mm(k, PK[:], KET[:, ch * 128:(ch + 1) * 128], VT[:, hs], True, True, [KET, VT], [PK])
                stt(k, Sst[ch][pb:pb + 64, :], Sst[ch][pb:pb + 64, :], dec[ch][pb:pb + 64, ci:ci + 1], PK[pb:pb + 64, :], ALU.mult, ALU.add,
                    [Sst[ch], dec[ch], PK], [Sst[ch]])
                cp(k, "pool", Sbf[ch][pb:pb + 64, :], Sst[ch][pb:pb + 64, :], [Sst[ch]], [Sbf[ch]])
            posts.append((PO, RT, sl))
            if len(posts) > 1:
                post(*posts.pop(0))
        post(*posts.pop(0))
        for j in range(4):
            k.dma("sp", c.yT.t.ap()[512 + j * 128:512 + (j + 1) * 128, :], ybT[:, j, :], r=[ybT], w=[c.yT.b])


def _gla_post(k, c, PO, RT, sl, osb, sqv, ssum, sig, yb, gn, pY, ybT):
    if True:
        if True:
            cp(k, "act", osb[:], PO[:], [PO], [osb])
            tt(k, "pool", sqv[:], osb[:], osb[:], ALU.mult, [osb], [sqv])
            k.op("dve", lambda e: e.tensor_reduce(out=ssum[:], in_=sqv[:].rearrange("p (h d) -> p h d", h=4), axis=AX.X, op=ALU.add),
                 r=[sqv], w=[ssum])
            act(k, ssum[:], ssum[:], AF.Ln, [ssum], [ssum], bias=EPS, scale=1.0 / 128.0)
            act(k, ssum[:], ssum[:], AF.Exp, [ssum], [ssum], scale=-0.5)
            osv = osb[:].rearrange("p (h d) -> p h d", h=4)
            tt(k, "dve", osv, osv, ssum[:].rearrange("p (h o) -> p h o", o=1).to_broadcast([64, 4, 128]), ALU.mult, [osb, ssum], [osb])
            tt(k, "pool", osb[:], osb[:], gn[:], ALU.mult, [osb, gn], [osb])
            act(k, sig[:], RT[:], AF.Sigmoid, [RT], [sig])
            tt(k, "pool", sig[:], sig[:], RT[:], ALU.mult, [sig, RT], [sig])
            tt(k, "dve", yb[:], osb[:], sig[:], ALU.mult, [osb, sig], [yb])
            for j in range(4):
                k.op("pe", lambda e, j=j: e.transpose(out=pY[:, j, :], in_=yb[:, j * 128:(j + 1) * 128], identity=c.ident_bf[:64, :64]),
                     r=[yb, c.ident_bf], w=[pY])
            cp(k, "dve", ybT[:, :, sl], pY[:], [pY], [ybT])


LAYER_STAGES.append(("B", stage_B))


def stage_C(k, c, l):
    zT = c.zT.t.ap()
    with k.scope():
        uf = k.sbuf("c_uf", [128, 128], F32)
        Ub = k.sbuf("c_Ub", [128, 128], BF16)
        onesb = k.sbuf("c_ones", [128, 128], BF16)
        k.dma("sp", uf[:], c.consts["U"].ap(), w=[uf])
        ts(k, "dve", Ub[:], uf[:], -1.0, None, ALU.mult, None, [uf], [Ub])
        k.op("pool", lambda e: e.memset(onesb[:], -1.0), w=[onesb])
        mk = k.sbuf("c_mk", [128, 4, 512], F32)
        k.dma("sp", mk[:], c.consts["sbmask"].ap(), w=[mk])
        cv = k.sbuf("c_cv", [128, 32, 512], BF16)
        k.dma("sp", cv[:], c.ztok["c_v"].t.ap().rearrange("(j p) n -> p j n", p=128), r=[c.ztok["c_v"]], w=[cv])
        yct = k.sbuf("c_yct", [128, 4, S], BF16)
        qT = [k.sbuf("c_qT%d" % i, [128, S], BF16) for i in range(1)]
        kT = [k.sbuf("c_kT%d" % i, [128, S], BF16) for i in range(1)]
        E = [k.sbuf("c_E%d" % i, [128, 512], F32) for i in range(2)]
        SP = [k.sbuf("c_SP%d" % i, [128, 512], F32) for i in range(3)]
        LK = [k.sbuf("c_LK%d" % i, [128, 512], BF16) for i in range(4)]
        T1 = [k.sbuf("c_T1%d" % i, [128, 512], F32) for i in range(6)]
        WT = [k.sbuf("c_WT%d" % i, [128, 512], BF16) for i in range(4)]
        pz = [k.psum("c_pz%d" % i, [128, 512], F32) for i in range(2)]
        pt = [k.psum("c_pt%d" % i, [128, 512], F32) for i in range(2)]
        pc = [k.psum("c_pc%d" % i, [128, 512], F32) for i in range(2)]
        po = [k.psum("c_po%d" % i, [128, 512], F32) for i in range(2)]
        Rb = [k.sbuf("c_R%d" % i, [128, 512], F32) for i in range(4)]
        for ch in range(4):
            Q, Kt = qT[0], kT[0]
            k.dma("sp", Q[:], zT[2576 + ch * 128:2576 + (ch + 1) * 128, :], r=[c.zT_b], w=[Q])
            k.dma("sp", Kt[:], zT[3088 + ch * 128:3088 + (ch + 1) * 128, :], r=[c.zT_b], w=[Kt])
            tiles = []
            for hh in range(2):
                for tg in range(NTG):
                    Js = list(range(4 * tg + 3, -1, -1))
                    for idx, J in enumerate(Js):
                        tiles.append(dict(hh=hh, tg=tg, J=J, first=idx == 0, last=idx == len(Js) - 1, grp=hh * NTG + tg))

            def info(n, t):
                pb = t["hh"] * 64
                J, tg = t["J"], t["tg"]
                return pb, J, tg, slice(tg * 512, (tg + 1) * 512), J >= 4 * tg, J - 4 * tg

            def stA(n, t):
                pb, J, tg, ts_, dg, jj = info(n, t)
                PZ, e_, sp_, lk_, t1_ = pz[n % 2], E[n % 2], SP[n % 3], LK[n % 4], T1[n % 6]
                mm(k, PZ[:], Kt[pb:pb + 64, J * 128:(J + 1) * 128], Q[pb:pb + 64, ts_], True, True, [Kt, Q], [PZ])
                act(k, e_[:], PZ[:], AF.Exp, [PZ], [e_], scale=0.125)
                act(k, sp_[:], e_[:], AF.Ln, [e_], [sp_], bias=1.0, scale=1.0)
                if dg:
                    tt(k, "pool", lk_[:], sp_[:], mk[:, jj, :], ALU.mult, [sp_, mk], [lk_])
                else:
                    cp(k, "act", lk_[:], sp_[:], [sp_], [lk_])
                stt(k, t1_[:], PZ[:], 0.125, sp_[:], ALU.mult, ALU.subtract, [PZ, sp_], [t1_])

            def stB(n, t):
                pb, J, tg, ts_, dg, jj = info(n, t)
                PT, PC, lk_, t1_ = pt[n % 2], pc[n % 2], LK[n % 4], T1[n % 6]
                mm(k, PT[:], Ub[:], lk_[:], True, True, [Ub, lk_], [PT])
                if not t["last"]:
                    mm(k, PC[:], onesb[:], lk_[:], True, True, [onesb, lk_], [PC])
                tt(k, "dve", t1_[:], PT[:], t1_[:], ALU.add, [PT, t1_], [t1_])
                if not t["first"]:
                    tt(k, "pool", t1_[:], t1_[:], Rb[n % 4][:], ALU.add, [t1_, Rb[n % 4]], [t1_])
                if not t["last"]:
                    if t["first"]:
                        cp(k, "dve", Rb[(n + 1) % 4][:], PC[:], [PC], [Rb[(n + 1) % 4]])
                    else:
                        tt(k, "dve", Rb[(n + 1) % 4][:], PC[:], Rb[n % 4][:], ALU.add, [PC, Rb[n % 4]], [Rb[(n + 1) % 4]])

            def stC(n, t):
                pb, J, tg, ts_, dg, jj = info(n, t)
                t1_, w_ = T1[n % 6], WT[n % 4]
                act(k, w_[:], t1_[:], AF.Exp, [t1_], [w_])
                if dg:
                    tt(k, "pool", w_[:], w_[:], mk[:, jj, :], ALU.mult, [w_, mk], [w_])

            def stD(n, t):
                pb, J, tg, ts_, dg, jj = info(n, t)
                w_ = WT[n % 4]
                PO = po[t["grp"] % 2]
                mm(k, PO[:], cv[:, J, ch * 128:(ch + 1) * 128], w_[:], t["first"], t["last"], [cv, w_], [PO])
                if t["last"]:
                    cp(k, "act", yct[pb:pb + 64, ch, ts_], PO[pb:pb + 64, :], [PO], [yct])

            NTL = len(tiles)
            for step in range(NTL + 3):
                for off, fn in ((0, stA), (1, stB), (2, stC), (3, stD)):
                    n = step - off
                    if 0 <= n < NTL:
                        fn(n, tiles[n])
        for ch in range(4):
            k.dma("sp", c.yT.t.ap()[1024 + ch * 128:1024 + (ch + 1) * 128, :], yct[:, ch, :], r=[yct], w=[c.yT.b])


LAYER_STAGES.append(("C", stage_C))


def stage_M(k, c, l):
    W = c.W
    zT = c.zT.t.ap()
    with k.scope():
        wbs = [k.sbuf("m_wb%d" % i, [128, 4, 1024], BF16) for i in range(2)]
        wo = k.sbuf("m_wo", [128, 8, 1024], BF16)
        k.dma("pool", wo[:], W["w_out"].ap()[l].rearrange("(kk p) d -> p kk d", p=128), w=[wo])
        ln_alloc(k, c, "1")
        bcast_load(k, "sp", c.ln_g, W["ln1_g"].ap()[l])
        bcast_load(k, "sp", c.ln_b, W["ln1_b"].ap()[l])
        bo = k.sbuf("m_bo", [128, 1024], F32)
        bcast_load(k, "sp", bo, W["b_out"].ap()[l])
        yt = [k.sbuf("m_yt%d" % i, [128, 16, 512], BF16) for i in range(1)]
        gt = [k.sbuf("m_gt%d" % i, [128, 8, 512], BF16) for i in range(2)]
        sg = [k.sbuf("m_sg%d" % i, [128, 512], F32) for i in range(2)]
        tmp = [k.sbuf("m_tmp%d" % i, [128, 512], F32) for i in range(2)]
        acc = k.sbuf("m_acc", [128, 8, 512], F32)
        mT = [k.sbuf("m_mT%d" % i, [128, 8, 512], BF16) for i in range(1)]
        V = [k.sbuf("m_V%d" % i, [128, 1024], F32) for i in range(2)]
        hold = [k.sbuf("m_hold%d" % i, [128, 1024], F32) for i in range(2)]
        pp = [k.psum("m_pp%d" % i, [128, 512], F32) for i in range(2)]
        pm = [k.psum("m_pm%d" % i, [128, 512], F32) for i in range(4)]
        n = 0
        ng = 0
        pend = [None]

        def mload(g):
            tg_, nb_ = divmod(g, 4)
            tsl_ = slice(tg_ * 512, (tg_ + 1) * 512)
            k.dma("pool", wbs[g % 2][:], W["w_branch"].ap()[l, nb_].rearrange("(cc p) d -> p cc d", p=128), w=[wbs[g % 2]])
            g0 = SEG["g_m"][0] + nb_ * 1024
            k.dma("sp", gt[g % 2][:], zT[g0:g0 + 1024, :].rearrange("(j p) t -> p j t", p=128)[:, :, tsl_], r=[c.zT_b], w=[gt[g % 2]])

        for tg in range(NTG):
            ts_ = slice(tg * 512, (tg + 1) * 512)
            YT, MT = yt[0], mT[0]
            k.dma("sp", YT[:], c.yT.t.ap().rearrange("(j p) t -> p j t", p=128)[:, :, ts_], r=[c.yT.b], w=[YT])
            for nb in range(4):
                GT = gt[ng % 2]
                wb = wbs[ng % 2]
                if ng == 0:
                    mload(0)
                if ng + 1 < 4 * NTG:
                    mload(ng + 1)
                ng += 1
                for dc in range(8):
                    P, SG, TM = pp[n % 2], sg[n % 2], tmp[n % 2]
                    n += 1
                    for cc in range(4):
                        mm(k, P[:], wb[:, cc, dc * 128:(dc + 1) * 128], YT[:, nb * 4 + cc, :], cc == 0, cc == 3, [wb, YT], [P])
                    act(k, SG[:], GT[:, dc, :], AF.Sigmoid, [GT], [SG])
                    if nb == 0:
                        tt(k, "dve", acc[:, dc, :], P[:], SG[:], ALU.mult, [P, SG], [acc])
                    else:
                        tt(k, "dve", TM[:], P[:], SG[:], ALU.mult, [P, SG], [TM])
                        if nb < 3:
                            tt(k, "pool", acc[:, dc, :], acc[:, dc, :], TM[:], ALU.add, [acc, TM], [acc])
                        else:
                            tt(k, "pool", MT[:, dc, :], acc[:, dc, :], TM[:], ALU.add, [acc, TM], [MT])
            for t4 in range(4):
                i = tg * 4 + t4
                H, VV = hold[i % 2], V[i % 2]
                k.dma("sp", H[:], c.h_tok.t.ap()[i * 128:(i + 1) * 128, :], r=[c.h_tok_b[i]], w=[H])
                for hf in range(2):
                    P = pm[(i % 2) * 2 + hf]
                    hs = slice(hf * 512, (hf + 1) * 512)
                    for dc in range(8):
                        mm(k, P[:], MT[:, dc, t4 * 128:(t4 + 1) * 128], wo[:, dc, hs], dc == 0, dc == 7, [MT, wo], [P])
                    stt(k, VV[:, hs], H[:, hs], ALPHA, P[:], ALU.mult, ALU.add, [H, P], [VV])
                tt(k, "pool", VV[:], VV[:], bo[:], ALU.add, [VV, bo], [VV])
                nb_ = ln_tile(k, c, i, VV, defer=True)
                if pend[0] is not None:
                    pend[0]()
                pend[0] = nb_
        pend[0]()


LAYER_STAGES.append(("M", stage_M))


SPARSE_LN2 = True


def stage_X(k, c, l):
    W = c.W

    def wload(nm, tag):
        t = k.sbuf(tag, [128, 8, 1024], BF16)
        k.dma("pool", t[:], W[nm].ap()[l].rearrange("(kk p) d -> p kk d", p=128), w=[t])
        return t

    with k.scope():
        kT = k.sbuf("x_kT", [128, 8, 256], BF16)
        vtok = k.sbuf("x_v", [128, 2, 1024], BF16)
        with k.scope():
            wk = wload("ca_wk", "x_wk")
            wv = wload("ca_wv", "x_wv")
            ps = [k.psum("x_ps%d" % i, [128, 512], F32) for i in range(2)]
            n = 0
            for dcc in range(8):
                P = ps[n % 2]
                n += 1
                for kk in range(8):
                    mm(k, P[:, :256], wk[:, kk, dcc * 128:(dcc + 1) * 128], c.memT[:, kk, :], kk == 0, kk == 7, [wk, c.memT], [P])
                cp(k, "act" if dcc % 2 else "dve", kT[:, dcc, :], P[:, :256], [P], [kT])
            for mt in range(2):
                for hf in range(2):
                    P = ps[n % 2]
                    n += 1
                    for kk in range(8):
                        mm(k, P[:], c.memT[:, kk, mt * 128:(mt + 1) * 128], wv[:, kk, hf * 512:(hf + 1) * 512], kk == 0, kk == 7, [wv, c.memT], [P])
                    cp(k, "act" if hf else "dve", vtok[:, mt, hf * 512:(hf + 1) * 512], P[:], [P], [vtok])
        wq = wload("ca_wq", "x_wq")
        wo = wload("ca_wo", "x_wo")
        ln_alloc(k, c, "2", tp=not SPARSE_LN2)
        bcast_load(k, "sp", c.ln_g, W["ln2_g"].ap()[l])
        bcast_load(k, "sp", c.ln_b, W["ln2_b"].ap()[l])
        qT = [k.sbuf("x_qT%d" % i, [128, 8, 512], BF16) for i in range(2)]
        pT = [k.sbuf("x_pT%d" % i, [128, 2, 512], BF16) for i in range(2)]
        oT = [k.sbuf("x_oT%d" % i, [128, 8, 512], BF16) for i in range(2)]
        pf4 = [k.sbuf("x_pf%d" % i, [128, 256], F32) for i in range(4)]
        mx4 = [k.sbuf("x_mx4%d" % i, [128, 1], F32) for i in range(4)]
        pb = [k.sbuf("x_pb%d" % i, [128, 256], BF16) for i in range(2)]
        mx = [k.sbuf("x_mx%d" % i, [128, 1], F32) for i in range(2)]
        rs = [k.sbuf("x_rs%d" % i, [128, 1], F32) for i in range(2)]
        V = [k.sbuf("x_V%d" % i, [128, 1024], F32) for i in range(2)]
        hold = [k.sbuf("x_hold%d" % i, [128, 1024], F32) for i in range(2)]
        pq = [k.psum("x_pq%d" % i, [128, 512], F32) for i in range(2)]
        pscs = [k.psum("x_psc%d" % i, [128, 256], F32) for i in range(2)]
        ptrs = [k.psum("x_ptr%d" % i, [128, 2, 128], BF16) for i in range(2 if SPARSE_LN2 else 1)]
        pca = [k.psum("x_pca%d" % i, [128, 512], F32) for i in range(2)]
        nq = 0
        ns = 0
        for tg in range(NTG):
            ts_ = slice(tg * 512, (tg + 1) * 512)
            QT, OT = qT[tg % 2], oT[tg % 2]
            for dcc in range(8):
                P = pq[nq % 2]
                nq += 1
                for kk in range(8):
                    mm(k, P[:], wq[:, kk, dcc * 128:(dcc + 1) * 128], c.hT[:, kk, ts_], kk == 0, kk == 7, [wq] + hTr(c, tg), [P])
                cp(k, "act" if dcc % 2 else "dve", QT[:, dcc, :], P[:], [P], [QT])
            units = [(hd, t4) for hd in range(4) for t4 in range(4)]

            def st1(ui, QT=QT):
                hd, t4 = units[ui]
                g = ns0 + ui
                psc, MX, PF = pscs[g % 2], mx4[g % 4], pf4[g % 4]
                tsl = slice(t4 * 128, (t4 + 1) * 128)
                for j in range(2):
                    mm(k, psc[:], QT[:, 2 * hd + j, tsl], kT[:, 2 * hd + j, :], j == 0, j == 1, [QT, kT], [psc])
                k.op("dve", lambda e, psc=psc, MX=MX: e.tensor_reduce(out=MX[:], in_=psc[:], axis=AX.X, op=ALU.max), r=[psc], w=[MX])
                ts(k, "dve", MX[:], MX[:], -1.0 / 16.0, None, ALU.mult, None, [MX], [MX])
                act(k, PF[:], psc[:], AF.Exp, [psc, MX], [PF], bias=MX[:, 0:1], scale=1.0 / 16.0)

            def st2(ui, OT=OT):
                hd, t4 = units[ui]
                g = ns0 + ui
                PF, RS, PB, ptr = pf4[g % 4], rs[g % 2], pb[g % 2], ptrs[g % len(ptrs)]
                PT_ = pT[hd % 2]
                tsl = slice(t4 * 128, (t4 + 1) * 128)
                k.op("dve", lambda e, RS=RS, PF=PF: e.tensor_reduce(out=RS[:], in_=PF[:], axis=AX.X, op=ALU.add), r=[PF], w=[RS])
                k.op("dve", lambda e, RS=RS: e.reciprocal(out=RS[:], in_=RS[:]), r=[RS], w=[RS])
                ts(k, "dve", PB[:], PF[:], RS[:, 0:1], None, ALU.mult, None, [PF, RS], [PB])
                for mt in range(2):
                    k.op("pe", lambda e, PB=PB, mt=mt, ptr=ptr: e.transpose(out=ptr[:, mt, :], in_=PB[:, mt * 128:(mt + 1) * 128], identity=c.ident_bf[:]),
                         r=[PB, c.ident_bf], w=[ptr])
                cp(k, "act", PT_[:, :, tsl], ptr[:], [ptr], [PT_])
                if t4 == 3:
                    for j in range(2):
                        P = pq[nqc[0] % 2]
                        nqc[0] += 1
                        for mt in range(2):
                            mm(k, P[:], vtok[:, mt, (2 * hd + j) * 128:(2 * hd + j + 1) * 128], PT_[:, mt, :], mt == 0, mt == 1, [vtok, PT_], [P])
                        cp(k, "act" if j else "dve", OT[:, 2 * hd + j, :], P[:], [P], [OT])

            ns0 = ns
            nqc = [nq]
            for step in range(len(units) + 2):
                if step < len(units):
                    st1(step)
                if step >= 2:
                    st2(step - 2)
            ns += len(units)
            nq = nqc[0]
            for t4 in range(4):
                i = tg * 4 + t4
                H, VV = hold[i % 2], V[i % 2]
                k.dma("sp", H[:], c.h_tok.t.ap()[i * 128:(i + 1) * 128, :], r=[c.h_tok_b[i]], w=[H])
                for hf in range(2):
                    P = pca[hf]
                    hs = slice(hf * 512, (hf + 1) * 512)
                    for kk in range(8):
                        mm(k, P[:], OT[:, kk, t4 * 128:(t4 + 1) * 128], wo[:, kk, hs], kk == 0, kk == 7, [OT, wo], [P])
                    stt(k, VV[:, hs], H[:, hs], ALPHA, P[:], ALU.mult, ALU.add, [H, P], [VV])
                ln_tile(k, c, i, VV, write_hT=not SPARSE_LN2, hb=SPARSE_LN2)


LAYER_STAGES.append(("X", stage_X))


def stage_R(k, c, l):
    W = c.W
    with k.scope():
        rw = k.sbuf("r_rw", [128, 8, 32], F32)
        k.dma("sp", rw[:], W["router_w"].ap()[l].rearrange("(kk p) e -> p kk e", p=128), w=[rw])
        rb = k.sbuf("r_rb", [128, 32], F32)
        bcast_load(k, "sp", rb, W["router_b"].ap()[l])
        ht = [k.sbuf("r_ht%d" % i, [128, 1024], F32) for i in range(2)]
        h32 = [k.sbuf("r_h32%d" % i, [128, 8, 128], F32) for i in range(2)]
        lg = [k.sbuf("r_lg%d" % i, [128, 32], F32) for i in range(2)]
        m8 = [k.sbuf("r_m8%d" % i, [128, 8], F32) for i in range(2)]
        msk = [k.sbuf("r_msk%d" % i, [128, 32], F32) for i in range(2)]
        nm = [k.sbuf("r_nm%d" % i, [128, 1], F32) for i in range(2)]
        den = [k.sbuf("r_den%d" % i, [128, 1], F32) for i in range(2)]
        p32 = [k.psum("r_p32%d" % i, [128, 4, 128], F32) for i in range(4)]
        plog = [k.psum("r_plog%d" % i, [128, 32], F32) for i in range(2)]
        for i in range(NT):
            b = i % 2
            HT, H32, LG, M8, MSK, NM, DEN = ht[b], h32[b], lg[b], m8[b], msk[b], nm[b], den[b]
            k.dma("sp", HT[:], c.h_tok.t.ap()[i * 128:(i + 1) * 128, :], r=[c.h_tok_b[i]], w=[HT])
            for j in range(8):
                PP = p32[b * 2 + j // 4]
                k.op("pe", lambda e, j=j, PP=PP, HT=HT: e.transpose(out=PP[:, j % 4, :], in_=HT[:, j * 128:(j + 1) * 128], identity=c.ident_f[:]),
                     r=[HT, c.ident_f], w=[PP])
            cp(k, "dve", H32[:, 0:4, :], p32[b * 2][:], [p32[b * 2]], [H32])
            cp(k, "act", H32[:, 4:8, :], p32[b * 2 + 1][:], [p32[b * 2 + 1]], [H32])
            PL = plog[b]
            for kk in range(8):
                mm(k, PL[:], H32[:, kk, :], rw[:, kk, :], kk == 0, kk == 7, [H32, rw], [PL])
            tt(k, "dve", LG[:], PL[:], rb[:], ALU.add, [PL, rb], [LG])
            k.op("dve", lambda e, M8=M8, LG=LG: e.max(out=M8[:], in_=LG[:]), r=[LG], w=[M8])
            ts(k, "dve", MSK[:], LG[:], M8[:, 3:4], None, ALU.is_ge, None, [LG, M8], [MSK])
            ts(k, "dve", NM[:], M8[:, 0:1], -1.0, None, ALU.mult, None, [M8], [NM])
            act(k, LG[:], LG[:], AF.Exp, [LG, NM], [LG], bias=NM[:, 0:1], scale=1.0)
            tt(k, "dve", LG[:], LG[:], MSK[:], ALU.mult, [LG, MSK], [LG])
            k.op("dve", lambda e, DEN=DEN, LG=LG: e.tensor_reduce(out=DEN[:], in_=LG[:], axis=AX.X, op=ALU.add), r=[LG], w=[DEN])
            k.op("dve", lambda e, DEN=DEN: e.reciprocal(out=DEN[:], in_=DEN[:]), r=[DEN], w=[DEN])
            ts(k, "dve", c.G[:, i, :], LG[:], DEN[:, 0:1], None, ALU.mult, None, [LG, DEN], [c.G])


LAYER_STAGES.append(("R", stage_R))


def stage_E(k, c, l):
    W = c.W
    NW = 5
    with k.scope():
        b1T = k.sbuf("e_b1T", [128, 16, 32], F32)
        with k.scope():
            t_ = load_cols(k, c, "e_b1", [(W["moe_b1"].ap()[l], 32)], width=2048)
            cp(k, "dve", b1T[:], t_[:], [t_], [b1T])
        b2 = k.sbuf("e_b2", [32, 1024], F32)
        k.dma("sp", b2[:], W["moe_b2"].ap()[l], w=[b2])
        ln_alloc(k, c, "3")
        bcast_load(k, "sp", c.ln_g, W["ln3_g"].ap()[l])
        bcast_load(k, "sp", c.ln_b, W["ln3_b"].ap()[l])
        ring = [k.sbuf("e_w%d" % i, [128, 8, 512], BF16) for i in range(NW)]
        AT = [k.sbuf("e_at%d" % i, [128, 8, 512], BF16) for i in range(2)]
        ffacc = k.sbuf("e_ff", [128, 4, 1024], F32)
        gc = [k.sbuf("e_gc%d" % i, [128, 512], F32) for i in range(2)]
        sgm = [k.sbuf("e_sg%d" % i, [128, 512], F32) for i in range(2)]
        lc = [k.sbuf("e_lc%d" % i, [128, 512], F32) for i in range(2)]
        GT = [k.sbuf("e_gt%d" % i, [32, 128], F32) for i in range(2)]
        V = [k.sbuf("e_V%d" % i, [128, 1024], F32) for i in range(2)]
        hold = [k.sbuf("e_hold%d" % i, [128, 1024], F32) for i in range(2)]
        pg = [k.psum("e_pg%d" % i, [128, 512], F32) for i in range(2)]
        pl = [k.psum("e_pl%d" % i, [128, 512], F32) for i in range(2)]
        py = [k.psum("e_py%d" % i, [128, 512], F32) for i in range(2)]
        w1 = W["moe_w1"].ap()[l]
        w2 = W["moe_w2"].ap()[l]
        nw = 0
        n = 0
        ny = 0
        na = 0
        for tg in range(NTG):
            ts_ = slice(tg * 512, (tg + 1) * 512)
            for e in range(NE):
                A = AT[na % 2]
                na += 1
                w1e = w1[e].rearrange("(kk p) f -> p kk f", p=128)
                w2e = w2[e].rearrange("(kk p) d -> p kk d", p=128)
                for q in range(4):
                    WQ = ring[nw % NW]
                    nw += 1
                    k.dma("pool", WQ[:, :, 0:256], w1e[:, :, q * 256:(q + 1) * 256], w=[WQ])
                    k.dma("pool", WQ[:, :, 256:512], w1e[:, :, 1024 + q * 256:1024 + (q + 1) * 256], w=[WQ])
                    for ci in range(2):
                        fc = 2 * q + ci
                        b = n % 2
                        n += 1
                        PG, PL, GC, SG, LC = pg[b], pl[b], gc[b], sgm[b], lc[b]
                        for kk in range(8):
                            mm(k, PG[:], WQ[:, kk, ci * 128:(ci + 1) * 128], c.hT[:, kk, ts_], kk == 0, kk == 7, [WQ] + hTr(c, tg), [PG])
                        for kk in range(8):
                            mm(k, PL[:], WQ[:, kk, 256 + ci * 128:256 + (ci + 1) * 128], c.hT[:, kk, ts_], kk == 0, kk == 7, [WQ] + hTr(c, tg), [PL])
                        ts(k, "dve", GC[:], PG[:], b1T[:, fc, e:e + 1], 7.0, ALU.add, ALU.min, [PG, b1T], [GC])
                        act(k, SG[:], GC[:], AF.Sigmoid, [GC], [SG], scale=1.702)
                        ts(k, "dve", LC[:], PL[:], b1T[:, 8 + fc, e:e + 1], 7.0, ALU.add, ALU.min, [PL, b1T], [LC])
                        ts(k, "dve", LC[:], LC[:], -7.0, 1.0, ALU.max, ALU.add, [LC], [LC])
                        tt(k, "dve", GC[:], GC[:], SG[:], ALU.mult, [GC, SG], [GC])
                        tt(k, "dve", A[:, fc, :], GC[:], LC[:], ALU.mult, [GC, LC], [A])
                for hf in range(2):
                    W2 = ring[nw % NW]
                    nw += 1
                    hs = slice(hf * 512, (hf + 1) * 512)
                    k.dma("pool", W2[:], w2e[:, :, hs], w=[W2])
                    for t4 in range(4):
                        PY = py[ny % 2]
                        ny += 1
                        for kk in range(8):
                            mm(k, PY[:], A[:, kk, t4 * 128:(t4 + 1) * 128], W2[:, kk, :], kk == 0, kk == 7, [A, W2], [PY])
                        gcol = c.G[:, tg * 4 + t4, e:e + 1]
                        if e == 0:
                            ts(k, "dve", ffacc[:, t4, hs], PY[:], gcol, None, ALU.mult, None, [PY, c.G], [ffacc])
                        else:
                            stt(k, ffacc[:, t4, hs], PY[:], gcol, ffacc[:, t4, hs], ALU.mult, ALU.add, [PY, c.G, ffacc], [ffacc])
            for t4 in range(4):
                i = tg * 4 + t4
                H, VV, GTt = hold[i % 2], V[i % 2], GT[i % 2]
                k.dma("sp", H[:], c.h_tok.t.ap()[i * 128:(i + 1) * 128, :], r=[c.h_tok_b[i]], w=[H])
                PY = py[ny % 2]
                ny += 1
                k.op("pe", lambda e_, i=i, PY=PY: e_.transpose(out=PY[:32, :128], in_=c.G[:, i, :], identity=c.ident_f[:]), r=[c.G, c.ident_f], w=[PY])
                cp(k, "act", GTt[:], PY[:32, :128], [PY], [GTt])
                for hf in range(2):
                    hs = slice(hf * 512, (hf + 1) * 512)
                    PY = py[ny % 2]
                    ny += 1
                    mm(k, PY[:], GTt[:], b2[:, hs], True, True, [GTt, b2], [PY])
                    stt(k, VV[:, hs], H[:, hs], ALPHA, ffacc[:, t4, hs], ALU.mult, ALU.add, [H, ffacc], [VV])
                    tt(k, "dve", VV[:, hs], VV[:, hs], PY[:], ALU.add, [VV, PY], [VV])
                ln_tile(k, c, i, VV)


LAYER_STAGES.append(("E", stage_E))


BS = 256
NBLK = 96
NSLOT = NBLK * BS


def stage_R2(k, c, l):
    with k.scope():
        def cload(name, shape):
            t = k.sbuf("r2_" + name, shape, F32)
            k.dma("sp", t[:], c.consts[name].ap(), w=[t])
            return t
        Lt = cload("Lt", [128, 128])
        thr = cload("thr", [128, 16])
        jrow = cload("jrow", [128, NBLK])
        kp = cload("kp", [128, 8])
        tokf = cload("tokid", [128, 32])
        toki = k.sbuf("r2_toki", [128, 32], I32)
        cp(k, "dve", toki[:], tokf[:], [tokf], [toki])
        onesf = k.sbuf("r2_ones", [128, 128], F32)
        k.op("pool", lambda e: e.memset(onesf[:], 1.0), w=[onesf])
        initf = k.sbuf("r2_initf", [128, NSLOT // 128], F32)
        initi = k.sbuf("r2_initi", [128, NSLOT // 128], I32)
        k.op("pool", lambda e: e.memset(initf[:], float(S)), w=[initf])
        cp(k, "dve", initi[:], initf[:], [initf], [initi])
        k.dma("sp", c.slot_tok.t.ap().rearrange("(p n) o -> p (n o)", p=128), initi[:], r=[initi], w=[c.slot_tok])
        mask = k.sbuf("r2_mask", [128, 32, 32], F32)
        ts(k, "dve", mask[:], c.G[:], 0.0, None, ALU.is_gt, None, [c.G], [mask])
        pos = k.sbuf("r2_pos", [128, 32, 32], F32)
        cm = [k.sbuf("r2_cm%d" % i, [128, 32], F32) for i in range(2)]
        k.op("pool", lambda e: e.memset(cm[0][:], 0.0), w=[cm[0]])
        pp = [k.psum("r2_pp%d" % i, [128, 32], F32) for i in range(2)]
        for i in range(NT):
            P = pp[i % 2]
            mm(k, P[:], Lt[:], mask[:, i, :], True, False, [Lt, mask], [P])
            mm(k, P[:], onesf[:], cm[i % 2][:], False, True, [onesf, cm[i % 2]], [P])
            cp(k, "act", pos[:, i, :], P[:], [P], [pos])
            tt(k, "dve", cm[(i + 1) % 2][:], cm[i % 2][:], mask[:, i, :], ALU.add, [cm[i % 2], mask], [cm[(i + 1) % 2]])
        P = pp[0]
        mm(k, P[:], onesf[:], cm[NT % 2][:], True, True, [onesf, cm[NT % 2]], [P])
        ncnt = k.sbuf("r2_n", [128, 32], F32)
        cp(k, "dve", ncnt[:], P[:], [P], [ncnt])
        cmp1 = k.sbuf("r2_cmp1", [128, 32, 16], F32)
        tt(k, "dve", cmp1[:], ncnt[:].rearrange("p (e o) -> p e o", o=1).to_broadcast([128, 32, 16]),
           thr[:].rearrange("p (o j) -> p o j", o=1).to_broadcast([128, 32, 16]), ALU.is_gt, [ncnt, thr], [cmp1])
        nblk = k.sbuf("r2_nblk", [128, 32], F32)
        k.op("dve", lambda e: e.tensor_reduce(out=nblk[:], in_=cmp1[:], axis=AX.X, op=ALU.add), r=[cmp1], w=[nblk])
        pend = k.sbuf("r2_pend", [128, 32], F32)
        k.op("dve", lambda e: e.tensor_tensor_scan(out=pend[:], data0=onesf[:, 0:32], data1=nblk[:], initial=0.0, op0=ALU.mult, op1=ALU.add),
             r=[onesf, nblk], w=[pend])
        pst = k.sbuf("r2_pst", [128, 32], F32)
        tt(k, "dve", pst[:], pend[:], nblk[:], ALU.subtract, [pend, nblk], [pst])
        ts(k, "dve", pst[:], pst[:], float(BS), 1.0, ALU.mult, ALU.add, [pst], [pst])
        cmp2 = k.sbuf("r2_cmp2", [128, NBLK, 32], F32)
        tt(k, "dve", cmp2[:], pend[:].rearrange("p (o e) -> p o e", o=1).to_broadcast([128, NBLK, 32]),
           jrow[:].rearrange("p (j o) -> p j o", o=1).to_broadcast([128, NBLK, 32]), ALU.is_le, [pend, jrow], [cmp2])
        blke = k.sbuf("r2_blke", [128, NBLK], F32)
        k.op("dve", lambda e: e.tensor_reduce(out=blke[:], in_=cmp2[:], axis=AX.X, op=ALU.add), r=[cmp2], w=[blke])
        ts(k, "dve", blke[:], blke[:], 31.0, None, ALU.min, None, [blke], [blke])
        blkg = k.sbuf("r2_blkg", [128, NBLK], F32)
        ts(k, "dve", blkg[:], blke[:], float(l * 32), None, ALU.add, None, [blke], [blkg])
        cp(k, "dve", c.eidx[:], blkg[:], [blkg], [c.eidx])
        widf = k.sbuf("r2_widf", [128, NBLK, 8], F32)
        stt(k, widf[:], blke[:].rearrange("p (j o) -> p j o", o=1).to_broadcast([128, NBLK, 8]), 1024.0,
            kp[:].rearrange("p (o q) -> p o q", o=1).to_broadcast([128, NBLK, 8]), ALU.mult, ALU.add, [blke, kp], [widf])
        cp(k, "dve", c.widx[:], widf[:], [widf], [c.widx])
        ss = [k.sbuf("r2_ss%d" % i, [128, 32], F32) for i in range(2)]
        m8 = [k.sbuf("r2_m8%d" % i, [128, 8], F32) for i in range(2)]
        oh = [k.sbuf("r2_oh%d" % i, [128, 32], F32) for i in range(2)]
        for i in range(NT):
            SS, M8 = ss[i % 2], m8[i % 2]
            tt(k, "dve", SS[:], pos[:, i, :], pst[:], ALU.add, [pos, pst], [SS])
            tt(k, "dve", SS[:], SS[:], mask[:, i, :], ALU.mult, [SS, mask], [SS])
            ts(k, "dve", SS[:], SS[:], -1.0, None, ALU.add, None, [SS], [SS])
            k.op("dve", lambda e, M8=M8, SS=SS: e.max(out=M8[:], in_=SS[:]), r=[SS], w=[M8])
            cp(k, "dve", c.dest[:, i, :], M8[:, 0:4], [M8], [c.dest])
            for q in range(4):
                OH = oh[q % 2]
                ts(k, "dve", OH[:], SS[:], M8[:, q:q + 1], None, ALU.is_equal, None, [SS, M8], [OH])
                tt(k, "dve", OH[:], OH[:], c.G[:, i, :], ALU.mult, [OH, c.G], [OH])
                k.op("dve", lambda e, OH=OH, i=i, q=q: e.tensor_reduce(out=c.gk[:, i, q:q + 1], in_=OH[:], axis=AX.X, op=ALU.add), r=[OH], w=[c.gk])
            for q in range(4):
                def scat(e, i=i, q=q):
                    return e.indirect_dma_start(out=c.slot_tok.t.ap(), out_offset=bass.IndirectOffsetOnAxis(ap=c.dest[:, i, q:q + 1], axis=0),
                                                in_=toki[:, i:i + 1], in_offset=None)
                k.dma_custom("pool", scat, r=[c.dest, toki, c.slot_tok], w=[])


def stage_CV(k, c, l):
    W = c.W
    for g in range(8):
        k.dma_bg("pool", c.w1b.t.ap()[g * 4096:(g + 1) * 4096, :], W["moe_w1"].ap()[l, 4 * g:4 * g + 4].rearrange("e r f -> (e r) f"),
                 slot=g, w=[c.cvb[g]])
        k.dma_bg("pool", c.w2b.t.ap()[g * 4096:(g + 1) * 4096, :], W["moe_w2"].ap()[l, 4 * g:4 * g + 4].rearrange("e r f -> (e r) f"),
                 slot=8 + g, w=[c.cvb[8 + g]])


def stage_E2(k, c, l):
    W = c.W
    w1rows = c.w1b.t.ap()
    w2rows = c.w2b.t.ap()
    b1rows = W["moe_b1"].ap().rearrange("l e f -> (l e) f")
    b2rows = W["moe_b2"].ap().rearrange("l e f -> (l e) f")
    with k.scope():
        w1b = [[Buf("w1t%d_%d" % (b_, q_)) for q_ in range(8)] for b_ in range(2)]
        w2b = [[Buf("w2t%d_%d" % (b_, q_)) for q_ in range(8)] for b_ in range(2)]
        w2t = [k.sbuf("e2_w2t%d" % i, [128, 8, 1024], BF16) for i in range(2)]
        b1t = [k.sbuf("e2_b1t%d" % i, [128, 2048], BF16) for i in range(2)]
        b2t = [k.sbuf("e2_b2t%d" % i, [128, 1024], F32) for i in range(2)]
        idxt = [k.sbuf("e2_idx%d" % i, [128, 2], I32) for i in range(3)]
        xg = [k.sbuf("e2_xg%d" % i, [128, 2, 1024], BF16) for i in range(2)]
        xgT = [k.sbuf("e2_xgT%d" % i, [128, 8, BS], BF16) for i in range(2)]
        AT = [k.sbuf("e2_at%d" % i, [128, 8, BS], BF16) for i in range(2)]
        gc = [k.sbuf("e2_gc%d" % i, [128, 512], F32) for i in range(2)]
        sgm = [k.sbuf("e2_sg%d" % i, [128, 512], F32) for i in range(2)]
        lc = [k.sbuf("e2_lc%d" % i, [128, 512], F32) for i in range(2)]
        atok = [k.sbuf("e2_atok%d" % i, [128, 1024], BF16) for i in range(2)]
        ysb = [k.sbuf("e2_ys%d" % i, [128, 1024], F32) for i in range(2)]
        onesr = k.sbuf("e2_onesr", [1, 128], BF16)
        k.op("pool", lambda e: e.memset(onesr[:], 1.0), w=[onesr])
        ptr = [k.psum("e2_ptr%d" % i, [128, 1024], BF16) for i in range(2)]
        pta = ptr
        pg = [k.psum("e2_pg%d" % i, [128, 512], F32) for i in range(2)]
        pl = [k.psum("e2_pl%d" % i, [128, 512], F32) for i in range(2)]
        py = [k.psum("e2_py%d" % i, [128, 512], F32) for i in range(2)]
        st2 = c.slot_tok.t.ap()

        def loads(j):
            b = j % 2
            IDX = idxt[j % 3]
            k.dma("sp", IDX[:], st2[j * BS:(j + 1) * BS, :].rearrange("(s p) o -> p (s o)", p=128), r=[c.slot_tok], w=[IDX],
                  allow_slow_non_contiguous=True)
            for s_ in range(2):
                def g(e, s_=s_, b=b, IDX=IDX):
                    return e.indirect_dma_start(out=xg[b][:, s_, :], out_offset=None, in_=c.hb.t.ap(),
                                                in_offset=bass.IndirectOffsetOnAxis(ap=IDX[:, s_:s_ + 1], axis=0))
                k.dma_custom("pool", g, r=[IDX, c.hb], w=[xg[b]])
            def gb1(e, b=b, j=j):
                return e.indirect_dma_start(out=b1t[b][:], out_offset=None, in_=b1rows,
                                            in_offset=bass.IndirectOffsetOnAxis(ap=c.eidx[:, j:j + 1], axis=0))
            k.dma_custom("pool", gb1, r=[c.eidx], w=[b1t[b]])
            for kk in range(8):
                def g1(e, kk=kk, b=b, j=j):
                    return e.indirect_dma_start(out=c.hT[:, kk, b * 2048:(b + 1) * 2048], out_offset=None, in_=w1rows,
                                                in_offset=bass.IndirectOffsetOnAxis(ap=c.widx[:, j, kk:kk + 1], axis=0))
                k.dma_custom("pool", g1, r=[c.widx] + c.cvb, w=[w1b[b][kk]])

        def loadsB(j):
            b = j % 2
            for kk in range(8):
                def g2(e, kk=kk, b=b, j=j):
                    return e.indirect_dma_start(out=w2t[b][:, kk, :], out_offset=None, in_=w2rows,
                                                in_offset=bass.IndirectOffsetOnAxis(ap=c.widx[:, j, kk:kk + 1], axis=0))
                k.dma_custom("pool", g2, r=[c.widx] + c.cvb, w=[w2b[b][kk]])

            def gb2(e, b=b, j=j):
                return e.indirect_dma_start(out=b2t[b][:], out_offset=None, in_=b2rows,
                                            in_offset=bass.IndirectOffsetOnAxis(ap=c.eidx[:, j:j + 1], axis=0))
            k.dma_custom("pool", gb2, r=[c.eidx], w=[b2t[b]])

        cnt = {"n": 0, "ny": 0}

        def xtrans(j):
            b = j % 2
            XT = xgT[b]
            for s_ in range(2):
                PT_ = ptr[s_]
                for kk in range(8):
                    k.op("pe", lambda e, kk=kk, s_=s_, PT_=PT_, b=b: e.transpose(out=PT_[:, kk * 128:(kk + 1) * 128], in_=xg[b][:, s_, kk * 128:(kk + 1) * 128],
                                                                                 identity=c.ident_bf[:]), r=[xg[b], c.ident_bf], w=[PT_])
                cp(k, "act" if s_ else "dve", XT[:, :, s_ * 128:(s_ + 1) * 128], PT_[:].rearrange("p (a q) -> p a q", a=8), [PT_], [XT])

        def phase1(j, s_):
            b = j % 2
            XT = xgT[b]
            ssl = slice(s_ * 128, (s_ + 1) * 128)
            for pr in range(2):
                bb = cnt["n"] % 2
                cnt["n"] += 1
                PG, PL, GC, SG, LC = pg[bb], pl[bb], gc[bb], sgm[bb], lc[bb]
                for (P, c0) in ((PG, pr * 512), (PL, 1024 + pr * 512)):
                    for kk in range(8):
                        mm(k, P[:], XT[:, kk, ssl], c.hT[:, kk, b * 2048 + c0:b * 2048 + c0 + 512], kk == 0, False, [XT, w1b[b][kk]], [P])
                    mm(k, P[:], onesr[0:1, :], b1t[b][0:1, c0:c0 + 512], False, True, [onesr, b1t[b]], [P])
                ts(k, "dve", GC[:], PG[:], 7.0, None, ALU.min, None, [PG], [GC])
                act(k, SG[:], GC[:], AF.Sigmoid, [GC], [SG], scale=1.702)
                ts(k, "dve", LC[:], PL[:], 7.0, -7.0, ALU.min, ALU.max, [PL], [LC])
                tt(k, "dve", GC[:], GC[:], SG[:], ALU.mult, [GC, SG], [GC])
                stt(k, atok[s_][:, pr * 512:(pr + 1) * 512], LC[:], 1.0, GC[:], ALU.add, ALU.mult, [LC, GC], [atok[s_]])

        def phase2a(j, s_):
            b = j % 2
            A = AT[b]
            ssl = slice(s_ * 128, (s_ + 1) * 128)
            PT_ = pta[s_]
            for kk in range(8):
                k.op("pe", lambda e, kk=kk, s_=s_, PT_=PT_: e.transpose(out=PT_[:, kk * 128:(kk + 1) * 128], in_=atok[s_][:, kk * 128:(kk + 1) * 128],
                                                                   identity=c.ident_bf[:]), r=[atok[s_], c.ident_bf], w=[PT_])
            cp(k, "act" if s_ else "dve", A[:, :, ssl], PT_[:].rearrange("p (a q) -> p a q", a=8), [PT_], [A])

        def phase2b(j, s_):
            b = j % 2
            A = AT[b]
            ssl = slice(s_ * 128, (s_ + 1) * 128)
            Y = ysb[s_]
            for hf in range(2):
                PY = py[cnt["ny"] % 2]
                cnt["ny"] += 1
                hs = slice(hf * 512, (hf + 1) * 512)
                for kk in range(8):
                    mm(k, PY[:], A[:, kk, ssl], w2t[b][:, kk, hs], kk == 0, kk == 7, [A, w2b[b][kk]], [PY])
                tt(k, "dve", Y[:, hs], PY[:], b2t[b][:, hs], ALU.add, [PY, b2t[b]], [Y])
            r0 = j * BS + s_ * 128
            k.dma("sp", c.ys.t.ap()[r0:r0 + 128, :], Y[:], r=[Y], w=[c.ys])

        loads(0)
        loadsB(0)
        loads(1)
        loadsB(1)
        NU = 2 * NBLK
        xtrans(0)
        for u in range(NU + 1):
            if u >= 1:
                phase2a(*divmod(u - 1, 2))
            if u < NU:
                j, s_ = divmod(u, 2)
                if s_ == 1 and j + 1 < NBLK:
                    xtrans(j + 1)
                phase1(j, s_)
                if s_ == 1 and j + 2 < NBLK:
                    loads(j + 2)
            if u >= 1:
                j2, s2 = divmod(u - 1, 2)
                phase2b(j2, s2)
                if s2 == 1 and j2 + 2 < NBLK:
                    loadsB(j2 + 2)


def stage_F(k, c, l):
    W = c.W
    with k.scope():
        ln_alloc(k, c, "3")
        bcast_load(k, "sp", c.ln_g, W["ln3_g"].ap()[l])
        bcast_load(k, "sp", c.ln_b, W["ln3_b"].ap()[l])
        rows = [k.sbuf("f_rows%d" % i, [128, 1024], F32) for i in range(8)]
        acc = [k.sbuf("f_acc%d" % i, [128, 1024], F32) for i in range(2)]
        hold = [k.sbuf("f_hold%d" % i, [128, 1024], F32) for i in range(2)]
        def fetch(i):
            H = hold[i % 2]
            k.dma("sp", H[:], c.h_tok.t.ap()[i * 128:(i + 1) * 128, :], r=[c.h_tok_b[i]], w=[H])
            for q in range(4):
                Rw = rows[(i % 2) * 4 + q]

                def g(e, Rw=Rw, i=i, q=q):
                    return e.indirect_dma_start(out=Rw[:], out_offset=None, in_=c.ys.t.ap(),
                                                in_offset=bass.IndirectOffsetOnAxis(ap=c.dest[:, i, q:q + 1], axis=0))
                k.dma_custom("pool", g, r=[c.dest, c.ys], w=[Rw])

        pend = None
        fetch(0)
        for i in range(NT):
            H, A = hold[i % 2], acc[i % 2]
            if i + 1 < NT:
                fetch(i + 1)
            for q in range(4):
                Rw = rows[(i % 2) * 4 + q]
                if q == 0:
                    ts(k, "dve", A[:], Rw[:], c.gk[:, i, 0:1], None, ALU.mult, None, [Rw, c.gk], [A])
                else:
                    stt(k, A[:], Rw[:], c.gk[:, i, q:q + 1], A[:], ALU.mult, ALU.add, [Rw, c.gk, A], [A])
            stt(k, A[:], H[:], ALPHA, A[:], ALU.mult, ALU.add, [H, A], [A])
            nb_ = ln_tile(k, c, i, A, defer=True)
            if pend is not None:
                pend()
            pend = nb_
        pend()


def make_consts2():
    c = {}
    j = np.arange(128)
    c["Lt"] = (j[:, None] < j[None, :]).astype(np.float32)
    c["thr"] = np.broadcast_to((np.arange(16) * BS).astype(np.float32)[None, :], (128, 16)).copy()
    c["jrow"] = np.broadcast_to(np.arange(NBLK).astype(np.float32)[None, :], (128, NBLK)).copy()
    c["kp"] = (np.arange(8)[None, :] * 128 + j[:, None]).astype(np.float32)
    c["tokid"] = (np.arange(32)[None, :] * 128 + j[:, None]).astype(np.float32)
    return c


CONST_SHAPES.update(dict(Lt=[128, 128], thr=[128, 16], jrow=[128, NBLK], kp=[128, 8], tokid=[128, 32]))
_mc1 = make_consts


def make_consts():
    c = _mc1()
    c.update(make_consts2())
    return c


SPARSE = True
if SPARSE:
    LAYER_STAGES[:] = [s for s in LAYER_STAGES if s[0] != "E"]
    LAYER_STAGES.insert(0, ("CV", stage_CV))
    LAYER_STAGES.extend([("R2", stage_R2), ("E2", stage_E2), ("F", stage_F)])


def build(nlayers=NL, upto=None, dbg=()):
    nc = bass.Bass("TRN2", target_bir_lowering=False)
    c = Ctx()
    c.nc = nc
    c.x = nc.dram_tensor("x", [S, D], F32, kind="ExternalInput")
    c.mem = nc.dram_tensor("mem", [256, D], F32, kind="ExternalInput")
    c.W = {n: nc.dram_tensor(n, (shp if n.startswith("ln0") else [nlayers] + shp[1:]), F32, kind="ExternalInput") for n, shp in WSPEC}
    c.consts = {n: nc.dram_tensor("c_" + n, shp, F32, kind="ExternalInput") for n, shp in CONST_SHAPES.items()}

    def scratch(name, shape, dt):
        kind = "ExternalOutput" if name in dbg else "Internal"
        t = T(nc.dram_tensor(name, list(shape), dt, kind=kind), name)
        return t

    c.h_tok = T(nc.dram_tensor("out", [S, D], F32, kind="ExternalOutput"), "out")
    c.h_tok_b = [Buf("htok%d" % i) for i in range(NT)]
    c.zT = scratch("zT", [INCOLS, S], BF16)
    c.zT_b = c.zT.b
    c.ztok = {n: scratch("ztok_" + n, [S, 512], BF16) for n in TOKMAJ}
    c.yT = scratch("yT", [2048, S], BF16)
    c.hb = scratch("hb", [S + 1, D], BF16)
    c.slot_tok = scratch("slot_tok", [NSLOT, 1], I32)
    c.ys = scratch("ys", [NSLOT, D], F32)
    c.w1b = scratch("w1b", [NE * 1024, 2048], BF16)
    c.w2b = scratch("w2b", [NE * 1024, 1024], BF16)
    c.cvb = [Buf("cv%d" % i) for i in range(16)]

    with ExitStack() as st:
        k = KB(nc, st)
        c.k = k
        c.hT = k.sbuf("hT", [128, 8, S], BF16)
        c.hT_b = [Buf("hT%d" % i) for i in range(NT)]
        c.ident_bf = k.sbuf("ident_bf", [128, 128], BF16)
        c.ident_f = k.sbuf("ident_f", [128, 128], F32)
        c.memT = k.sbuf("memT", [128, 8, 256], BF16)
        c.G = k.sbuf("G", [128, 32, 32], F32)
        c.gk = k.sbuf("gk", [128, 32, 4], F32)
        c.dest = k.sbuf("dest", [128, 32, 4], I32)
        c.widx = k.sbuf("widx", [128, NBLK, 8], I32)
        c.eidx = k.sbuf("eidx", [128, NBLK], I32)
        stages = []
        stages.append(("prologue", lambda: stage_prologue(k, c)))
        for l in range(nlayers):
            stages.append(("inproj%d" % l, lambda l=l: stage_inproj(k, c, l)))
            for nm, fn in LAYER_STAGES:
                stages.append((nm + str(l), lambda l=l, fn=fn: fn(k, c, l)))
        for nm, fn in stages:
            fn()
            if upto is not None and nm == upto:
                break
        k.barrier()
        k.finish()
    c.ninstr = k.nins
    return nc, c


_CACHE = {}


def _in_maps(inputs, ncores=4):
    consts = make_consts()
    maps = []
    for b in range(ncores):
        m = {"x": np.ascontiguousarray(inputs["x"][b]), "mem": np.ascontiguousarray(inputs["mem"][b])}
        for n, _ in WSPEC:
            m[n] = np.ascontiguousarray(inputs[n], dtype=np.float32)
        for n, v in consts.items():
            m["c_" + n] = v
        maps.append(m)
    return maps


def kernel(**inputs):
    if "nc" not in _CACHE:
        _CACHE["nc"] = build()[0]
    nc = _CACHE["nc"]
    maps = _in_maps(inputs, 4)
    res = run_bass_kernel_spmd(nc, maps, core_ids=[0, 1, 2, 3])
    return np.stack([np.asarray(r["out"], dtype=np.float32) for r in res.results], axis=0)
```
